# Optimizing a Trainium2 kernel written in Bass

```python
import jax, jax.numpy as jnp
from jax import lax
import numpy as np

D_MODEL = 1024
BATCH = 8
SEQ = 2048
DEPTH = 4

HEAD_DIM = 64
GRID_W = 64
MEM_LEN = 256
Q_BLOCK = 128
ROPE_THETA = 10000.0
EPS = 1e-6
NEG = -1e30

A_HEADS = 8
A_KV_HEADS = 2
B_HEADS = 8
B_KV_HEADS = 2
B_WINDOW = 128
C_HEADS = 8
C_Q_RANK = 256
C_KV_RANK = 128
C_NOPE = 64
C_ROPE = 32
C_V = 64
D_HEADS = 8
D_WIN_R = 8
D_WIN_C = 16
X_HEADS = 4
X_HEAD_DIM = 128
D_FF = -(-8 * D_MODEL // (3 * 256)) * 256

AB_SIZES = (A_HEADS * HEAD_DIM, A_KV_HEADS * HEAD_DIM, A_KV_HEADS * HEAD_DIM,
            B_HEADS * HEAD_DIM, B_KV_HEADS * HEAD_DIM, B_KV_HEADS * HEAD_DIM)
CD_SIZES = (C_Q_RANK, C_KV_RANK, C_ROPE,
            D_HEADS * HEAD_DIM, D_HEADS * HEAD_DIM, D_HEADS * HEAD_DIM)
IN_AB = sum(AB_SIZES)
IN_CD = sum(CD_SIZES)
MIX_AB = (A_HEADS + B_HEADS) * HEAD_DIM
MIX_CD = C_HEADS * C_V + D_HEADS * HEAD_DIM

kernel_name = "hybrid_gqa_swa_mla_natten_encoder"


def _split(z, sizes):
    idx = [int(v) for v in np.cumsum(sizes)[:-1]]
    return jnp.split(z, idx, axis=-1)


def rms_norm(x, g):
    xf = x.astype(jnp.float32)
    y = xf * lax.rsqrt(jnp.mean(xf * xf, axis=-1, keepdims=True) + EPS)
    return (y * g.astype(jnp.float32)).astype(x.dtype)


def rope_angles(pos, dim):
    inv = ROPE_THETA ** (-jnp.arange(0, dim, 2, dtype=jnp.float32) / dim)
    return pos.astype(jnp.float32)[:, None] * inv[None, :]


def apply_rope(x, ang):
    cos = jnp.cos(ang)[None, :, None, :]
    sin = jnp.sin(ang)[None, :, None, :]
    x1, x2 = jnp.split(x.astype(jnp.float32), 2, axis=-1)
    out = jnp.concatenate([x1 * cos - x2 * sin, x1 * sin + x2 * cos], axis=-1)
    return out.astype(x.dtype)


def blocked_dense_attention(q, k, v, scale):
    B, S = q.shape[0], q.shape[1]
    nb = S // Q_BLOCK
    qb = q.reshape(B, nb, Q_BLOCK, *q.shape[2:]).swapaxes(0, 1)

    def one_block(q_blk):
        s = jnp.einsum('bqhgd,bkhd->bhgqk', q_blk, k,
                       preferred_element_type=jnp.float32) * scale
        p = jax.nn.softmax(s, axis=-1).astype(v.dtype)
        return jnp.einsum('bhgqk,bkhd->bqhgd', p, v)

    out = lax.map(one_block, qb)
    return out.swapaxes(0, 1).reshape(B, S, -1)


def window_attention_with_sink(q, k, v, sink, scale):
    B, S, Hkv, G, d = q.shape
    nb = S // Q_BLOCK
    pad = ((0, 0), (Q_BLOCK, Q_BLOCK), (0, 0), (0, 0))
    kp = jnp.pad(k, pad).reshape(B, nb + 2, Q_BLOCK, Hkv, d)
    vp = jnp.pad(v, pad).reshape(B, nb + 2, Q_BLOCK, Hkv, d)
    kb = jnp.concatenate([kp[:, :-2], kp[:, 1:-1], kp[:, 2:]], axis=2)
    vb = jnp.concatenate([vp[:, :-2], vp[:, 1:-1], vp[:, 2:]], axis=2)
    qb = q.reshape(B, nb, Q_BLOCK, Hkv, G, d)
    s = jnp.einsum('bnqhgd,bnkhd->bnhgqk', qb, kb,
                   preferred_element_type=jnp.float32) * scale
    blk = jnp.arange(nb)[:, None, None] * Q_BLOCK
    q_abs = blk + jnp.arange(Q_BLOCK)[None, :, None]
    k_abs = blk - Q_BLOCK + jnp.arange(3 * Q_BLOCK)[None, None, :]
    valid = (jnp.abs(k_abs - q_abs) <= B_WINDOW) & (k_abs >= 0) & (k_abs < S)
    s = jnp.where(valid[None, :, None, None], s, NEG)
    sink_col = jnp.broadcast_to(sink.astype(jnp.float32).reshape(1, 1, Hkv, G, 1, 1),
                                s.shape[:-1] + (1,))
    p = jax.nn.softmax(jnp.concatenate([s, sink_col], axis=-1), axis=-1)[..., :-1]
    out = jnp.einsum('bnhgqk,bnkhd->bnqhgd', p.astype(v.dtype), vb)
    return out.reshape(B, S, Hkv * G * d)


def neighbourhood_attention(q, k, v, rpb, scale):
    B, S, H, d = q.shape
    rows = S // GRID_W
    wr = min(D_WIN_R, rows)
    c = jnp.arange(GRID_W)
    c0 = jnp.clip(c - D_WIN_C // 2, 0, GRID_W - D_WIN_C)
    col_valid = (c[None, :] >= c0[:, None]) & (c[None, :] < c0[:, None] + D_WIN_C)
    valid = jnp.tile(col_valid, (1, wr))
    col_idx = jnp.clip(c[None, :] - c[:, None] + (D_WIN_C - 1), 0, 2 * D_WIN_C - 2)
    qg = q.reshape(B, rows, GRID_W, H, d)
    kg = k.reshape(B, rows, GRID_W, H, d)
    vg = v.reshape(B, rows, GRID_W, H, d)
    rpb_f = rpb.astype(jnp.float32)

    def one_row(args):
        q_row, r = args
        r0 = jnp.clip(r - wr // 2, 0, rows - wr)
        k_win = lax.dynamic_slice_in_dim(kg, r0, wr, axis=1).reshape(B, wr * GRID_W, H, d)
        v_win = lax.dynamic_slice_in_dim(vg, r0, wr, axis=1).reshape(B, wr * GRID_W, H, d)
        s = jnp.einsum('bqhd,bkhd->bhqk', q_row, k_win,
                       preferred_element_type=jnp.float32) * scale
        row_idx = r0 + jnp.arange(wr) - r + (D_WIN_R - 1)
        bias = rpb_f[:, row_idx[None, :, None], col_idx[:, None, :]]
        s = jnp.where(valid, s + bias.reshape(H, GRID_W, wr * GRID_W)[None], NEG)
        p = jax.nn.softmax(s, axis=-1).astype(v.dtype)
        return jnp.einsum('bhqk,bkhd->bqhd', p, v_win)

    out = lax.map(one_row, (qg.swapaxes(0, 1), jnp.arange(rows)))
    return out.swapaxes(0, 1).reshape(B, S, H * d)


def mixer_ab(h, w_in, g_qa, g_ka, sink, w_out, ang_1d, ang_2d):
    B, S, _ = h.shape
    qa, ka, va, qb, kb, vb = _split(h @ w_in, AB_SIZES)
    qa = apply_rope(rms_norm(qa.reshape(B, S, A_HEADS, HEAD_DIM), g_qa), ang_2d)
    ka = apply_rope(rms_norm(ka.reshape(B, S, A_KV_HEADS, HEAD_DIM), g_ka), ang_2d)
    qa = qa.reshape(B, S, A_KV_HEADS, A_HEADS // A_KV_HEADS, HEAD_DIM)
    va = va.reshape(B, S, A_KV_HEADS, HEAD_DIM)
    oa = blocked_dense_attention(qa, ka, va, HEAD_DIM ** -0.5)
    qb = apply_rope(qb.reshape(B, S, B_HEADS, HEAD_DIM), ang_1d)
    kb = apply_rope(kb.reshape(B, S, B_KV_HEADS, HEAD_DIM), ang_1d)
    qb = qb.reshape(B, S, B_KV_HEADS, B_HEADS // B_KV_HEADS, HEAD_DIM)
    vb = vb.reshape(B, S, B_KV_HEADS, HEAD_DIM)
    ob = window_attention_with_sink(qb, kb, vb, sink, HEAD_DIM ** -0.5)
    return jnp.concatenate([oa, ob], axis=-1) @ w_out


def mixer_cd(h, w_in, g_cq, g_ckv, w_uq, w_ukv, rpb, w_out, ang_c):
    B, S, _ = h.shape
    cq, ckv, kr, qd, kd, vd = _split(h @ w_in, CD_SIZES)
    q = (rms_norm(cq, g_cq) @ w_uq).reshape(B, S, C_HEADS, C_NOPE + C_ROPE)
    q_nope, q_rope = jnp.split(q, [C_NOPE], axis=-1)
    q_rope = apply_rope(q_rope, ang_c)
    kv = (rms_norm(ckv, g_ckv) @ w_ukv).reshape(B, S, C_HEADS, C_NOPE + C_V)
    k_nope, v_c = jnp.split(kv, [C_NOPE], axis=-1)
    k_rope = apply_rope(kr.reshape(B, S, 1, C_ROPE), ang_c)
    qc = jnp.concatenate([q_nope, q_rope], axis=-1)[:, :, :, None, :]
    kc = jnp.concatenate([k_nope, jnp.broadcast_to(k_rope, (B, S, C_HEADS, C_ROPE))], axis=-1)
    oc = blocked_dense_attention(qc, kc, v_c, (C_NOPE + C_ROPE) ** -0.5)
    od = neighbourhood_attention(qd.reshape(B, S, D_HEADS, HEAD_DIM),
                                 kd.reshape(B, S, D_HEADS, HEAD_DIM),
                                 vd.reshape(B, S, D_HEADS, HEAD_DIM), rpb, HEAD_DIM ** -0.5)
    return jnp.concatenate([oc, od], axis=-1) @ w_out


def memory_cross_attention(h, m, w_q, w_kv, w_o):
    B, S, _ = h.shape
    q = (h @ w_q).reshape(B, S, X_HEADS, X_HEAD_DIM)
    k, v = jnp.split(m @ w_kv, 2, axis=-1)
    k = k.reshape(B, m.shape[1], X_HEADS, X_HEAD_DIM)
    v = v.reshape(B, m.shape[1], X_HEADS, X_HEAD_DIM)
    s = jnp.einsum('bqhd,bkhd->bhqk', q, k, preferred_element_type=jnp.float32) * X_HEAD_DIM ** -0.5
    p = jax.nn.softmax(s, axis=-1).astype(v.dtype)
    o = jnp.einsum('bhqk,bkhd->bqhd', p, v).reshape(B, S, X_HEADS * X_HEAD_DIM)
    return o @ w_o


def swiglu(h, w_gate_up, w_down):
    g, u = jnp.split(h @ w_gate_up, 2, axis=-1)
    return (jax.nn.silu(g) * u) @ w_down


def setup_inputs(seed: int = 0) -> dict:
    key = jax.random.key(seed)
    ks = jax.random.split(key, 24)
    n_even = (DEPTH + 1) // 2
    n_odd = DEPTH // 2

    def w(k, shape, fan_in):
        return jax.random.normal(k, shape, jnp.float32) * fan_in ** -0.5

    def gain(k, shape):
        return 1.0 + 0.05 * jax.random.normal(k, shape, jnp.float32)

    return {
        'x': jax.random.normal(ks[0], (BATCH, SEQ, D_MODEL), jnp.float32),
        'mem': jax.random.normal(ks[1], (BATCH, MEM_LEN, D_MODEL), jnp.float32),
        'g_mix': gain(ks[2], (DEPTH, D_MODEL)),
        'w_in_ab': w(ks[3], (n_even, D_MODEL, IN_AB), D_MODEL),
        'g_qa': gain(ks[4], (n_even, HEAD_DIM)),
        'g_ka': gain(ks[5], (n_even, HEAD_DIM)),
        'sink_b': jax.random.normal(ks[6], (n_even, B_HEADS), jnp.float32),
        'w_out_ab': w(ks[7], (n_even, MIX_AB, D_MODEL), MIX_AB),
        'w_in_cd': w(ks[8], (n_odd, D_MODEL, IN_CD), D_MODEL),
        'g_cq': gain(ks[9], (n_odd, C_Q_RANK)),
        'g_ckv': gain(ks[10], (n_odd, C_KV_RANK)),
        'w_uq': w(ks[11], (n_odd, C_Q_RANK, C_HEADS * (C_NOPE + C_ROPE)), C_Q_RANK),
        'w_ukv': w(ks[12], (n_odd, C_KV_RANK, C_HEADS * (C_NOPE + C_V)), C_KV_RANK),
        'rpb_d': 0.1 * jax.random.normal(ks[13], (n_odd, D_HEADS, 2 * D_WIN_R - 1, 2 * D_WIN_C - 1), jnp.float32),
        'w_out_cd': w(ks[14], (n_odd, MIX_CD, D_MODEL), MIX_CD),
        'g_xq': gain(ks[15], (DEPTH, D_MODEL)),
        'g_mem': gain(ks[16], (DEPTH, D_MODEL)),
        'w_xq': w(ks[17], (DEPTH, D_MODEL, X_HEADS * X_HEAD_DIM), D_MODEL),
        'w_xkv': w(ks[18], (DEPTH, D_MODEL, 2 * X_HEADS * X_HEAD_DIM), D_MODEL),
        'w_xo': w(ks[19], (DEPTH, X_HEADS * X_HEAD_DIM, D_MODEL), X_HEADS * X_HEAD_DIM),
        'g_ffn': gain(ks[20], (DEPTH, D_MODEL)),
        'w_gate_up': w(ks[21], (DEPTH, D_MODEL, 2 * D_FF), D_MODEL),
        'w_down': w(ks[22], (DEPTH, D_FF, D_MODEL), D_FF),
        'g_final': gain(ks[23], (D_MODEL,)),
    }


def reference(x, mem, g_mix, w_in_ab, g_qa, g_ka, sink_b, w_out_ab, w_in_cd, g_cq, g_ckv,
              w_uq, w_ukv, rpb_d, w_out_cd, g_xq, g_mem, w_xq, w_xkv, w_xo, g_ffn,
              w_gate_up, w_down, g_final):
    S = x.shape[1]
    pos = jnp.arange(S)
    row = pos // GRID_W
    col = pos % GRID_W
    ang_1d = rope_angles(pos, HEAD_DIM)
    ang_2d = jnp.concatenate([rope_angles(row, HEAD_DIM // 2),
                              rope_angles(col, HEAD_DIM // 2)], axis=-1)
    ang_c = rope_angles(pos, C_ROPE)
    for i in range(DEPTH):
        j = i // 2
        h = rms_norm(x, g_mix[i])
        if i % 2 == 0:
            x = x + mixer_ab(h, w_in_ab[j], g_qa[j], g_ka[j], sink_b[j], w_out_ab[j],
                             ang_1d, ang_2d)
        else:
            x = x + mixer_cd(h, w_in_cd[j], g_cq[j], g_ckv[j], w_uq[j], w_ukv[j],
                             rpb_d[j], w_out_cd[j], ang_c)
        x = x + memory_cross_attention(rms_norm(x, g_xq[i]), rms_norm(mem, g_mem[i]),
                                       w_xq[i], w_xkv[i], w_xo[i])
        x = x + swiglu(rms_norm(x, g_ffn[i]), w_gate_up[i], w_down[i])
    return rms_norm(x, g_final)
```

```python
import os
import numpy as np
import concourse.bass as bass
import concourse.mybir as mybir
from concourse.bass_utils import run_bass_kernel_spmd

F32 = mybir.dt.float32
BF16 = mybir.dt.bfloat16
ALU = mybir.AluOpType
AF = mybir.ActivationFunctionType

SEM_ROTATE = 30000
NCORES = 8
S = 2048
D = 1024
DFF = 2816
EPS = 1e-6


class Tile:
    __slots__ = ("ap", "name", "last_w", "readers", "excl", "lw_read")

    def __init__(self, ap, name="", excl=False):
        self.ap = ap
        self.name = name
        self.last_w = None
        self.readers = {}
        self.excl = excl
        self.lw_read = False

    def __getitem__(self, idx):
        return self.ap[idx]


class SemCounter:
    def __init__(self, fw, name, step=1):
        self.fw = fw
        self.name = name
        self.step = step
        self.gen = 0
        self.sem = fw.nc.alloc_semaphore(f"{name}_{self.gen}")
        self.count = 0
        fw.all_ctrs.append(self)

    def next_token(self):
        if (self.count + 1) * self.step > SEM_ROTATE:
            self.gen += 1
            self.sem = self.fw.nc.alloc_semaphore(f"{self.name}_{self.gen}")
            self.count = 0
        return (self.sem, (self.count + 1) * self.step)

    def commit(self):
        self.count += 1

    def cur_token(self):
        if self.count == 0:
            return None
        return (self.sem, self.count * self.step)


class Engine:
    def __init__(self, fw, key):
        self.key = key
        self.ops = []
        self.ctr = SemCounter(fw, f"s_{key}")
        self.seen = {}


class FW:
    def __init__(self, nc):
        self.nc = nc
        self.all_ctrs = []
        self.eng = {k: Engine(self, k) for k in ("pe", "act", "dve", "pool", "sp")}

    def dmasem(self, name):
        return SemCounter(self, name, step=16)

    def _collect(self, e, reads, writes, skip_self=False):
        waits = {}

        def add(tok):
            if tok is None:
                return
            s, v = tok
            if skip_self and s is e.ctr.sem:
                return
            if e.seen.get(s, 0) >= v:
                return
            if waits.get(s, 0) < v:
                waits[s] = v

        for t in reads:
            if t.excl and t.lw_read and t.last_w is not None and t.last_w[0] is e.ctr.sem:
                continue
            add(t.last_w)
            if t.excl:
                for s, v in t.readers.items():
                    add((s, v))
        for t in writes:
            add(t.last_w)
            for s, v in t.readers.items():
                add((s, v))
        for s, v in waits.items():
            e.seen[s] = v
        return list(waits.items())

    def _update(self, tok, reads, writes):
        for t in reads:
            if t.excl:
                t.last_w = tok
                t.lw_read = True
            else:
                t.readers[tok[0]] = max(t.readers.get(tok[0], 0), tok[1])
        for t in writes:
            t.last_w = tok
            t.lw_read = False
            t.readers = {}

    def emit(self, ek, fn, reads=(), writes=(), inc=True):
        e = self.eng[ek]
        waits = self._collect(e, reads, writes, skip_self=(ek == "pe"))
        tok = e.ctr.next_token()
        if inc:
            e.ctr.commit()
        e.ops.append((waits, fn, tok if inc else None))
        self._update(tok, reads, writes)
        return tok

    def dma(self, qk, out, in_, sem, reads=(), writes=()):
        e = self.eng[qk]
        waits = self._collect(e, reads, writes)
        tok = sem.next_token()
        sem.commit()

        def fn(engine, out=out, in_=in_):
            return engine.dma_start(out=out, in_=in_)

        e.ops.append((waits, fn, tok + (True,)))
        self._update(tok, reads, writes)
        return tok

    def wait_tokens(self, ek, toks):
        e = self.eng[ek]
        waits = []
        for tok in toks:
            if tok is None:
                continue
            s, v = tok
            if e.seen.get(s, 0) < v:
                e.seen[s] = v
                waits.append((s, v))
        if waits:
            e.ops.append((waits, None, None))

    def barrier(self):
        toks = [c.cur_token() for c in self.all_ctrs]
        for ek in self.eng:
            self.wait_tokens(ek, toks)

    def build(self):
        nc = self.nc
        handles = {"pe": "tensor", "act": "scalar", "dve": "vector", "pool": "gpsimd", "sp": "sync"}
        with nc.Block() as block:
            for ek, attr in handles.items():
                e = self.eng[ek]
                if not e.ops:
                    continue

                def body(engine, e=e):
                    for waits, fn, tok in e.ops:
                        for s, v in waits:
                            engine.wait_ge(s, v)
                        if fn is None:
                            continue
                        ins = fn(engine)
                        if tok is not None:
                            ins.then_inc(tok[0], 16 if len(tok) == 3 else 1)

                getattr(block, attr)(body)


def _rope_inv(dim):
    return 10000.0 ** (-np.arange(0, dim, 2, dtype=np.float64) / dim)


def host_consts():
    pos = np.arange(S, dtype=np.float64)
    row = np.floor(pos / 64)
    col = pos % 64
    ang1d = pos[:, None] * _rope_inv(64)[None, :]
    ang2d = np.concatenate([row[:, None] * _rope_inv(32)[None, :],
                            col[:, None] * _rope_inv(32)[None, :]], -1)
    angc = pos[:, None] * _rope_inv(32)[None, :]

    def tab64(ang):
        cos = np.zeros((128, S)); sin = np.zeros((128, S))
        for p in range(128):
            d = p % 64
            cos[p] = np.cos(ang[:, d % 32])
            sin[p] = np.sin(ang[:, d % 32]) * (-1.0 if d < 32 else 1.0)
        return cos, sin

    c2, s2 = tab64(ang2d)
    c1, s1 = tab64(ang1d)
    cc = np.zeros((128, S)); sc = np.zeros((128, S))
    cc[0:64] = 1.0
    for p in range(64, 96):
        d = p - 64
        cc[p] = np.cos(angc[:, d % 16])
        sc[p] = np.sin(angc[:, d % 16]) * (-1.0 if d < 16 else 1.0)
    tables = np.stack([c2, s2, c1, s1, cc, sc]).astype(np.float32)

    perm = np.zeros((4, 128, 128), np.float32)
    for m in range(128):
        k = (m % 64 + 32) % 64 + 64 * (m // 64)
        perm[0, k, m] = 1.0
    for m in range(64, 96):
        k = 64 + ((m - 64 + 16) % 32)
        perm[1, k, m] = 1.0
    for k in range(128):
        for m in range(128):
            if k // 64 == m // 64:
                perm[2, k, m] = 1.0
    perm[3] = 1.0

    bm = np.zeros((6, 128, 512), np.float32)
    p = np.arange(128)[:, None]; f = np.arange(512)[None, :]
    for i, rel in enumerate(range(-1, 5)):
        bm[i] = np.where(np.abs(128 * rel + p - f) <= 128, 0.0, -30000.0).astype(np.float32)

    c = np.arange(64)
    c0 = np.clip(c - 8, 0, 48)
    colvalid = ((c[None, :] >= c0[:, None]) & (c[None, :] < c0[:, None] + 16))
    cv = colvalid.T.astype(np.float32)
    dmask = np.zeros((2, 2, 64, 16, 64), np.float32)
    for a_ in range(2):
        for jj in range(16):
            jr = jj - a_
            if 0 <= jr <= 14:
                dmask[0, a_, :, jj, :] = cv
            if 4 <= jr <= 11:
                dmask[1, a_, :, jj, :] = cv
    dcol = dmask.reshape(2, 128, 1024)
    return tables, perm, bm, dcol.astype(np.float32)


def d_row_valid(r, kr):
    r0 = min(max(r - 4, 0), 24)
    return r0 <= kr < r0 + 8


def build_program(depth=4, dbg_stop=None):
    nc = bass.Bass("TRN2", target_bir_lowering=False)
    fw = FW(nc)

    def din(name, shape):
        return nc.dram_tensor(name, list(shape), F32, kind="ExternalInput").ap()

    xT_d = din("xT", [8, 128, S])
    memT_d = din("memT", [8, 128, 256])
    w_in_ab = din("w_in_ab", [2, D, 1536]); w_out_ab = din("w_out_ab", [2, D, D])
    w_in_cd = din("w_in_cd", [2, D, 1952]); w_out_cd = din("w_out_cd", [2, D, D])
    w_uq = din("w_uq", [2, 256, 768]); w_ukv = din("w_ukv", [2, 128, 1024])
    w_xq = din("w_xq", [4, D, 512]); w_xkv = din("w_xkv", [4, D, D]); w_xo = din("w_xo", [4, 512, D])
    w_gu = din("w_gate_up", [4, D, 2 * DFF]); w_dn = din("w_down", [4, DFF, D])
    NG = 4 * 32 + 8 + 2 * 4 + 2 * 3
    gcols_d = din("gcols", [128, NG])
    sink_d = din("sinkb", [128, 16])
    tables_d = din("tables", [6, 128, S])
    perm_d = din("perm", [4, 128, 128])
    ident_d = din("ident", [128, 128])
    bmask_d = din("bmask", [6, 128, 512])
    dmask_d = din("dmask", [2, 128, 1024])
    dneg_d = din("dneg", [2, 128, 1024])
    dbias_d = din("dbias", [2, 8, 128, 1024])
    out_d = nc.dram_tensor("outT", [8, 128, S], F32, kind="ExternalOutput").ap()
    dbg_d = nc.dram_tensor("dbgT", [8, 128, S], F32, kind="ExternalOutput").ap() if dbg_stop is not None else None

    xT_h = nc.alloc_sbuf_tensor("xT_sb", [128, 8, S], F32)
    xT = [[Tile(xT_h[:, c, 512 * t:512 * (t + 1)], f"x{c}_{t}") for t in range(4)] for c in range(8)]
    perm_h = nc.alloc_sbuf_tensor("perm_sb", [128, 4, 128], BF16)
    permT = Tile(perm_h, "perm")
    gcols_h = nc.alloc_sbuf_tensor("gcols_sb", [128, NG], F32)
    gcolsT = Tile(gcols_h, "gcols")
    sink_h = nc.alloc_sbuf_tensor("sink_sb", [128, 16], F32)
    sinkT = Tile(sink_h, "sink")
    ARENA = nc.sbuf_bytes_remaining - 64
    ARENA -= ARENA % 64
    R = nc.alloc_sbuf_tensor("arena", [128, ARENA // 2], BF16)

    class Carver:
        def __init__(self):
            self.off = 0

        def bf(self, n):
            o = self.off
            self.off += (2 * n + 63) // 64 * 64
            assert self.off <= ARENA, f"arena overflow {self.off} > {ARENA}"
            return R[:, o // 2:o // 2 + n]

        def f32(self, n):
            return self.bf(2 * n).bitcast(F32)

    psb = [nc.alloc_psum_tensor(f"psb{i}", [128, 1024], F32) for i in range(4)]
    bank = []
    for i in range(4):
        bank.append(Tile(psb[i][:, 0:512], f"bank{2 * i}", excl=True))
        bank.append(Tile(psb[i][:, 512:1024], f"bank{2 * i + 1}", excl=True))

    s_x = [fw.dmasem(f"dx{t}") for t in range(4)]
    s_c = fw.dmasem("dconst")
    s_out = fw.dmasem("dout")

    for t in range(4):
        for c in range(8):
            tk = fw.dma("sp", xT[c][t][:, :], xT_d[c, :, 512 * t:512 * (t + 1)], s_x[t], writes=[xT[c][t]])
        for c in range(8):
            xT[c][t].last_w = tk
    fw.dma("pool", permT[:, :, :], perm_d.rearrange("a p m -> p a m"), s_c, writes=[permT])
    fw.dma("sp", gcolsT[:, :], gcols_d, fw.dmasem("dconst2"), writes=[gcolsT])
    fw.dma("sp", sinkT[:, :], sink_d, fw.dmasem("dconst3"), writes=[sinkT])
    s_wq = [fw.dmasem(f"wq_{i}") for i in range(2)]
    s_wk = fw.dmasem("wk"); s_wv = fw.dmasem("wv"); s_tab = [fw.dmasem("tab0"), fw.dmasem("tab1")]; s_misc = fw.dmasem("misc"); s_misc2 = fw.dmasem("misc2"); s_misc3 = fw.dmasem("misc3"); s_misc4 = fw.dmasem("misc4"); s_misc5 = fw.dmasem("misc5"); s_misc6 = fw.dmasem("misc6")
    s_mem = fw.dmasem("mem")
    s_wg = [fw.dmasem(f"wg_{i}") for i in range(4)]
    s_wd = [fw.dmasem(f"wd_{i}") for i in range(3)]
    eps_h = nc.alloc_sbuf_tensor("eps_sb", [128, 2], F32)
    epsT = Tile(eps_h, "eps")
    fw.emit("dve", lambda e: e.memset(epsT[:, 0:1], EPS), writes=[epsT])
    fw.emit("dve", lambda e: e.memset(epsT[:, 1:2], 64.0 * EPS), writes=[epsT])

    def rsqrt_ps(b, b_ap, o, o_ap, scale, which, rows):
        fw.emit("act", lambda e: e.activation(out=o_ap, in_=b_ap, func=AF.Ln, bias=epsT[rows, which:which + 1], scale=scale),
                reads=[b, epsT], writes=[o])
        fw.emit("act", lambda e: e.activation(out=o_ap, in_=o_ap, func=AF.Exp, scale=-0.5), reads=[o], writes=[o])

    fw.emit("act", lambda e: e.activation(out=sinkT[:, :], in_=sinkT[:, :], func=AF.Exp), reads=[sinkT], writes=[sinkT])

    swap64 = lambda r: permT[r, 0, r]
    swapC = lambda r: permT[r, 1, r]

    def gcol(i, rows=slice(0, 128)):
        return gcolsT[rows, i:i + 1]

    G_MIX, G_XQ, G_MEM, G_FFN = 0, 8, 16, 24
    G_FINAL = 128
    G_QA = lambda j: 136 + 4 * j
    G_CQ = lambda j: 144 + 3 * j

    class BankRR:
        def __init__(self, ids):
            self.ids = ids
            self.i = 0

        def get(self, exclude=()):
            while True:
                b = bank[self.ids[self.i % len(self.ids)]]
                self.i += 1
                if not any(b is x for x in exclude):
                    return b

    def mm_group(out_ap, out_tile, pairs, reads):
        n = len(pairs)
        for i, (l, r) in enumerate(pairs):
            fw.emit("pe", lambda e, l=l, r=r, i=i: e.matmul(out_ap, lhsT=l, rhs=r, start=(i == 0), stop=(i == n - 1)),
                    reads=reads, writes=[out_tile], inc=(i == n - 1))

    def load_w(dst_tile, dst_ap, src_ap, sem):
        return fw.dma("pool", dst_ap, src_ap, sem, writes=[dst_tile])

    def rmsnorm_T(src, ncols, gbase, dst, dst_tiles, nchunks, brr, tmp, inv_n, ones_ap, rows=slice(0, 128), src_tiles=None):
        fast = len(tmp) == 3
        sq_t = tmp[0]
        rs_l = tmp[1] if fast else [tmp[1]]
        xr_l = tmp[2] if fast else None
        ntt = ncols // 512 if ncols >= 512 else 1
        w = min(512, ncols)
        n_act_sq = 5 if fast else (nchunks + 1) // 2

        def stats(t):
            cs = slice(w * t, w * (t + 1))
            b = brr.get()
            rs_t = rs_l[t % len(rs_l)]
            for c in range(nchunks):
                st = src_tiles(c, t)
                on_act = (c % 2 == 0) if not fast else (c not in (2, 6))
                sqb = sq_t[c % len(sq_t)]
                if on_act:
                    fw.emit("act", lambda e, c=c, cs=cs, sqb=sqb: e.activation(out=sqb[rows, 0:w], in_=src(c, cs), func=AF.Square),
                            reads=[st], writes=[sqb])
                else:
                    fw.emit("pool", lambda e, c=c, cs=cs, sqb=sqb: e.tensor_tensor(out=sqb[rows, 0:w], in0=src(c, cs), in1=src(c, cs), op=ALU.mult),
                            reads=[st], writes=[sqb])
                fw.emit("pe", lambda e, c=c, b=b, sqb=sqb: e.matmul(b[rows, 0:w], lhsT=ones_ap, rhs=sqb[rows, 0:w], start=(c == 0), stop=(c == nchunks - 1)),
                        reads=[sqb, permT], writes=[b], inc=True)
            if not fast:
                rsqrt_ps(b, b[rows, 0:w], rs_t, rs_t[rows, 0:w], inv_n, 0, rows)
            return b

        def rsq(t, b):
            rs_t = rs_l[t % len(rs_l)]
            rsqrt_ps(b, b[rows, 0:w], rs_t, rs_t[rows, 0:w], inv_n, 0, rows)

        def apply(t):
            cs = slice(w * t, w * (t + 1))
            rs_t = rs_l[t % len(rs_l)]
            for c in range(nchunks):
                st = src_tiles(c, t)
                if fast and c in (3, 6):
                    xr = xr_l[0 if c == 3 else 1]
                    fw.emit("pool", lambda e, c=c, cs=cs, xr=xr, rs_t=rs_t: e.tensor_tensor(out=xr[rows, 0:w], in0=src(c, cs), in1=rs_t[rows, 0:w], op=ALU.mult),
                            reads=[st, rs_t], writes=[xr])
                    fw.emit("act", lambda e, c=c, cs=cs, xr=xr: e.activation(out=dst(c, cs), in_=xr[rows, 0:w], func=AF.Copy, scale=gcol(gbase + c, rows)),
                            reads=[xr, gcolsT], writes=[dst_tiles(c, t)])
                else:
                    fw.emit("dve", lambda e, c=c, cs=cs, rs_t=rs_t: e.scalar_tensor_tensor(out=dst(c, cs), in0=src(c, cs), scalar=gcol(gbase + c, rows), in1=rs_t[rows, 0:w], op0=ALU.mult, op1=ALU.mult),
                            reads=[st, rs_t, gcolsT], writes=[dst_tiles(c, t)])

        if fast:
            banks_ = {}
            for t in range(ntt + 2):
                if t < ntt:
                    banks_[t] = stats(t)
                if 1 <= t <= ntt:
                    rsq(t - 1, banks_[t - 1])
                if t >= 2:
                    apply(t - 2)
        else:
            for t in range(ntt):
                stats(t)
                apply(t)

    for L in range(depth):
        even = (L % 2 == 0)
        j = L // 2
        gb = 32 * L
        if L > 0:
            fw.barrier()
        cv = Carver()
        hT_ap = cv.bf(8 * S).rearrange("p (c s) -> p c s", s=S)
        hT = [[Tile(hT_ap[:, c, 512 * t:512 * (t + 1)], f"h{c}_{t}") for t in range(4)] for c in range(8)]
        OT_ap = cv.bf(8 * S).rearrange("p (c s) -> p c s", s=S)
        OT = [[Tile(OT_ap[:, c, 512 * t:512 * (t + 1)], f"o{c}_{t}") for t in range(4)] for c in range(8)]
        cosT = Tile(cv.bf(S), "cos"); sinT = Tile(cv.bf(S), "sin")
        Qap = cv.bf(S); Kap = cv.bf(S)
        Qt = [Tile(Qap[:, 512 * t:512 * (t + 1)], f"q{t}") for t in range(4)]
        Kt = [Tile(Kap[:, 512 * t:512 * (t + 1)], f"k{t}") for t in range(4)]
        Vap = cv.bf(16 * 2 * 192).rearrange("p (t s d) -> p t s d", s=2, d=192)
        Vt = [Tile(Vap[:, 4 * g:4 * (g + 1), :, :], f"v{g}") for g in range(4)]
        Pt = [Tile(cv.bf(1024), f"P{i}") for i in range(2)]
        NB_ROPE = 3 if even else 1
        SKEW = 2 if even else 1
        live_banks = []
        qsb_l = [Tile(cv.bf(512), f"qsb{i}") for i in range(3 if even else 2)]
        sq = [Tile(cv.bf(512), f"sq{i}") for i in range(3 if even else 2)]
        t1_l = [Tile(cv.f32(512), f"t1_{i}") for i in range(NB_ROPE)]
        t2_l = [Tile(cv.f32(512), f"t2_{i}") for i in range(NB_ROPE)]
        spt_l = [Tile(cv.f32(512), f"sp{i}") for i in range(NB_ROPE)] if even else None
        t1 = t1_l[0]; t2 = t2_l[0]
        rope_ctr = [0]
        rec_l = [Tile(cv.f32(512), "rec0"), Tile(cv.f32(512), "rec1")]
        rec = rec_l[0]
        rs_t = Tile(cv.f32(512), "rs")
        if not even:
            spt_l = [rs_t]
        ntmp = ([Tile(Qap[:, 512 * i:512 * (i + 1)], f"nsq{i}") for i in range(4)],
                [Tile(Kap[:, 1024 * i:1024 * (i + 1)].bitcast(F32), f"nrs{i}") for i in range(2)],
                [Tile(Vap.rearrange("p t s d -> p (t s d)")[:, 1024 * i:1024 * (i + 1)].bitcast(F32), f"nxr{i}") for i in range(2)])
        wq = [Tile(cv.bf(8 * 128).rearrange("p (k m) -> p k m", m=128), f"wq{i}") for i in range(2)]
        wk = Tile(cv.bf(8 * 128).rearrange("p (k m) -> p k m", m=128), "wk")
        wv = Tile(cv.bf(8 * 128).rearrange("p (k m) -> p k m", m=128), "wv")
        if even:
            identT = Tile(cv.bf(128), "ident")
            fw.dma("pool", identT[:, :], ident_d, s_misc6, writes=[identT])
            bmask = Tile(cv.bf(6 * 512).rearrange("p (a f) -> p a f", f=512), "bmask")
            fw.dma("pool", bmask[:, :, :], bmask_d.rearrange("a p f -> p a f"), s_misc, writes=[bmask])
        else:
            cqn_ap = cv.bf(2 * S).rearrange("p (c s) -> p c s", s=S)
            cqn = [[Tile(cqn_ap[:, c, 512 * t:512 * (t + 1)], f"cq{c}_{t}") for t in range(4)] for c in range(2)]
            Pt.append(Tile(cqn_ap[:, 0, 0:1024], "P2alias"))
            ckvn_ap = cv.bf(S)
            ckvn = [Tile(ckvn_ap[:, 512 * t:512 * (t + 1)], f"ckv{t}") for t in range(4)]
            kr_ap = cv.bf(S)
            krT = [Tile(kr_ap[:, 512 * t:512 * (t + 1)], f"kr{t}") for t in range(4)]
            dEf = Tile(cv.bf(2 * 1024).rearrange("p (h f) -> p h f", f=1024), "dEf")
            dEi = Tile(cv.bf(2 * 1024).rearrange("p (h f) -> p h f", f=1024), "dEi")
        brr = BankRR([0, 1, 2, 3, 4, 5, 6, 7])
        ones_all = permT[:, 3, :]
        blockones = permT[:, 2, :]

        rmsnorm_T(lambda c, cs: xT_h[:, c, cs], S, gb + G_MIX, lambda c, cs: hT_ap[:, c, cs], lambda c, t: hT[c][t],
                  8, brr, ntmp, 1.0 / D, ones_all, src_tiles=lambda c, t: xT[c][t])
        fw.barrier()

        def proj_bank(b, M, wt, wcols, src_ap, src_tiles, KC, t, krows=slice(0, 128)):
            pairs = [(wt[krows, kc, wcols], src_ap(kc, slice(512 * t, 512 * (t + 1)))) for kc in range(KC)]
            mm_group(b[0:M, :], b, pairs, [wt] + [src_tiles(kc, t) for kc in range(KC)])

        def load_tables(i):
            fw.dma("pool", cosT[:, :], tables_d[2 * i], s_tab[0], writes=[cosT])
            fw.dma("pool", sinT[:, :], tables_d[2 * i + 1], s_tab[1], writes=[sinT])

        def run_chains(chains):
            n = len(chains)
            ctxs = [None] * n
            for i in range(n + SKEW):
                if i < n:
                    ctxs[i] = chains[i][0]()
                if i >= SKEW:
                    chains[i - SKEW][1](ctxs[i - SKEW])

        def mk_chain(wt, M, wcols, src_ap, src_tiles, KC, t, rows, dst_t, mode, swap_ap=None, gains=None, norm=False, after=None):
            def A():
                b = brr.get(exclude=live_banks)
                live_banks.append(b)
                proj_bank(b, M, wt, wcols, src_ap, src_tiles, KC, t)
                if mode == "plain":
                    return (b, None, 0)
                qi = rope_ctr[0] % len(qsb_l)
                ri = rope_ctr[0] % NB_ROPE
                rope_ctr[0] += 1
                qsb = qsb_l[qi]
                fw.emit("act", lambda e: e.activation(out=qsb[rows, :], in_=b[rows, :], func=AF.Copy), reads=[b], writes=[qsb])
                return (b, qsb, ri)

            def B(ctx):
                b, qsb, ri = ctx
                if mode == "plain":
                    fw.emit("dve", lambda e: e.tensor_copy(out=dst_t[t][rows, :], in_=b[rows, :]), reads=[b], writes=[dst_t[t]])
                else:
                    rope_B(qsb, ri, dst_t, t, rows, swap_ap, gains, norm, b)
                live_banks.remove(b)
                if after is not None:
                    after()
            return (A, B)

        def rope_B(qsb, ri, dst_t, t, rows, swap_ap, gains=None, norm=False, b=None):
            tok = slice(512 * t, 512 * (t + 1))
            t1 = t1_l[ri]; t2 = t2_l[ri]; spt = spt_l[ri]; sqr = sq[ri % len(sq)]
            b2 = brr.get(exclude=live_banks)
            fw.emit("pe", lambda e: e.matmul(b2[rows, :], lhsT=swap_ap, rhs=qsb[rows, :], start=True, stop=True),
                    reads=[qsb, permT], writes=[b2])
            if norm:
                fw.emit("pool", lambda e: e.tensor_tensor(out=sqr[rows, :], in0=qsb[rows, :], in1=qsb[rows, :], op=ALU.mult),
                        reads=[qsb], writes=[sqr])
                b3 = brr.get(exclude=live_banks + [b2])
                fw.emit("pe", lambda e: e.matmul(b3[rows, :], lhsT=blockones, rhs=sqr[rows, :], start=True, stop=True),
                        reads=[sqr, permT], writes=[b3])
                rsqrt_ps(b3, b3[rows, :], spt, spt[rows, :], 1.0, 1, rows)
            if gains is not None:
                g0, g1 = gains
                fw.emit("dve", lambda e: e.scalar_tensor_tensor(out=t1[rows, :], in0=b[rows, :], scalar=gcol(g0, rows), in1=cosT[rows, tok], op0=ALU.mult, op1=ALU.mult),
                        reads=[b, cosT, gcolsT], writes=[t1])
                fw.emit("dve", lambda e: e.scalar_tensor_tensor(out=t2[rows, :], in0=b2[rows, :], scalar=gcol(g1, rows), in1=sinT[rows, tok], op0=ALU.mult, op1=ALU.mult),
                        reads=[b2, sinT, gcolsT], writes=[t2])
            else:
                fw.emit("dve", lambda e: e.tensor_tensor(out=t1[rows, :], in0=qsb[rows, :], in1=cosT[rows, tok], op=ALU.mult),
                        reads=[qsb, cosT], writes=[t1])
                fw.emit("dve", lambda e: e.tensor_tensor(out=t2[rows, :], in0=b2[rows, :], in1=sinT[rows, tok], op=ALU.mult),
                        reads=[b2, sinT], writes=[t2])
            if norm:
                fw.emit("dve", lambda e: e.tensor_tensor(out=t1[rows, :], in0=t1[rows, :], in1=t2[rows, :], op=ALU.add),
                        reads=[t1, t2], writes=[t1])
                fw.emit("dve", lambda e: e.tensor_tensor(out=dst_t[t][rows, :], in0=t1[rows, :], in1=spt[rows, :], op=ALU.mult),
                        reads=[t1, spt], writes=[dst_t[t]])
            else:
                fw.emit("pool", lambda e: e.tensor_tensor(out=dst_t[t][rows, :], in0=t1[rows, :], in1=t2[rows, :], op=ALU.add),
                        reads=[t1, t2], writes=[dst_t[t]])

        def plain_evac(b, dst_t, t, rows):
            fw.emit("dve", lambda e: e.tensor_copy(out=dst_t[t][rows, :], in_=b[rows, :]), reads=[b], writes=[dst_t[t]])

        def init_v_ones():
            for g in range(4):
                fw.emit("pool", lambda e, g=g: e.memset(Vt[g][:, :, :, :], 1.0), writes=[Vt[g]])

        def v_proj(wt, ncols, src_ap, src_tiles, KC, nslots):
            for g in range(4):
                b = brr.get()
                for q in range(4):
                    tt = 4 * g + q
                    pairs = [(src_ap(kc, slice(128 * tt, 128 * (tt + 1))), wt[:, kc, 0:ncols]) for kc in range(KC)]
                    mm_group(b[:, 128 * q:128 * q + ncols], b, pairs, [wt] + [src_tiles(kc, tt // 4) for kc in range(KC)])
                fw.emit("dve", lambda e, g=g, b=b: e.tensor_copy(
                    out=Vt[g][:, :, 0:nslots, 64:128],
                    in_=b[:, :].rearrange("p (q s d) -> p q s d", q=4, s=2)[:, :, 0:nslots, :]),
                    reads=[b], writes=[Vt[g]])

        def attention(slots, items_for_T, scale, chunk, addcols, vslot, post_exp=None, lookahead=1, act_recip=False, nS=2, o_single=False, add_mm=None, act_recip_last=False):
            work = []
            for T in range(4):
                its = items_for_T(T)
                for ii, it in enumerate(its):
                    work.append((T, it, ii == 0, ii == len(its) - 1))
            started = {}

            def crange(it):
                cr = it[0][3] if len(it[0]) > 3 and it[0][3] is not None else (0, 512)
                return cr

            def emit_qk(w, idx):
                T, it, first, last = w
                c0, c1 = crange(it)
                for bi, sub in enumerate(it):
                    sl, kt = sub[0], sub[1]
                    b = bank[2 * (idx % nS) + bi]
                    qt, qr = slots[sl]["q"]; ktl, kr = slots[sl]["k"]
                    am = add_mm(T, sub, c0, c1) if add_mm is not None else None
                    fw.emit("pe", lambda e, b=b, qt=qt, qr=qr, ktl=ktl, kr=kr, kt=kt, T=T, c0=c0, c1=c1, am=am: e.matmul(
                        b[:, 0:c1 - c0], lhsT=ktl[kt // 4][kr, 128 * (kt % 4):128 * (kt % 4 + 1)], rhs=qt[T][qr, c0:c1], start=True, stop=(am is None)),
                        reads=[ktl[kt // 4], qt[T]], writes=[b], inc=(am is None))
                    if am is not None:
                        aap, atiles = am
                        fw.emit("pe", lambda e, b=b, aap=aap, c0=c0, c1=c1, idt=identT: e.matmul(
                            b[:, 0:c1 - c0], lhsT=idt[:, :], rhs=aap, start=False, stop=True),
                            reads=[identT] + atiles, writes=[b])

            def emit_rest(w, idx):
                T, it, first, last = w
                n = len(it)
                c0, c1 = crange(it)
                nc_ = c1 - c0
                bs = [bank[2 * (idx % nS) + bi] for bi in range(n)]
                P = Pt[idx % nS]
                src = psb[idx % nS]
                if nc_ == 512:
                    fw.emit("act", lambda e: e.activation(out=P[:, 0:512 * n], in_=src[:, 0:512 * n], func=AF.Exp, scale=scale),
                            reads=bs, writes=[P])
                else:
                    fw.emit("act", lambda e: e.activation(out=P[:, :].rearrange("p (b f) -> p b f", b=2)[:, 0:n, 0:nc_],
                                                          in_=src[:, :].rearrange("p (b f) -> p b f", b=2)[:, 0:n, 0:nc_], func=AF.Exp, scale=scale),
                            reads=bs, writes=[P])
                if post_exp is not None:
                    post_exp(P, T, it, c0, c1)
                for bi, sub in enumerate(it):
                    sl, kt = sub[0], sub[1]
                    ob = bank[6 + sl] if o_single else bank[4 + 2 * (T % 2) + sl]
                    st = not started.get((sl, T), False)
                    started[(sl, T)] = True
                    is_last = last and all(s2[0] != sl for s2 in it[bi + 1:])
                    vs = vslot(sl)
                    lo = slots[sl]["lo"]
                    vcols = slice(64, 192) if lo else slice(0, 128)
                    fw.emit("pe", lambda e, ob=ob, kt=kt, vs=vs, vcols=vcols, bi=bi, st=st, is_last=is_last, c0=c0, c1=c1, nc_=nc_: e.matmul(
                        ob[:, c0:c1], lhsT=Vt[kt // 4][:, kt % 4, vs, vcols], rhs=P[:, 512 * bi:512 * bi + nc_], start=st, stop=is_last,
                        skip_group_check=(add_mm is not None)),
                        reads=[Vt[kt // 4], P], writes=[ob], inc=True)
                if last:
                    for sl in sorted(set(s2[0] for s2 in it)):
                        ob = bank[6 + sl] if o_single else bank[4 + 2 * (T % 2) + sl]
                        rec = rec_l[(T + sl) % 2]
                        lo = slots[sl]["lo"]
                        orow = slice(0, 64) if lo else slice(64, 128)
                        drow = slice(64, 128) if lo else slice(0, 64)
                        ac = addcols(sl)
                        if act_recip or (act_recip_last and T == 3):
                            if ac is None:
                                fw.emit("act", lambda e, ob=ob, drow=drow, rec=rec: e.activation(out=rec[drow, :], in_=ob[drow, :], func=AF.Ln), reads=[ob], writes=[rec])
                            else:
                                fw.emit("act", lambda e, ob=ob, drow=drow, rec=rec, ac=ac: e.activation(out=rec[drow, :], in_=ob[drow, :], func=AF.Ln, bias=ac[drow, :]), reads=[ob, sinkT], writes=[rec])
                            fw.emit("act", lambda e, drow=drow, rec=rec: e.activation(out=rec[drow, :], in_=rec[drow, :], func=AF.Exp, scale=-1.0), reads=[rec], writes=[rec])
                        elif ac is None:
                            fw.emit("dve", lambda e, ob=ob, drow=drow, rec=rec: e.reciprocal(out=rec[drow, :], in_=ob[drow, :]), reads=[ob], writes=[rec])
                        else:
                            fw.emit("dve", lambda e, ob=ob, drow=drow, ac=ac, rec=rec: e.tensor_scalar(out=rec[drow, :], in0=ob[drow, :], scalar1=ac[drow, :], scalar2=None, op0=ALU.add),
                                    reads=[ob, sinkT], writes=[rec])
                            fw.emit("dve", lambda e, drow=drow, rec=rec: e.reciprocal(out=rec[drow, :], in_=rec[drow, :]), reads=[rec], writes=[rec])
                        fw.emit("dve", lambda e, ob=ob, orow=orow, drow=drow, T=T, rec=rec: e.tensor_tensor(out=OT[chunk][T][orow, :], in0=ob[orow, :], in1=rec[drow, :], op=ALU.mult),
                                reads=[ob, rec], writes=[OT[chunk][T]])

            n = len(work)
            for i in range(n + lookahead):
                if i < n:
                    emit_qk(work[i], i)
                if i - lookahead >= 0:
                    emit_rest(work[i - lookahead], i - lookahead)

        hsrc = lambda kc, cs: hT_ap[:, kc, cs]
        hsrc_t = lambda kc, t: hT[kc][t]
        init_v_ones()

        if even:
            W = w_in_ab[j].rearrange("(kc p) m -> p kc m", p=128)
            def loads_even(ui):
                mixer, u = divmod(ui, 4)
                g = u // 2
                qbase = 0 if mixer == 0 else 768
                kbase = 512 if mixer == 0 else 1280
                vbase = 640 if mixer == 0 else 1408
                wqt = wq[u % 2]
                load_w(wqt, wqt[:, :, :], W[:, :, qbase + 128 * u:qbase + 128 * (u + 1)], s_wq[u % 2])
                if u % 2 == 0:
                    load_w(wk, wk[:, :, 0:64], W[:, :, kbase + 64 * g:kbase + 64 * (g + 1)], s_wk)
                    load_w(wk, wk[:, :, 64:128], W[:, :, kbase + 64 * g:kbase + 64 * (g + 1)], s_wk)
                    load_w(wv, wv[:, :, 0:64], W[:, :, vbase + 64 * g:vbase + 64 * (g + 1)], s_wv)
                    load_w(wv, wv[:, :, 64:128], W[:, :, vbase + 64 * g:vbase + 64 * (g + 1)], s_wv)

            loads_even(0)
            for mixer in range(2):
                load_tables(mixer)
                for u in range(4):
                    g = u // 2
                    wqt = wq[u % 2]
                    newkv = (u % 2 == 0)
                    rows = slice(0, 128)
                    chains = []
                    for t in range(4):
                        if mixer == 0:
                            chains.append(mk_chain(wqt, 128, slice(0, 128), hsrc, hsrc_t, 8, t, rows, Qt, "rope", swap64(rows), (G_QA(j), G_QA(j) + 1), True))
                            if newkv:
                                chains.append(mk_chain(wk, 128, slice(0, 128), hsrc, hsrc_t, 8, t, rows, Kt, "rope", swap64(rows), (G_QA(j) + 2, G_QA(j) + 3), True))
                        else:
                            chains.append(mk_chain(wqt, 128, slice(0, 128), hsrc, hsrc_t, 8, t, rows, Qt, "rope", swap64(rows)))
                            if newkv:
                                chains.append(mk_chain(wk, 128, slice(0, 128), hsrc, hsrc_t, 8, t, rows, Kt, "rope", swap64(rows)))
                    run_chains(chains)
                    if newkv:
                        v_proj(wv, 128, hsrc, hsrc_t, 8, 1)
                    if 4 * mixer + u + 1 < 8:
                        loads_even(4 * mixer + u + 1)
                    slots = [dict(q=(Qt, slice(0, 64)), k=(Kt, slice(0, 64)), lo=True),
                             dict(q=(Qt, slice(64, 128)), k=(Kt, slice(64, 128)), lo=False)]
                    if mixer == 0:
                        items = lambda T: [[(0, kt, None), (1, kt, None)] for kt in range(16)]
                        attention(slots, items, 8.0, u, lambda sl: None, lambda sl: 0)
                    else:
                        def items(T):
                            out_ = []
                            for kt in range(max(0, 4 * T - 1), min(16, 4 * T + 5)):
                                rel = kt - 4 * T
                                cr = (max(0, 128 * rel - 128), min(512, 128 * rel + 256))
                                out_.append([(0, kt, rel + 1, cr), (1, kt, rel + 1, cr)])
                            return out_

                        def addm(T, sub, c0, c1):
                            return (bmask[:, sub[2], c0:c1], [bmask])

                        sc0 = 8 * j
                        attention(slots, items, 0.125, 4 + u, lambda sl, u=u: sinkT[:, sc0 + 2 * u + sl:sc0 + 2 * u + sl + 1], lambda sl: 0, act_recip=False, add_mm=addm, act_recip_last=True)
            Wout = w_out_ab[j]
        else:
            W = w_in_cd[j].rearrange("(kc p) m -> p kc m", p=128)
            Wuq = w_uq[j].rearrange("(kc p) m -> p kc m", p=128)
            Wukv = w_ukv[j].rearrange("(kc p) m -> p kc m", p=128)
            load_tables(2)
            latc = [t1, t2]
            wl0, wl1 = wq[0], wq[1]
            load_w(wl0, wl0[:, :, :], W[:, :, 0:128], s_wq[0])
            load_w(wl1, wl1[:, :, :], W[:, :, 128:256], s_wq[1])
            for t in range(4):
                for c, wl in enumerate((wl0, wl1)):
                    b = brr.get()
                    proj_bank(b, 128, wl, slice(0, 128), hsrc, hsrc_t, 8, t)
                    fw.emit("dve", lambda e, b=b, c=c: e.tensor_copy(out=latc[c][:, :], in_=b[:, :]), reads=[b], writes=[latc[c]])
                bq = brr.get()
                for c in range(2):
                    fw.emit("pool", lambda e, c=c: e.tensor_tensor(out=sq[c][:, :], in0=latc[c][:, :], in1=latc[c][:, :], op=ALU.mult), reads=[latc[c]], writes=[sq[c]])
                    fw.emit("pe", lambda e, c=c, bq=bq: e.matmul(bq[:, :], lhsT=ones_all, rhs=sq[c][:, :], start=(c == 0), stop=(c == 1)), reads=[sq[c], permT], writes=[bq])
                rsqrt_ps(bq, bq[:, :], rs_t, rs_t[:, :], 1.0 / 256, 0, slice(0, 128))
                for c in range(2):
                    fw.emit("dve", lambda e, c=c, t=t: e.scalar_tensor_tensor(out=cqn[c][t][:, :], in0=latc[c][:, :], scalar=gcol(G_CQ(j) + c), in1=rs_t[:, :], op0=ALU.mult, op1=ALU.mult),
                            reads=[latc[c], rs_t, gcolsT], writes=[cqn[c][t]])
            load_w(wk, wk[:, :, :], W[:, :, 256:384], s_wk)
            for t in range(4):
                b = brr.get()
                proj_bank(b, 128, wk, slice(0, 128), hsrc, hsrc_t, 8, t)
                fw.emit("dve", lambda e, b=b: e.tensor_copy(out=t1[:, :], in_=b[:, :]), reads=[b], writes=[t1])
                bq = brr.get()
                fw.emit("pool", lambda e: e.tensor_tensor(out=sq[0][:, :], in0=t1[:, :], in1=t1[:, :], op=ALU.mult), reads=[t1], writes=[sq[0]])
                fw.emit("pe", lambda e, bq=bq: e.matmul(bq[:, :], lhsT=ones_all, rhs=sq[0][:, :], start=True, stop=True), reads=[sq[0], permT], writes=[bq])
                rsqrt_ps(bq, bq[:, :], rs_t, rs_t[:, :], 1.0 / 128, 0, slice(0, 128))
                fw.emit("dve", lambda e, t=t: e.scalar_tensor_tensor(out=ckvn[t][:, :], in0=t1[:, :], scalar=gcol(G_CQ(j) + 2), in1=rs_t[:, :], op0=ALU.mult, op1=ALU.mult),
                        reads=[t1, rs_t, gcolsT], writes=[ckvn[t]])
            load_w(wv, wv[:, :, :], W[:, :, 320:448], s_wv)
            r96 = slice(64, 96)
            run_chains([mk_chain(wv, 128, slice(0, 128), hsrc, hsrc_t, 8, t, slice(0, 128), krT, "rope", swapC(slice(0, 128))) for t in range(4)])

            cq_src = lambda kc, cs: cqn_ap[:, kc, cs]
            cq_src_t = lambda kc, t: cqn[kc][t]
            ckv_src = lambda kc, cs: ckvn_ap[:, cs]
            ckv_src_t = lambda kc, t: ckvn[t]
            r0_96 = slice(0, 96)
            def loads_c(h):
                wqt = wq[h % 2]
                load_w(wqt, wqt[:, 0:2, 0:96], Wuq[:, :, 96 * h:96 * (h + 1)], s_wq[h % 2])
                load_w(wk, wk[:, 0:1, 0:64], Wukv[:, :, 128 * h:128 * h + 64], s_wk)
                load_w(wv, wv[:, 0:1, 0:64], Wukv[:, :, 128 * h + 64:128 * h + 128], s_wv)

            loads_c(0)
            for h in range(8):
                wqt = wq[h % 2]
                chains = []
                for t in range(4):
                    chains.append(mk_chain(wqt, 96, slice(0, 96), cq_src, cq_src_t, 2, t, r0_96, Qt, "rope", swapC(r0_96)))
                    chains.append(mk_chain(wk, 64, slice(0, 64), ckv_src, ckv_src_t, 1, t, slice(0, 64), Kt, "plain",
                                           after=lambda t=t: fw.emit("act", lambda e, t=t: e.activation(out=Kt[t][r96, :], in_=krT[t][r96, :], func=AF.Copy), reads=[krT[t]], writes=[Kt[t]])))
                run_chains(chains)
                v_proj(wv, 64, ckv_src, ckv_src_t, 1, 1)
                if h + 1 < 8:
                    loads_c(h + 1)
                lo = (h % 2 == 0)
                slots = [dict(q=(Qt, r0_96), k=(Kt, r0_96), lo=lo)]
                items = lambda T: [[(0, 2 * i, None), (0, 2 * i + 1, None)] for i in range(8)]
                attention(slots, items, 96.0 ** -0.5, h // 2, lambda sl: None, lambda sl: 0)

            fw.barrier()
            identT = Tile(sq[0].ap[:, 0:128], "ident")
            fw.dma("pool", identT[:, :], ident_d, s_misc6, writes=[identT])
            def loads_d(u):
                wqt = wq[u % 2]
                load_w(wqt, wqt[:, :, :], W[:, :, 416 + 128 * u:416 + 128 * (u + 1)], s_wq[u % 2])
                load_w(wk, wk[:, :, :], W[:, :, 928 + 128 * u:928 + 128 * (u + 1)], s_wk)
                load_w(wv, wv[:, :, :], W[:, :, 1440 + 128 * u:1440 + 128 * (u + 1)], s_wv)

            loads_d(0)
            for u in range(4):
                wqt = wq[u % 2]
                stg = [t1_l[0], t2_l[0]]
                stv = [x[:, :].bitcast(BF16) for x in stg]
                fw.dma("pool", stv[0], dneg_d[0], s_misc4, writes=[stg[0]])
                fw.dma("pool", stv[1], dneg_d[1], s_misc5, writes=[stg[1]])
                for hh in range(2):
                    fw.dma("pool", dEi[:, hh, :], dbias_d[j, 2 * u + hh], s_misc, writes=[dEi])
                for hh in range(2):
                    fw.emit("dve", lambda e, hh=hh: e.scalar_tensor_tensor(out=dEf[:, hh, :], in0=dEi[:, hh, :], scalar=8.0, in1=stv[0], op0=ALU.mult, op1=ALU.add),
                            reads=[dEi, stg[0]], writes=[dEf])
                for hh in range(2):
                    fw.emit("dve", lambda e, hh=hh: e.scalar_tensor_tensor(out=dEi[:, hh, :], in0=dEi[:, hh, :], scalar=8.0, in1=stv[1], op0=ALU.mult, op1=ALU.add),
                            reads=[dEi, stg[1]], writes=[dEi])
                rows = slice(0, 128)
                chains = []
                for t in range(4):
                    chains.append(mk_chain(wqt, 128, slice(0, 128), hsrc, hsrc_t, 8, t, rows, Qt, "plain"))
                    chains.append(mk_chain(wk, 128, slice(0, 128), hsrc, hsrc_t, 8, t, rows, Kt, "plain"))
                run_chains(chains)
                v_proj(wv, 128, hsrc, hsrc_t, 8, 2)
                if u + 1 < 4:
                    loads_d(u + 1)
                slots = [dict(q=(Qt, slice(0, 64)), k=(Kt, slice(0, 64)), lo=True),
                         dict(q=(Qt, slice(64, 128)), k=(Kt, slice(64, 128)), lo=False)]

                def items(T):
                    lo_r = min(max(8 * T - 4, 0), 24)
                    hi_r = min(max(8 * T + 7 - 4, 0), 24) + 7
                    out_ = []
                    for kt in range(lo_r // 2, hi_r // 2 + 1):
                        vb = [bq for bq in range(8) if d_row_valid(8 * T + bq, 2 * kt) or d_row_valid(8 * T + bq, 2 * kt + 1)]
                        cr = (64 * min(vb), 64 * (max(vb) + 1))
                        out_.append([(0, kt, None, cr), (1, kt, None, cr)])
                    return out_

                def addm(T, sub, c0, c1):
                    sl, kt = sub[0], sub[1]
                    tab = dEi if T in (1, 2) else dEf
                    jj0 = 7 - 2 * kt + 8 * T + c0 // 64
                    assert 0 <= jj0 and jj0 + (c1 - c0) // 64 <= 16, (T, kt, jj0)
                    return (tab[:, sl, 64 * jj0:64 * jj0 + (c1 - c0)], [tab])

                def post(P, T, it, c0, c1):
                    if T not in (0, 3):
                        return
                    blo = c0 // 64
                    nb = (c1 - c0) // 64
                    for bi, sub in enumerate(it):
                        kt = sub[1]
                        base = 512 * bi
                        for a_ in range(2):
                            kr_ = 2 * kt + a_
                            pr = slice(64 * a_, 64 * (a_ + 1))
                            valid = [d_row_valid(8 * T + bq, kr_) for bq in range(blo, blo + nb)]
                            bq = 0
                            while bq < nb:
                                e0 = bq
                                while bq < nb and valid[bq] == valid[e0]:
                                    bq += 1
                                if not valid[e0]:
                                    cs = slice(base + 64 * e0, base + 64 * bq)
                                    fw.emit("pool", lambda e, cs=cs, pr=pr: e.memset(P[pr, cs], 0.0), writes=[P])

                attention(slots, items, 0.125, 4 + u, lambda sl: None, lambda sl: sl, post_exp=post, act_recip=False, add_mm=addm, act_recip_last=True)
            Wout = w_out_cd[j]

        def out_proj(Wsrc, KC, src_ap, src_tiles, wbufs, wsems):
            Wr = Wsrc.rearrange("(kc p) m -> p kc m", p=128)
            for fo in range(8):
                wt = wbufs[fo % len(wbufs)]
                load_w(wt, wt[:, 0:KC, :], Wr[:, :, 128 * fo:128 * (fo + 1)], wsems[fo % len(wbufs)])
                for t in range(4):
                    b = brr.get()
                    proj_bank(b, 128, wt, slice(0, 128), src_ap, src_tiles, KC, t)
                    fw.emit("dve", lambda e, b=b, fo=fo, t=t: e.tensor_tensor(out=xT[fo][t][:, :], in0=xT[fo][t][:, :], in1=b[:, :], op=ALU.add),
                            reads=[b, xT[fo][t]], writes=[xT[fo][t]])

        out_proj(Wout, 8, lambda kc, cs: OT_ap[:, kc, cs], lambda kc, t: OT[kc][t],
                 [wq[0], wq[1], wk, wv], [s_wq[0], s_wq[1], s_wk, s_wv])
        if dbg_stop == (L, "mix"):
            s_dbg = fw.dmasem("dbg")
            for c in range(8):
                for t in range(4):
                    srcT = hT if os.environ.get("DBG_DUMP", "OT") == "hT" else OT
                    tk = fw.dma("pool", dbg_d[c, :, 512 * t:512 * (t + 1)], srcT[c][t][:, :], s_dbg, reads=[srcT[c][t]])
            fw.wait_tokens("pool", [tk])
            break

        brr = BankRR([0, 1, 2, 3, 4, 5])
        fw.barrier()
        rmsnorm_T(lambda c, cs: xT_h[:, c, cs], S, gb + G_XQ, lambda c, cs: hT_ap[:, c, cs], lambda c, t: hT[c][t],
                  8, brr, ntmp, 1.0 / D, ones_all, src_tiles=lambda c, t: xT[c][t])
        memf_ap = None
        memf = Tile(Vap.rearrange("p t s d -> p (t s d)")[:, 0:4096].bitcast(F32).rearrange("p (c s) -> p c s", s=256), "memf")
        memn = Tile(Vap.rearrange("p t s d -> p (t s d)")[:, 4096:6144].rearrange("p (c s) -> p c s", s=256), "memn")
        fw.barrier()
        fw.dma("sp", memf[:, :, :], memT_d.rearrange("c p s -> p c s"), s_mem, writes=[memf])
        rmsnorm_T(lambda c, cs: memf[:, c, cs], 256, gb + G_MEM, lambda c, cs: memn[:, c, cs], lambda c, t: memn,
                  8, brr, (sq, rs_t), 1.0 / D, ones_all, src_tiles=lambda c, t: memf)
        kx = Tile(Kap[:, 0:1024].rearrange("p (h s) -> p h s", s=256), "kx")
        vx = Tile(Kap[:, 1024:2048].rearrange("p (t f) -> p t f", f=512), "vx")
        Wkv = w_xkv[L].rearrange("(kc p) m -> p kc m", p=128)
        Wq = w_xq[L].rearrange("(kc p) m -> p kc m", p=128)
        fw.barrier()
        xw = [wq[0], wq[1], wk, wv]
        xs = [s_wq[0], s_wq[1], s_wk, s_wv]
        for h in range(4):
            wt = xw[h]
            load_w(wt, wt[:, :, :], Wkv[:, :, 128 * h:128 * (h + 1)], xs[h])
        for h in range(4):
            wt = xw[h]
            b = brr.get()
            pairs = [(wt[:, kc, :], memn[:, kc, :]) for kc in range(8)]
            mm_group(b[:, 0:256], b, pairs, [wt, memn])
            fw.emit("dve", lambda e, b=b, h=h: e.tensor_copy(out=kx[:, h, :], in_=b[:, 0:256]), reads=[b], writes=[kx])
        for hp in range(4):
            wt = xw[hp]
            load_w(wt, wt[:, :, :], Wkv[:, :, 512 + 128 * hp:512 + 128 * (hp + 1)], xs[hp])
        for hp in range(4):
            wt = xw[hp]
            for tt in range(2):
                b = brr.get()
                pairs = [(memn[:, kc, 128 * tt:128 * (tt + 1)], wt[:, kc, :]) for kc in range(8)]
                mm_group(b[:, 0:128], b, pairs, [wt, memn])
                fw.emit("dve", lambda e, b=b, hp=hp, tt=tt: e.tensor_copy(out=vx[:, tt, 128 * hp:128 * (hp + 1)], in_=b[:, 0:128]), reads=[b], writes=[vx])
        xscale = 128.0 ** -0.5
        for h in range(4):
            wt = xw[h]
            load_w(wt, wt[:, :, :], Wq[:, :, 128 * h:128 * (h + 1)], xs[h])
        for h in range(4):
            wt = xw[h]
            for t in range(4):
                b = brr.get()
                proj_bank(b, 128, wt, slice(0, 128), hsrc, hsrc_t, 8, t)
                plain_evac(b, OT[4 + h], t, slice(0, 128))
        xitems = [(h, t) for h in range(4) for t in range(4)]

        def x_qk(i):
            h, t = xitems[i]
            for kt in range(2):
                bb = bank[2 * (i % 2) + kt]
                fw.emit("pe", lambda e, bb=bb, kt=kt, t=t, h=h: e.matmul(bb[:, :], lhsT=kx[:, h, 128 * kt:128 * (kt + 1)], rhs=OT[4 + h][t][:, :], start=True, stop=True),
                        reads=[kx, OT[4 + h][t]], writes=[bb])

        def x_rest(i):
            h, t = xitems[i]
            p = i % 2
            P = Pt[p]
            fw.emit("act", lambda e, p=p, P=P: e.activation(out=P[:, :], in_=psb[p][:, :], func=AF.Exp, scale=xscale),
                    reads=[bank[2 * p], bank[2 * p + 1]], writes=[P])
            ob = bank[4 + 2 * p]; db = bank[5 + 2 * p]; rc = rec_l[p]
            for kt in range(2):
                fw.emit("pe", lambda e, kt=kt, P=P, h=h, ob=ob: e.matmul(ob[:, :], lhsT=vx[:, kt, 128 * h:128 * (h + 1)], rhs=P[:, 512 * kt:512 * (kt + 1)], start=(kt == 0), stop=(kt == 1)),
                        reads=[vx, P], writes=[ob])
            for kt in range(2):
                fw.emit("pe", lambda e, kt=kt, P=P, db=db: e.matmul(db[:, :], lhsT=ones_all, rhs=P[:, 512 * kt:512 * (kt + 1)], start=(kt == 0), stop=(kt == 1)),
                        reads=[permT, P], writes=[db])

        def x_fin(i):
            h, t = xitems[i]
            p = i % 2
            ob = bank[4 + 2 * p]; db = bank[5 + 2 * p]; rc = rec_l[p]
            fw.emit("act", lambda e, db=db, rc=rc: e.activation(out=rc[:, :], in_=db[:, :], func=AF.Ln), reads=[db], writes=[rc])
            fw.emit("act", lambda e, rc=rc: e.activation(out=rc[:, :], in_=rc[:, :], func=AF.Exp, scale=-1.0), reads=[rc], writes=[rc])
            fw.emit("dve", lambda e, h=h, t=t, ob=ob, rc=rc: e.tensor_tensor(out=OT[h][t][:, :], in0=ob[:, :], in1=rc[:, :], op=ALU.mult),
                    reads=[ob, rc], writes=[OT[h][t]])

        nx = len(xitems)
        for i in range(nx + 2):
            if i < nx:
                x_qk(i)
            if 1 <= i <= nx:
                x_rest(i - 1)
            if i >= 2:
                x_fin(i - 2)
        out_proj(w_xo[L], 4, lambda kc, cs: OT_ap[:, kc, cs], lambda kc, t: OT[kc][t],
                 [wq[0], wq[1], wk, wv], [s_wq[0], s_wq[1], s_wk, s_wv])
        if dbg_stop == (L, "xattn"):
            break

        fw.barrier()
        cv = Carver()
        hT_ap = cv.bf(8 * S).rearrange("p (c s) -> p c s", s=S)
        hT = [[Tile(hT_ap[:, c, 512 * t:512 * (t + 1)], f"fh{c}_{t}") for t in range(4)] for c in range(8)]
        act_ap = cv.bf(22 * 1024).rearrange("p (j s) -> p j s", s=1024)
        actT = [[Tile(act_ap[:, jj, 512 * t:512 * (t + 1)], f"a{jj}_{t}") for t in range(2)] for jj in range(22)]
        sq_f = [Tile(cv.bf(512), f"fsq{i}") for i in range(4)]
        rs_f = [Tile(cv.f32(512), "frs"), Tile(cv.f32(512), "frs1")]
        xr_f = [Tile(cv.f32(512), "fxr0"), Tile(cv.f32(512), "fxr1")]
        sg = [Tile(cv.f32(512), f"sg{i}") for i in range(2)]
        NWG, NWD = 4, 3
        wg = [Tile(cv.bf(8 * 256).rearrange("p (k m) -> p k m", m=256), f"wg{i}") for i in range(NWG)]
        wd = [Tile(cv.bf(22 * 128).rearrange("p (k m) -> p k m", m=128), f"wd{i}") for i in range(NWD)]
        brr = BankRR([0, 1, 2, 3, 4, 5, 6, 7])
        rmsnorm_T(lambda c, cs: xT_h[:, c, cs], S, gb + G_FFN, lambda c, cs: hT_ap[:, c, cs], lambda c, t: hT[c][t],
                  8, brr, (sq_f, rs_f, xr_f), 1.0 / D, ones_all, src_tiles=lambda c, t: xT[c][t])
        Wg = w_gu[L].rearrange("(kc p) m -> p kc m", p=128)
        Wd = w_dn[L].rearrange("(kc p) m -> p kc m", p=128)
        for half in range(2):
            for jj in range(22):
                wt = wg[jj % NWG]
                load_w(wt, wt[:, :, 0:128], Wg[:, :, 128 * jj:128 * (jj + 1)], s_wg[jj % NWG])
                load_w(wt, wt[:, :, 128:256], Wg[:, :, DFF + 128 * jj:DFF + 128 * (jj + 1)], s_wg[jj % NWG])
                for t2i in range(2):
                    t = 2 * half + t2i
                    bg = brr.get()
                    proj_bank(bg, 128, wt, slice(0, 128), lambda kc, cs: hT_ap[:, kc, cs], lambda kc, t: hT[kc][t], 8, t)
                    bu = brr.get()
                    proj_bank(bu, 128, wt, slice(128, 256), lambda kc, cs: hT_ap[:, kc, cs], lambda kc, t: hT[kc][t], 8, t)
                    sgt = sg[t2i]
                    fw.emit("act", lambda e, bg=bg, sgt=sgt: e.activation(out=sgt[:, :], in_=bg[:, :], func=AF.Silu), reads=[bg], writes=[sgt])
                    fw.emit("dve", lambda e, bu=bu, sgt=sgt, jj=jj, t2i=t2i: e.tensor_tensor(out=actT[jj][t2i][:, :], in0=bu[:, :], in1=sgt[:, :], op=ALU.mult),
                            reads=[bu, sgt], writes=[actT[jj][t2i]])
            for fo in range(8):
                wt = wd[fo % NWD]
                load_w(wt, wt[:, :, :], Wd[:, :, 128 * fo:128 * (fo + 1)], s_wd[fo % NWD])
                for t2i in range(2):
                    t = 2 * half + t2i
                    b = brr.get()
                    pairs = [(wt[:, jj, :], act_ap[:, jj, 512 * t2i:512 * (t2i + 1)]) for jj in range(22)]
                    mm_group(b[:, :], b, pairs, [wt] + [actT[jj][t2i] for jj in range(22)])
                    fw.emit("dve", lambda e, b=b, fo=fo, t=t: e.tensor_tensor(out=xT[fo][t][:, :], in0=xT[fo][t][:, :], in1=b[:, :], op=ALU.add),
                            reads=[b, xT[fo][t]], writes=[xT[fo][t]])
        if dbg_stop == (L, "ffn"):
            break

    fw.barrier()
    cv = Carver()
    sq_z = [Tile(cv.bf(512), "zsq0"), Tile(cv.bf(512), "zsq1")]
    rs_z = Tile(cv.f32(512), "zrs")
    ob_ap = cv.f32(8 * S).rearrange("p (c s) -> p c s", s=S)
    obT = [[Tile(ob_ap[:, c, 512 * t:512 * (t + 1)], f"ob{c}_{t}") for t in range(4)] for c in range(8)]
    brr = BankRR([0, 1, 2, 3, 4, 5, 6, 7])
    if dbg_stop is None:
        rmsnorm_T(lambda c, cs: xT_h[:, c, cs], S, G_FINAL, lambda c, cs: ob_ap[:, c, cs], lambda c, t: obT[c][t],
                  8, brr, (sq_z, rs_z), 1.0 / D, permT[:, 3, :], src_tiles=lambda c, t: xT[c][t])
        src_t = obT
    else:
        src_t = xT
    last = []
    for t in range(4):
        for c in range(8):
            last.append(fw.dma("sp", out_d[c, :, 512 * t:512 * (t + 1)], src_t[c][t][:, :], s_out, reads=[src_t[c][t]]))
    fw.wait_tokens("sp", [last[-1]])
    fw.build()
    return nc


def make_in_maps(inputs):
    f = lambda a: np.ascontiguousarray(np.asarray(a, dtype=np.float32))
    x = f(inputs["x"]); mem = f(inputs["mem"])
    tables, perm, bm, dcol = host_consts()
    NG = 4 * 32 + 8 + 2 * 4 + 2 * 3
    gcols = np.zeros((128, NG), np.float32)
    colz = lambda g: np.asarray(g, np.float32).reshape(-1, 128).T
    for L in range(4):
        gcols[:, 32 * L + 0:32 * L + 8] = colz(inputs["g_mix"][L])
        gcols[:, 32 * L + 8:32 * L + 16] = colz(inputs["g_xq"][L])
        gcols[:, 32 * L + 16:32 * L + 24] = colz(inputs["g_mem"][L])
        gcols[:, 32 * L + 24:32 * L + 32] = colz(inputs["g_ffn"][L])
    gcols[:, 128:136] = colz(inputs["g_final"])
    p = np.arange(128)
    for j in range(2):
        gq = np.asarray(inputs["g_qa"][j], np.float32); gk = np.asarray(inputs["g_ka"][j], np.float32)
        gcols[:, 136 + 4 * j + 0] = gq[p % 64]
        gcols[:, 136 + 4 * j + 1] = gq[(p % 64 + 32) % 64]
        gcols[:, 136 + 4 * j + 2] = gk[p % 64]
        gcols[:, 136 + 4 * j + 3] = gk[(p % 64 + 32) % 64]
        gcols[:, 144 + 3 * j:144 + 3 * j + 2] = colz(inputs["g_cq"][j])
        gcols[:, 144 + 3 * j + 2] = np.asarray(inputs["g_ckv"][j], np.float32)
    sink = np.zeros((128, 16), np.float32)
    for j in range(2):
        sink[:, 8 * j:8 * j + 8] = np.asarray(inputs["sink_b"][j], np.float32)[None, :]
    rpb = np.asarray(inputs["rpb_d"], np.float32)
    kc = np.arange(64)[:, None]; c = np.arange(64)[None, :]
    cidx = np.clip(kc - c + 15, 0, 30)
    g = rpb[:, :, ::-1, :][:, :, :, cidx]
    g = np.transpose(g, (0, 1, 3, 2, 4))
    dbias = np.zeros((2, 8, 2, 64, 16, 64), np.float32)
    dbias[:, :, 0, :, 0:15, :] = g
    dbias[:, :, 1, :, 1:16, :] = g
    dbias = np.ascontiguousarray(dbias.reshape(2, 8, 128, 1024))
    shared = dict(
        w_in_ab=f(inputs["w_in_ab"]), w_out_ab=f(inputs["w_out_ab"]), w_in_cd=f(inputs["w_in_cd"]),
        w_out_cd=f(inputs["w_out_cd"]), w_uq=f(inputs["w_uq"]), w_ukv=f(inputs["w_ukv"]),
        w_xq=f(inputs["w_xq"]), w_xkv=f(inputs["w_xkv"]), w_xo=f(inputs["w_xo"]),
        w_gate_up=f(inputs["w_gate_up"]), w_down=f(inputs["w_down"]),
        gcols=gcols, sinkb=sink, tables=tables, perm=perm, ident=np.eye(128, dtype=np.float32), bmask=bm, dmask=dcol, dneg=((dcol - 1.0) * 30000.0).astype(np.float32), dbias=dbias)
    maps = []
    for b in range(NCORES):
        m = dict(shared)
        m["xT"] = np.ascontiguousarray(x[b].T.reshape(8, 128, S))
        m["memT"] = np.ascontiguousarray(mem[b].T.reshape(8, 128, 256))
        maps.append(m)
    return maps


def kernel(**inputs):
    nc = build_program()
    maps = make_in_maps(inputs)
    res = run_bass_kernel_spmd(nc, maps, core_ids=list(range(NCORES)))
    out = np.stack([np.asarray(r["outT"], np.float32).reshape(D, S).T for r in res.results], 0)
    return np.ascontiguousarray(out.astype(np.float32))
```

```python
import os
import numpy as np
import concourse.bass as bass
import concourse.mybir as mybir
from concourse.bass_utils import run_bass_kernel_spmd

F32 = mybir.dt.float32
BF16 = mybir.dt.bfloat16
ALU = mybir.AluOpType
AF = mybir.ActivationFunctionType

SEM_ROTATE = 30000
NCORES = 8
S = 2048
D = 1024
DFF = 2816
EPS = 1e-6


class Tile:
    __slots__ = ("ap", "name", "last_w", "readers", "excl", "lw_read")

    def __init__(self, ap, name="", excl=False):
        self.ap = ap
        self.name = name
        self.last_w = None
        self.readers = {}
        self.excl = excl
        self.lw_read = False

    def __getitem__(self, idx):
        return self.ap[idx]


class SemCounter:
    def __init__(self, fw, name, step=1):
        self.fw = fw
        self.name = name
        self.step = step
        self.gen = 0
        self.sem = fw.nc.alloc_semaphore(f"{name}_{self.gen}")
        self.count = 0
        fw.all_ctrs.append(self)

    def next_token(self):
        if (self.count + 1) * self.step > SEM_ROTATE:
            self.gen += 1
            self.sem = self.fw.nc.alloc_semaphore(f"{self.name}_{self.gen}")
            self.count = 0
        return (self.sem, (self.count + 1) * self.step)

    def commit(self):
        self.count += 1

    def cur_token(self):
        if self.count == 0:
            return None
        return (self.sem, self.count * self.step)


class Engine:
    def __init__(self, fw, key):
        self.key = key
        self.ops = []
        self.ctr = SemCounter(fw, f"s_{key}")
        self.seen = {}


class FW:
    def __init__(self, nc):
        self.nc = nc
        self.all_ctrs = []
        self.eng = {k: Engine(self, k) for k in ("pe", "act", "dve", "pool", "sp")}

    def dmasem(self, name):
        return SemCounter(self, name, step=16)

    def _collect(self, e, reads, writes, skip_self=False):
        waits = {}

        def add(tok):
            if tok is None:
                return
            s, v = tok
            if skip_self and s is e.ctr.sem:
                return
            if e.seen.get(s, 0) >= v:
                return
            if waits.get(s, 0) < v:
                waits[s] = v

        for t in reads:
            if t.excl and t.lw_read and t.last_w is not None and t.last_w[0] is e.ctr.sem:
                continue
            add(t.last_w)
            if t.excl:
                for s, v in t.readers.items():
                    add((s, v))
        for t in writes:
            add(t.last_w)
            for s, v in t.readers.items():
                add((s, v))
        for s, v in waits.items():
            e.seen[s] = v
        return list(waits.items())

    def _update(self, tok, reads, writes):
        for t in reads:
            if t.excl:
                t.last_w = tok
                t.lw_read = True
            else:
                t.readers[tok[0]] = max(t.readers.get(tok[0], 0), tok[1])
        for t in writes:
            t.last_w = tok
            t.lw_read = False
            t.readers = {}

    def emit(self, ek, fn, reads=(), writes=(), inc=True):
        e = self.eng[ek]
        waits = self._collect(e, reads, writes, skip_self=(ek == "pe"))
        tok = e.ctr.next_token()
        if inc:
            e.ctr.commit()
        e.ops.append((waits, fn, tok if inc else None))
        self._update(tok, reads, writes)
        return tok

    def dma(self, qk, out, in_, sem, reads=(), writes=()):
        e = self.eng[qk]
        waits = self._collect(e, reads, writes)
        tok = sem.next_token()
        sem.commit()

        def fn(engine, out=out, in_=in_):
            return engine.dma_start(out=out, in_=in_)

        e.ops.append((waits, fn, tok + (True,)))
        self._update(tok, reads, writes)
        return tok

    def wait_tokens(self, ek, toks):
        e = self.eng[ek]
        waits = []
        for tok in toks:
            if tok is None:
                continue
            s, v = tok
            if e.seen.get(s, 0) < v:
                e.seen[s] = v
                waits.append((s, v))
        if waits:
            e.ops.append((waits, None, None))

    def barrier(self):
        toks = [c.cur_token() for c in self.all_ctrs]
        for ek in self.eng:
            self.wait_tokens(ek, toks)

    def build(self):
        nc = self.nc
        handles = {"pe": "tensor", "act": "scalar", "dve": "vector", "pool": "gpsimd", "sp": "sync"}
        with nc.Block() as block:
            for ek, attr in handles.items():
                e = self.eng[ek]
                if not e.ops:
                    continue

                def body(engine, e=e):
                    for waits, fn, tok in e.ops:
                        for s, v in waits:
                            engine.wait_ge(s, v)
                        if fn is None:
                            continue
                        ins = fn(engine)
                        if tok is not None:
                            ins.then_inc(tok[0], 16 if len(tok) == 3 else 1)

                getattr(block, attr)(body)


def _rope_inv(dim):
    return 10000.0 ** (-np.arange(0, dim, 2, dtype=np.float64) / dim)


def host_consts():
    pos = np.arange(S, dtype=np.float64)
    row = np.floor(pos / 64)
    col = pos % 64
    ang1d = pos[:, None] * _rope_inv(64)[None, :]
    ang2d = np.concatenate([row[:, None] * _rope_inv(32)[None, :],
                            col[:, None] * _rope_inv(32)[None, :]], -1)
    angc = pos[:, None] * _rope_inv(32)[None, :]

    def tab64(ang):
        cos = np.zeros((128, S)); sin = np.zeros((128, S))
        for p in range(128):
            d = p % 64
            cos[p] = np.cos(ang[:, d % 32])
            sin[p] = np.sin(ang[:, d % 32]) * (-1.0 if d < 32 else 1.0)
        return cos, sin

    c2, s2 = tab64(ang2d)
    c1, s1 = tab64(ang1d)
    cc = np.zeros((128, S)); sc = np.zeros((128, S))
    cc[0:64] = 1.0
    for p in range(64, 96):
        d = p - 64
        cc[p] = np.cos(angc[:, d % 16])
        sc[p] = np.sin(angc[:, d % 16]) * (-1.0 if d < 16 else 1.0)
    tables = np.stack([c2, s2, c1, s1, cc, sc]).astype(np.float32)

    perm = np.zeros((4, 128, 128), np.float32)
    for m in range(128):
        k = (m % 64 + 32) % 64 + 64 * (m // 64)
        perm[0, k, m] = 1.0
    for m in range(64, 96):
        k = 64 + ((m - 64 + 16) % 32)
        perm[1, k, m] = 1.0
    for k in range(128):
        for m in range(128):
            if k // 64 == m // 64:
                perm[2, k, m] = 1.0
    perm[3] = 1.0

    bm = np.zeros((6, 128, 512), np.float32)
    p = np.arange(128)[:, None]; f = np.arange(512)[None, :]
    for i, rel in enumerate(range(-1, 5)):
        bm[i] = np.where(np.abs(128 * rel + p - f) <= 128, 0.0, -30000.0).astype(np.float32)

    c = np.arange(64)
    c0 = np.clip(c - 8, 0, 48)
    colvalid = ((c[None, :] >= c0[:, None]) & (c[None, :] < c0[:, None] + 16))
    cv = colvalid.T.astype(np.float32)
    dmask = np.zeros((2, 2, 64, 16, 64), np.float32)
    for a_ in range(2):
        for jj in range(16):
            jr = jj - a_
            if 0 <= jr <= 14:
                dmask[0, a_, :, jj, :] = cv
            if 4 <= jr <= 11:
                dmask[1, a_, :, jj, :] = cv
    dcol = dmask.reshape(2, 128, 1024)
    return tables, perm, bm, dcol.astype(np.float32)


def d_row_valid(r, kr):
    r0 = min(max(r - 4, 0), 24)
    return r0 <= kr < r0 + 8


def build_program(depth=4, dbg_stop=None):
    nc = bass.Bass("TRN2", target_bir_lowering=False)
    fw = FW(nc)

    def din(name, shape):
        return nc.dram_tensor(name, list(shape), F32, kind="ExternalInput").ap()

    xT_d = din("xT", [8, 128, S])
    memT_d = din("memT", [8, 128, 256])
    w_in_ab = din("w_in_ab", [2, D, 1536]); w_out_ab = din("w_out_ab", [2, D, D])
    w_in_cd = din("w_in_cd", [2, D, 1952]); w_out_cd = din("w_out_cd", [2, D, D])
    w_uq = din("w_uq", [2, 256, 768]); w_ukv = din("w_ukv", [2, 128, 1024])
    w_xq = din("w_xq", [4, D, 512]); w_xkv = din("w_xkv", [4, D, D]); w_xo = din("w_xo", [4, 512, D])
    w_gu = din("w_gate_up", [4, D, 2 * DFF]); w_dn = din("w_down", [4, DFF, D])
    NG = 4 * 32 + 8 + 2 * 4 + 2 * 3
    gcols_d = din("gcols", [128, NG])
    sink_d = din("sinkb", [128, 16])
    tables_d = din("tables", [6, 128, S])
    perm_d = din("perm", [4, 128, 128])
    ident_d = din("ident", [128, 128])
    bmask_d = din("bmask", [6, 128, 512])
    dmask_d = din("dmask", [2, 128, 1024])
    dneg_d = din("dneg", [2, 128, 1024])
    dbias_d = din("dbias", [2, 8, 128, 1024])
    out_d = nc.dram_tensor("outT", [8, 128, S], F32, kind="ExternalOutput").ap()
    dbg_d = nc.dram_tensor("dbgT", [8, 128, S], F32, kind="ExternalOutput").ap() if dbg_stop is not None else None

    xT_h = nc.alloc_sbuf_tensor("xT_sb", [128, 8, S], F32)
    xT = [[Tile(xT_h[:, c, 512 * t:512 * (t + 1)], f"x{c}_{t}") for t in range(4)] for c in range(8)]
    perm_h = nc.alloc_sbuf_tensor("perm_sb", [128, 4, 128], BF16)
    permT = Tile(perm_h, "perm")
    gcols_h = nc.alloc_sbuf_tensor("gcols_sb", [128, NG], F32)
    gcolsT = Tile(gcols_h, "gcols")
    sink_h = nc.alloc_sbuf_tensor("sink_sb", [128, 16], F32)
    sinkT = Tile(sink_h, "sink")
    ARENA = nc.sbuf_bytes_remaining - 64
    ARENA -= ARENA % 64
    R = nc.alloc_sbuf_tensor("arena", [128, ARENA // 2], BF16)

    class Carver:
        def __init__(self):
            self.off = 0

        def bf(self, n):
            o = self.off
            self.off += (2 * n + 63) // 64 * 64
            assert self.off <= ARENA, f"arena overflow {self.off} > {ARENA}"
            return R[:, o // 2:o // 2 + n]

        def f32(self, n):
            return self.bf(2 * n).bitcast(F32)

    psb = [nc.alloc_psum_tensor(f"psb{i}", [128, 1024], F32) for i in range(4)]
    bank = []
    for i in range(4):
        bank.append(Tile(psb[i][:, 0:512], f"bank{2 * i}", excl=True))
        bank.append(Tile(psb[i][:, 512:1024], f"bank{2 * i + 1}", excl=True))

    s_x = [fw.dmasem(f"dx{t}") for t in range(4)]
    s_c = fw.dmasem("dconst")
    s_out = fw.dmasem("dout")

    for t in range(4):
        for c in range(8):
            tk = fw.dma("sp", xT[c][t][:, :], xT_d[c, :, 512 * t:512 * (t + 1)], s_x[t], writes=[xT[c][t]])
        for c in range(8):
            xT[c][t].last_w = tk
    fw.dma("pool", permT[:, :, :], perm_d.rearrange("a p m -> p a m"), s_c, writes=[permT])
    fw.dma("sp", gcolsT[:, :], gcols_d, fw.dmasem("dconst2"), writes=[gcolsT])
    fw.dma("sp", sinkT[:, :], sink_d, fw.dmasem("dconst3"), writes=[sinkT])
    s_wq = [fw.dmasem(f"wq_{i}") for i in range(2)]
    s_wk = fw.dmasem("wk"); s_wv = fw.dmasem("wv"); s_tab = [fw.dmasem("tab0"), fw.dmasem("tab1")]; s_misc = fw.dmasem("misc"); s_misc2 = fw.dmasem("misc2"); s_misc3 = fw.dmasem("misc3"); s_misc4 = fw.dmasem("misc4"); s_misc5 = fw.dmasem("misc5"); s_misc6 = fw.dmasem("misc6")
    s_mem = fw.dmasem("mem")
    s_wg = [fw.dmasem(f"wg_{i}") for i in range(4)]
    s_wd = [fw.dmasem(f"wd_{i}") for i in range(3)]
    eps_h = nc.alloc_sbuf_tensor("eps_sb", [128, 2], F32)
    epsT = Tile(eps_h, "eps")
    fw.emit("dve", lambda e: e.memset(epsT[:, 0:1], EPS), writes=[epsT])
    fw.emit("dve", lambda e: e.memset(epsT[:, 1:2], 64.0 * EPS), writes=[epsT])

    def rsqrt_ps(b, b_ap, o, o_ap, scale, which, rows):
        fw.emit("act", lambda e: e.activation(out=o_ap, in_=b_ap, func=AF.Ln, bias=epsT[rows, which:which + 1], scale=scale),
                reads=[b, epsT], writes=[o])
        fw.emit("act", lambda e: e.activation(out=o_ap, in_=o_ap, func=AF.Exp, scale=-0.5), reads=[o], writes=[o])

    fw.emit("act", lambda e: e.activation(out=sinkT[:, :], in_=sinkT[:, :], func=AF.Exp), reads=[sinkT], writes=[sinkT])

    swap64 = lambda r: permT[r, 0, r]
    swapC = lambda r: permT[r, 1, r]

    def gcol(i, rows=slice(0, 128)):
        return gcolsT[rows, i:i + 1]

    G_MIX, G_XQ, G_MEM, G_FFN = 0, 8, 16, 24
    G_FINAL = 128
    G_QA = lambda j: 136 + 4 * j
    G_CQ = lambda j: 144 + 3 * j

    class BankRR:
        def __init__(self, ids):
            self.ids = ids
            self.i = 0

        def get(self, exclude=()):
            while True:
                b = bank[self.ids[self.i % len(self.ids)]]
                self.i += 1
                if not any(b is x for x in exclude):
                    return b

    def mm_group(out_ap, out_tile, pairs, reads):
        n = len(pairs)
        for i, (l, r) in enumerate(pairs):
            fw.emit("pe", lambda e, l=l, r=r, i=i: e.matmul(out_ap, lhsT=l, rhs=r, start=(i == 0), stop=(i == n - 1)),
                    reads=reads, writes=[out_tile], inc=(i == n - 1))

    def load_w(dst_tile, dst_ap, src_ap, sem):
        return fw.dma("pool", dst_ap, src_ap, sem, writes=[dst_tile])

    def rmsnorm_T(src, ncols, gbase, dst, dst_tiles, nchunks, brr, tmp, inv_n, ones_ap, rows=slice(0, 128), src_tiles=None):
        fast = len(tmp) == 3
        sq_t = tmp[0]
        rs_l = tmp[1] if fast else [tmp[1]]
        xr_l = tmp[2] if fast else None
        ntt = ncols // 512 if ncols >= 512 else 1
        w = min(512, ncols)
        n_act_sq = 5 if fast else (nchunks + 1) // 2

        def stats(t):
            cs = slice(w * t, w * (t + 1))
            b = brr.get()
            rs_t = rs_l[t % len(rs_l)]
            for c in range(nchunks):
                st = src_tiles(c, t)
                on_act = (c % 2 == 0) if not fast else (c not in (2, 6))
                sqb = sq_t[c % len(sq_t)]
                if on_act:
                    fw.emit("act", lambda e, c=c, cs=cs, sqb=sqb: e.activation(out=sqb[rows, 0:w], in_=src(c, cs), func=AF.Square),
                            reads=[st], writes=[sqb])
                else:
                    fw.emit("pool", lambda e, c=c, cs=cs, sqb=sqb: e.tensor_tensor(out=sqb[rows, 0:w], in0=src(c, cs), in1=src(c, cs), op=ALU.mult),
                            reads=[st], writes=[sqb])
                fw.emit("pe", lambda e, c=c, b=b, sqb=sqb: e.matmul(b[rows, 0:w], lhsT=ones_ap, rhs=sqb[rows, 0:w], start=(c == 0), stop=(c == nchunks - 1)),
                        reads=[sqb, permT], writes=[b], inc=True)
            if not fast:
                rsqrt_ps(b, b[rows, 0:w], rs_t, rs_t[rows, 0:w], inv_n, 0, rows)
            return b

        def rsq(t, b):
            rs_t = rs_l[t % len(rs_l)]
            rsqrt_ps(b, b[rows, 0:w], rs_t, rs_t[rows, 0:w], inv_n, 0, rows)

        def apply(t):
            cs = slice(w * t, w * (t + 1))
            rs_t = rs_l[t % len(rs_l)]
            for c in range(nchunks):
                st = src_tiles(c, t)
                if fast and c in (3, 6):
                    xr = xr_l[0 if c == 3 else 1]
                    fw.emit("pool", lambda e, c=c, cs=cs, xr=xr, rs_t=rs_t: e.tensor_tensor(out=xr[rows, 0:w], in0=src(c, cs), in1=rs_t[rows, 0:w], op=ALU.mult),
                            reads=[st, rs_t], writes=[xr])
                    fw.emit("act", lambda e, c=c, cs=cs, xr=xr: e.activation(out=dst(c, cs), in_=xr[rows, 0:w], func=AF.Copy, scale=gcol(gbase + c, rows)),
                            reads=[xr, gcolsT], writes=[dst_tiles(c, t)])
                else:
                    fw.emit("dve", lambda e, c=c, cs=cs, rs_t=rs_t: e.scalar_tensor_tensor(out=dst(c, cs), in0=src(c, cs), scalar=gcol(gbase + c, rows), in1=rs_t[rows, 0:w], op0=ALU.mult, op1=ALU.mult),
                            reads=[st, rs_t, gcolsT], writes=[dst_tiles(c, t)])

        if fast:
            banks_ = {}
            for t in range(ntt + 2):
                if t < ntt:
                    banks_[t] = stats(t)
                if 1 <= t <= ntt:
                    rsq(t - 1, banks_[t - 1])
                if t >= 2:
                    apply(t - 2)
        else:
            for t in range(ntt):
                stats(t)
                apply(t)

    for L in range(depth):
        even = (L % 2 == 0)
        j = L // 2
        gb = 32 * L
        if L > 0:
            fw.barrier()
        cv = Carver()
        hT_ap = cv.bf(8 * S).rearrange("p (c s) -> p c s", s=S)
        hT = [[Tile(hT_ap[:, c, 512 * t:512 * (t + 1)], f"h{c}_{t}") for t in range(4)] for c in range(8)]
        OT_ap = cv.bf(8 * S).rearrange("p (c s) -> p c s", s=S)
        OT = [[Tile(OT_ap[:, c, 512 * t:512 * (t + 1)], f"o{c}_{t}") for t in range(4)] for c in range(8)]
        cosT = Tile(cv.bf(S), "cos"); sinT = Tile(cv.bf(S), "sin")
        Qap = cv.bf(S); Kap = cv.bf(S)
        Qt = [Tile(Qap[:, 512 * t:512 * (t + 1)], f"q{t}") for t in range(4)]
        Kt = [Tile(Kap[:, 512 * t:512 * (t + 1)], f"k{t}") for t in range(4)]
        Vap = cv.bf(16 * 2 * 192).rearrange("p (t s d) -> p t s d", s=2, d=192)
        Vt = [Tile(Vap[:, 4 * g:4 * (g + 1), :, :], f"v{g}") for g in range(4)]
        Pt = [Tile(cv.bf(1024), f"P{i}") for i in range(2)]
        NB_ROPE = 3 if even else 1
        SKEW = 2 if even else 1
        live_banks = []
        qsb_l = [Tile(cv.bf(512), f"qsb{i}") for i in range(3 if even else 2)]
        sq = [Tile(cv.bf(512), f"sq{i}") for i in range(3 if even else 2)]
        t1_l = [Tile(cv.f32(512), f"t1_{i}") for i in range(NB_ROPE)]
        t2_l = [Tile(cv.f32(512), f"t2_{i}") for i in range(NB_ROPE)]
        spt_l = [Tile(cv.f32(512), f"sp{i}") for i in range(NB_ROPE)] if even else None
        t1 = t1_l[0]; t2 = t2_l[0]
        rope_ctr = [0]
        rec_l = [Tile(cv.f32(512), "rec0"), Tile(cv.f32(512), "rec1")]
        rec = rec_l[0]
        rs_t = Tile(cv.f32(512), "rs")
        if not even:
            spt_l = [rs_t]
        ntmp = ([Tile(Qap[:, 512 * i:512 * (i + 1)], f"nsq{i}") for i in range(4)],
                [Tile(Kap[:, 1024 * i:1024 * (i + 1)].bitcast(F32), f"nrs{i}") for i in range(2)],
                [Tile(Vap.rearrange("p t s d -> p (t s d)")[:, 1024 * i:1024 * (i + 1)].bitcast(F32), f"nxr{i}") for i in range(2)])
        wq = [Tile(cv.bf(8 * 128).rearrange("p (k m) -> p k m", m=128), f"wq{i}") for i in range(2)]
        wk = Tile(cv.bf(8 * 128).rearrange("p (k m) -> p k m", m=128), "wk")
        wv = Tile(cv.bf(8 * 128).rearrange("p (k m) -> p k m", m=128), "wv")
        if even:
            identT = Tile(cv.bf(128), "ident")
            fw.dma("pool", identT[:, :], ident_d, s_misc6, writes=[identT])
            bmask = Tile(cv.bf(6 * 512).rearrange("p (a f) -> p a f", f=512), "bmask")
            fw.dma("pool", bmask[:, :, :], bmask_d.rearrange("a p f -> p a f"), s_misc, writes=[bmask])
        else:
            cqn_ap = cv.bf(2 * S).rearrange("p (c s) -> p c s", s=S)
            cqn = [[Tile(cqn_ap[:, c, 512 * t:512 * (t + 1)], f"cq{c}_{t}") for t in range(4)] for c in range(2)]
            Pt.append(Tile(cqn_ap[:, 0, 0:1024], "P2alias"))
            ckvn_ap = cv.bf(S)
            ckvn = [Tile(ckvn_ap[:, 512 * t:512 * (t + 1)], f"ckv{t}") for t in range(4)]
            kr_ap = cv.bf(S)
            krT = [Tile(kr_ap[:, 512 * t:512 * (t + 1)], f"kr{t}") for t in range(4)]
            dEf = Tile(cv.bf(2 * 1024).rearrange("p (h f) -> p h f", f=1024), "dEf")
            dEi = Tile(cv.bf(2 * 1024).rearrange("p (h f) -> p h f", f=1024), "dEi")
        brr = BankRR([0, 1, 2, 3, 4, 5, 6, 7])
        ones_all = permT[:, 3, :]
        blockones = permT[:, 2, :]

        rmsnorm_T(lambda c, cs: xT_h[:, c, cs], S, gb + G_MIX, lambda c, cs: hT_ap[:, c, cs], lambda c, t: hT[c][t],
                  8, brr, ntmp, 1.0 / D, ones_all, src_tiles=lambda c, t: xT[c][t])
        fw.barrier()

        def proj_bank(b, M, wt, wcols, src_ap, src_tiles, KC, t, krows=slice(0, 128)):
            pairs = [(wt[krows, kc, wcols], src_ap(kc, slice(512 * t, 512 * (t + 1)))) for kc in range(KC)]
            mm_group(b[0:M, :], b, pairs, [wt] + [src_tiles(kc, t) for kc in range(KC)])

        def load_tables(i):
            fw.dma("pool", cosT[:, :], tables_d[2 * i], s_tab[0], writes=[cosT])
            fw.dma("pool", sinT[:, :], tables_d[2 * i + 1], s_tab[1], writes=[sinT])

        def run_chains(chains):
            n = len(chains)
            ctxs = [None] * n
            fins = [None] * n
            for i in range(n + SKEW + 1):
                if i < n:
                    ctxs[i] = chains[i][0]()
                if SKEW <= i < n + SKEW:
                    fins[i - SKEW] = chains[i - SKEW][1](ctxs[i - SKEW])
                jf = i - SKEW - 1
                if jf >= 0 and fins[jf] is not None:
                    fins[jf]()

        def mk_chain(wt, M, wcols, src_ap, src_tiles, KC, t, rows, dst_t, mode, swap_ap=None, gains=None, norm=False, after=None):
            def A():
                b = brr.get(exclude=live_banks)
                live_banks.append(b)
                proj_bank(b, M, wt, wcols, src_ap, src_tiles, KC, t)
                if mode == "plain":
                    return (b, None, 0)
                qi = rope_ctr[0] % len(qsb_l)
                ri = rope_ctr[0] % NB_ROPE
                rope_ctr[0] += 1
                qsb = qsb_l[qi]
                fw.emit("act", lambda e: e.activation(out=qsb[rows, :], in_=b[rows, :], func=AF.Copy), reads=[b], writes=[qsb])
                return (b, qsb, ri)

            def B(ctx):
                b, qsb, ri = ctx
                if mode == "plain":
                    fw.emit("dve", lambda e: e.tensor_copy(out=dst_t[t][rows, :], in_=b[rows, :]), reads=[b], writes=[dst_t[t]])
                    fin = None
                else:
                    fin = rope_B(qsb, ri, dst_t, t, rows, swap_ap, gains, norm, b)
                live_banks.remove(b)
                if after is not None:
                    after()
                return fin
            return (A, B)

        def rope_B(qsb, ri, dst_t, t, rows, swap_ap, gains=None, norm=False, b=None):
            tok = slice(512 * t, 512 * (t + 1))
            t1 = t1_l[ri]; t2 = t2_l[ri]; spt = spt_l[ri]; sqr = sq[ri % len(sq)]
            b2 = brr.get(exclude=live_banks)
            fw.emit("pe", lambda e: e.matmul(b2[rows, :], lhsT=swap_ap, rhs=qsb[rows, :], start=True, stop=True),
                    reads=[qsb, permT], writes=[b2])
            if norm:
                fw.emit("pool", lambda e: e.tensor_tensor(out=sqr[rows, :], in0=qsb[rows, :], in1=qsb[rows, :], op=ALU.mult),
                        reads=[qsb], writes=[sqr])
                b3 = brr.get(exclude=live_banks + [b2])
                fw.emit("pe", lambda e: e.matmul(b3[rows, :], lhsT=blockones, rhs=sqr[rows, :], start=True, stop=True),
                        reads=[sqr, permT], writes=[b3])
                live_banks.append(b3)
            if gains is not None:
                g0, g1 = gains
                fw.emit("dve", lambda e: e.scalar_tensor_tensor(out=t1[rows, :], in0=b[rows, :], scalar=gcol(g0, rows), in1=cosT[rows, tok], op0=ALU.mult, op1=ALU.mult),
                        reads=[b, cosT, gcolsT], writes=[t1])
                fw.emit("dve", lambda e: e.scalar_tensor_tensor(out=t2[rows, :], in0=b2[rows, :], scalar=gcol(g1, rows), in1=sinT[rows, tok], op0=ALU.mult, op1=ALU.mult),
                        reads=[b2, sinT, gcolsT], writes=[t2])
            else:
                fw.emit("dve", lambda e: e.tensor_tensor(out=t1[rows, :], in0=qsb[rows, :], in1=cosT[rows, tok], op=ALU.mult),
                        reads=[qsb, cosT], writes=[t1])
                fw.emit("dve", lambda e: e.tensor_tensor(out=t2[rows, :], in0=b2[rows, :], in1=sinT[rows, tok], op=ALU.mult),
                        reads=[b2, sinT], writes=[t2])
            if norm:
                fw.emit("dve", lambda e: e.tensor_tensor(out=t1[rows, :], in0=t1[rows, :], in1=t2[rows, :], op=ALU.add),
                        reads=[t1, t2], writes=[t1])

                def fin():
                    rsqrt_ps(b3, b3[rows, :], spt, spt[rows, :], 1.0, 1, rows)
                    live_banks.remove(b3)
                    fw.emit("dve", lambda e: e.tensor_tensor(out=dst_t[t][rows, :], in0=t1[rows, :], in1=spt[rows, :], op=ALU.mult),
                            reads=[t1, spt], writes=[dst_t[t]])
                return fin
            else:
                fw.emit("pool", lambda e: e.tensor_tensor(out=dst_t[t][rows, :], in0=t1[rows, :], in1=t2[rows, :], op=ALU.add),
                        reads=[t1, t2], writes=[dst_t[t]])

        def plain_evac(b, dst_t, t, rows):
            fw.emit("dve", lambda e: e.tensor_copy(out=dst_t[t][rows, :], in_=b[rows, :]), reads=[b], writes=[dst_t[t]])

        def init_v_ones():
            for g in range(4):
                fw.emit("pool", lambda e, g=g: e.memset(Vt[g][:, :, :, :], 1.0), writes=[Vt[g]])

        def v_proj(wt, ncols, src_ap, src_tiles, KC, nslots):
            for g in range(4):
                b = brr.get()
                for q in range(4):
                    tt = 4 * g + q
                    pairs = [(src_ap(kc, slice(128 * tt, 128 * (tt + 1))), wt[:, kc, 0:ncols]) for kc in range(KC)]
                    mm_group(b[:, 128 * q:128 * q + ncols], b, pairs, [wt] + [src_tiles(kc, tt // 4) for kc in range(KC)])
                fw.emit("dve", lambda e, g=g, b=b: e.tensor_copy(
                    out=Vt[g][:, :, 0:nslots, 64:128],
                    in_=b[:, :].rearrange("p (q s d) -> p q s d", q=4, s=2)[:, :, 0:nslots, :]),
                    reads=[b], writes=[Vt[g]])

        def attention(slots, items_for_T, scale, chunk, addcols, vslot, post_exp=None, lookahead=1, act_recip=False, nS=2, o_single=False, add_mm=None, act_recip_last=False):
            work = []
            for T in range(4):
                its = items_for_T(T)
                for ii, it in enumerate(its):
                    work.append((T, it, ii == 0, ii == len(its) - 1))
            started = {}

            def crange(it):
                cr = it[0][3] if len(it[0]) > 3 and it[0][3] is not None else (0, 512)
                return cr

            def emit_qk(w, idx):
                T, it, first, last = w
                c0, c1 = crange(it)
                for bi, sub in enumerate(it):
                    sl, kt = sub[0], sub[1]
                    b = bank[2 * (idx % nS) + bi]
                    qt, qr = slots[sl]["q"]; ktl, kr = slots[sl]["k"]
                    am = add_mm(T, sub, c0, c1) if add_mm is not None else None
                    fw.emit("pe", lambda e, b=b, qt=qt, qr=qr, ktl=ktl, kr=kr, kt=kt, T=T, c0=c0, c1=c1, am=am: e.matmul(
                        b[:, 0:c1 - c0], lhsT=ktl[kt // 4][kr, 128 * (kt % 4):128 * (kt % 4 + 1)], rhs=qt[T][qr, c0:c1], start=True, stop=(am is None)),
                        reads=[ktl[kt // 4], qt[T]], writes=[b], inc=(am is None))
                    if am is not None:
                        aap, atiles = am
                        fw.emit("pe", lambda e, b=b, aap=aap, c0=c0, c1=c1, idt=identT: e.matmul(
                            b[:, 0:c1 - c0], lhsT=idt[:, :], rhs=aap, start=False, stop=True),
                            reads=[identT] + atiles, writes=[b])

            def emit_rest(w, idx):
                T, it, first, last = w
                n = len(it)
                c0, c1 = crange(it)
                nc_ = c1 - c0
                bs = [bank[2 * (idx % nS) + bi] for bi in range(n)]
                P = Pt[idx % nS]
                src = psb[idx % nS]
                if nc_ == 512:
                    fw.emit("act", lambda e: e.activation(out=P[:, 0:512 * n], in_=src[:, 0:512 * n], func=AF.Exp, scale=scale),
                            reads=bs, writes=[P])
                else:
                    fw.emit("act", lambda e: e.activation(out=P[:, :].rearrange("p (b f) -> p b f", b=2)[:, 0:n, 0:nc_],
                                                          in_=src[:, :].rearrange("p (b f) -> p b f", b=2)[:, 0:n, 0:nc_], func=AF.Exp, scale=scale),
                            reads=bs, writes=[P])
                if post_exp is not None:
                    post_exp(P, T, it, c0, c1)
                for bi, sub in enumerate(it):
                    sl, kt = sub[0], sub[1]
                    ob = bank[6 + sl] if o_single else bank[4 + 2 * (T % 2) + sl]
                    st = not started.get((sl, T), False)
                    started[(sl, T)] = True
                    is_last = last and all(s2[0] != sl for s2 in it[bi + 1:])
                    vs = vslot(sl)
                    lo = slots[sl]["lo"]
                    vcols = slice(64, 192) if lo else slice(0, 128)
                    fw.emit("pe", lambda e, ob=ob, kt=kt, vs=vs, vcols=vcols, bi=bi, st=st, is_last=is_last, c0=c0, c1=c1, nc_=nc_: e.matmul(
                        ob[:, c0:c1], lhsT=Vt[kt // 4][:, kt % 4, vs, vcols], rhs=P[:, 512 * bi:512 * bi + nc_], start=st, stop=is_last,
                        skip_group_check=(add_mm is not None)),
                        reads=[Vt[kt // 4], P], writes=[ob], inc=True)
                if last:
                    for sl in sorted(set(s2[0] for s2 in it)):
                        ob = bank[6 + sl] if o_single else bank[4 + 2 * (T % 2) + sl]
                        rec = rec_l[(T + sl) % 2]
                        lo = slots[sl]["lo"]
                        orow = slice(0, 64) if lo else slice(64, 128)
                        drow = slice(64, 128) if lo else slice(0, 64)
                        ac = addcols(sl)
                        if act_recip or (act_recip_last and T == 3):
                            if ac is None:
                                fw.emit("act", lambda e, ob=ob, drow=drow, rec=rec: e.activation(out=rec[drow, :], in_=ob[drow, :], func=AF.Ln), reads=[ob], writes=[rec])
                            else:
                                fw.emit("act", lambda e, ob=ob, drow=drow, rec=rec, ac=ac: e.activation(out=rec[drow, :], in_=ob[drow, :], func=AF.Ln, bias=ac[drow, :]), reads=[ob, sinkT], writes=[rec])
                            fw.emit("act", lambda e, drow=drow, rec=rec: e.activation(out=rec[drow, :], in_=rec[drow, :], func=AF.Exp, scale=-1.0), reads=[rec], writes=[rec])
                        elif ac is None:
                            fw.emit("dve", lambda e, ob=ob, drow=drow, rec=rec: e.reciprocal(out=rec[drow, :], in_=ob[drow, :]), reads=[ob], writes=[rec])
                        else:
                            fw.emit("dve", lambda e, ob=ob, drow=drow, ac=ac, rec=rec: e.tensor_scalar(out=rec[drow, :], in0=ob[drow, :], scalar1=ac[drow, :], scalar2=None, op0=ALU.add),
                                    reads=[ob, sinkT], writes=[rec])
                            fw.emit("dve", lambda e, drow=drow, rec=rec: e.reciprocal(out=rec[drow, :], in_=rec[drow, :]), reads=[rec], writes=[rec])
                        fw.emit("dve", lambda e, ob=ob, orow=orow, drow=drow, T=T, rec=rec: e.tensor_tensor(out=OT[chunk][T][orow, :], in0=ob[orow, :], in1=rec[drow, :], op=ALU.mult),
                                reads=[ob, rec], writes=[OT[chunk][T]])

            n = len(work)
            for i in range(n + lookahead):
                if i < n:
                    emit_qk(work[i], i)
                if i - lookahead >= 0:
                    emit_rest(work[i - lookahead], i - lookahead)

        hsrc = lambda kc, cs: hT_ap[:, kc, cs]
        hsrc_t = lambda kc, t: hT[kc][t]
        init_v_ones()

        if even:
            W = w_in_ab[j].rearrange("(kc p) m -> p kc m", p=128)
            def loads_even(ui):
                mixer, u = divmod(ui, 4)
                g = u // 2
                qbase = 0 if mixer == 0 else 768
                kbase = 512 if mixer == 0 else 1280
                vbase = 640 if mixer == 0 else 1408
                wqt = wq[u % 2]
                load_w(wqt, wqt[:, :, :], W[:, :, qbase + 128 * u:qbase + 128 * (u + 1)], s_wq[u % 2])
                if u % 2 == 0:
                    load_w(wk, wk[:, :, 0:64], W[:, :, kbase + 64 * g:kbase + 64 * (g + 1)], s_wk)
                    load_w(wk, wk[:, :, 64:128], W[:, :, kbase + 64 * g:kbase + 64 * (g + 1)], s_wk)
                    load_w(wv, wv[:, :, 0:64], W[:, :, vbase + 64 * g:vbase + 64 * (g + 1)], s_wv)
                    load_w(wv, wv[:, :, 64:128], W[:, :, vbase + 64 * g:vbase + 64 * (g + 1)], s_wv)

            loads_even(0)
            for mixer in range(2):
                load_tables(mixer)
                for u in range(4):
                    g = u // 2
                    wqt = wq[u % 2]
                    newkv = (u % 2 == 0)
                    rows = slice(0, 128)
                    chains = []
                    for t in range(4):
                        if mixer == 0:
                            chains.append(mk_chain(wqt, 128, slice(0, 128), hsrc, hsrc_t, 8, t, rows, Qt, "rope", swap64(rows), (G_QA(j), G_QA(j) + 1), True))
                            if newkv:
                                chains.append(mk_chain(wk, 128, slice(0, 128), hsrc, hsrc_t, 8, t, rows, Kt, "rope", swap64(rows), (G_QA(j) + 2, G_QA(j) + 3), True))
                        else:
                            chains.append(mk_chain(wqt, 128, slice(0, 128), hsrc, hsrc_t, 8, t, rows, Qt, "rope", swap64(rows)))
                            if newkv:
                                chains.append(mk_chain(wk, 128, slice(0, 128), hsrc, hsrc_t, 8, t, rows, Kt, "rope", swap64(rows)))
                    run_chains(chains)
                    if newkv:
                        v_proj(wv, 128, hsrc, hsrc_t, 8, 1)
                    if 4 * mixer + u + 1 < 8:
                        loads_even(4 * mixer + u + 1)
                    slots = [dict(q=(Qt, slice(0, 64)), k=(Kt, slice(0, 64)), lo=True),
                             dict(q=(Qt, slice(64, 128)), k=(Kt, slice(64, 128)), lo=False)]
                    if mixer == 0:
                        items = lambda T: [[(0, kt, None), (1, kt, None)] for kt in range(16)]
                        attention(slots, items, 8.0, u, lambda sl: None, lambda sl: 0)
                    else:
                        def items(T):
                            out_ = []
                            for kt in range(max(0, 4 * T - 1), min(16, 4 * T + 5)):
                                rel = kt - 4 * T
                                cr = (max(0, 128 * rel - 128), min(512, 128 * rel + 256))
                                out_.append([(0, kt, rel + 1, cr), (1, kt, rel + 1, cr)])
                            return out_

                        def addm(T, sub, c0, c1):
                            return (bmask[:, sub[2], c0:c1], [bmask])

                        sc0 = 8 * j
                        attention(slots, items, 0.125, 4 + u, lambda sl, u=u: sinkT[:, sc0 + 2 * u + sl:sc0 + 2 * u + sl + 1], lambda sl: 0, act_recip=False, add_mm=addm, act_recip_last=True)
            Wout = w_out_ab[j]
        else:
            W = w_in_cd[j].rearrange("(kc p) m -> p kc m", p=128)
            Wuq = w_uq[j].rearrange("(kc p) m -> p kc m", p=128)
            Wukv = w_ukv[j].rearrange("(kc p) m -> p kc m", p=128)
            load_tables(2)
            latc = [t1, t2]
            wl0, wl1 = wq[0], wq[1]
            load_w(wl0, wl0[:, :, :], W[:, :, 0:128], s_wq[0])
            load_w(wl1, wl1[:, :, :], W[:, :, 128:256], s_wq[1])
            for t in range(4):
                for c, wl in enumerate((wl0, wl1)):
                    b = brr.get()
                    proj_bank(b, 128, wl, slice(0, 128), hsrc, hsrc_t, 8, t)
                    fw.emit("dve", lambda e, b=b, c=c: e.tensor_copy(out=latc[c][:, :], in_=b[:, :]), reads=[b], writes=[latc[c]])
                bq = brr.get()
                for c in range(2):
                    fw.emit("pool", lambda e, c=c: e.tensor_tensor(out=sq[c][:, :], in0=latc[c][:, :], in1=latc[c][:, :], op=ALU.mult), reads=[latc[c]], writes=[sq[c]])
                    fw.emit("pe", lambda e, c=c, bq=bq: e.matmul(bq[:, :], lhsT=ones_all, rhs=sq[c][:, :], start=(c == 0), stop=(c == 1)), reads=[sq[c], permT], writes=[bq])
                rsqrt_ps(bq, bq[:, :], rs_t, rs_t[:, :], 1.0 / 256, 0, slice(0, 128))
                for c in range(2):
                    fw.emit("dve", lambda e, c=c, t=t: e.scalar_tensor_tensor(out=cqn[c][t][:, :], in0=latc[c][:, :], scalar=gcol(G_CQ(j) + c), in1=rs_t[:, :], op0=ALU.mult, op1=ALU.mult),
                            reads=[latc[c], rs_t, gcolsT], writes=[cqn[c][t]])
            load_w(wk, wk[:, :, :], W[:, :, 256:384], s_wk)
            for t in range(4):
                b = brr.get()
                proj_bank(b, 128, wk, slice(0, 128), hsrc, hsrc_t, 8, t)
                fw.emit("dve", lambda e, b=b: e.tensor_copy(out=t1[:, :], in_=b[:, :]), reads=[b], writes=[t1])
                bq = brr.get()
                fw.emit("pool", lambda e: e.tensor_tensor(out=sq[0][:, :], in0=t1[:, :], in1=t1[:, :], op=ALU.mult), reads=[t1], writes=[sq[0]])
                fw.emit("pe", lambda e, bq=bq: e.matmul(bq[:, :], lhsT=ones_all, rhs=sq[0][:, :], start=True, stop=True), reads=[sq[0], permT], writes=[bq])
                rsqrt_ps(bq, bq[:, :], rs_t, rs_t[:, :], 1.0 / 128, 0, slice(0, 128))
                fw.emit("dve", lambda e, t=t: e.scalar_tensor_tensor(out=ckvn[t][:, :], in0=t1[:, :], scalar=gcol(G_CQ(j) + 2), in1=rs_t[:, :], op0=ALU.mult, op1=ALU.mult),
                        reads=[t1, rs_t, gcolsT], writes=[ckvn[t]])
            load_w(wv, wv[:, :, :], W[:, :, 320:448], s_wv)
            r96 = slice(64, 96)
            run_chains([mk_chain(wv, 128, slice(0, 128), hsrc, hsrc_t, 8, t, slice(0, 128), krT, "rope", swapC(slice(0, 128))) for t in range(4)])

            cq_src = lambda kc, cs: cqn_ap[:, kc, cs]
            cq_src_t = lambda kc, t: cqn[kc][t]
            ckv_src = lambda kc, cs: ckvn_ap[:, cs]
            ckv_src_t = lambda kc, t: ckvn[t]
            r0_96 = slice(0, 96)
            def loads_c(h):
                wqt = wq[h % 2]
                load_w(wqt, wqt[:, 0:2, 0:96], Wuq[:, :, 96 * h:96 * (h + 1)], s_wq[h % 2])
                load_w(wk, wk[:, 0:1, 0:64], Wukv[:, :, 128 * h:128 * h + 64], s_wk)
                load_w(wv, wv[:, 0:1, 0:64], Wukv[:, :, 128 * h + 64:128 * h + 128], s_wv)

            loads_c(0)
            for h in range(8):
                wqt = wq[h % 2]
                chains = []
                for t in range(4):
                    chains.append(mk_chain(wqt, 96, slice(0, 96), cq_src, cq_src_t, 2, t, r0_96, Qt, "rope", swapC(r0_96)))
                    chains.append(mk_chain(wk, 64, slice(0, 64), ckv_src, ckv_src_t, 1, t, slice(0, 64), Kt, "plain",
                                           after=lambda t=t: fw.emit("act", lambda e, t=t: e.activation(out=Kt[t][r96, :], in_=krT[t][r96, :], func=AF.Copy), reads=[krT[t]], writes=[Kt[t]])))
                run_chains(chains)
                v_proj(wv, 64, ckv_src, ckv_src_t, 1, 1)
                if h + 1 < 8:
                    loads_c(h + 1)
                lo = (h % 2 == 0)
                slots = [dict(q=(Qt, r0_96), k=(Kt, r0_96), lo=lo)]
                items = lambda T: [[(0, 2 * i, None), (0, 2 * i + 1, None)] for i in range(8)]
                attention(slots, items, 96.0 ** -0.5, h // 2, lambda sl: None, lambda sl: 0)

            fw.barrier()
            identT = Tile(sq[0].ap[:, 0:128], "ident")
            fw.dma("pool", identT[:, :], ident_d, s_misc6, writes=[identT])
            def loads_d(u):
                wqt = wq[u % 2]
                load_w(wqt, wqt[:, :, :], W[:, :, 416 + 128 * u:416 + 128 * (u + 1)], s_wq[u % 2])
                load_w(wk, wk[:, :, :], W[:, :, 928 + 128 * u:928 + 128 * (u + 1)], s_wk)
                load_w(wv, wv[:, :, :], W[:, :, 1440 + 128 * u:1440 + 128 * (u + 1)], s_wv)

            loads_d(0)
            for u in range(4):
                wqt = wq[u % 2]
                stg = [t1_l[0], t2_l[0]]
                stv = [x[:, :].bitcast(BF16) for x in stg]
                fw.dma("pool", stv[0], dneg_d[0], s_misc4, writes=[stg[0]])
                fw.dma("pool", stv[1], dneg_d[1], s_misc5, writes=[stg[1]])
                for hh in range(2):
                    fw.dma("pool", dEi[:, hh, :], dbias_d[j, 2 * u + hh], s_misc, writes=[dEi])
                for hh in range(2):
                    fw.emit("dve", lambda e, hh=hh: e.scalar_tensor_tensor(out=dEf[:, hh, :], in0=dEi[:, hh, :], scalar=8.0, in1=stv[0], op0=ALU.mult, op1=ALU.add),
                            reads=[dEi, stg[0]], writes=[dEf])
                for hh in range(2):
                    fw.emit("dve", lambda e, hh=hh: e.scalar_tensor_tensor(out=dEi[:, hh, :], in0=dEi[:, hh, :], scalar=8.0, in1=stv[1], op0=ALU.mult, op1=ALU.add),
                            reads=[dEi, stg[1]], writes=[dEi])
                rows = slice(0, 128)
                chains = []
                for t in range(4):
                    chains.append(mk_chain(wqt, 128, slice(0, 128), hsrc, hsrc_t, 8, t, rows, Qt, "plain"))
                    chains.append(mk_chain(wk, 128, slice(0, 128), hsrc, hsrc_t, 8, t, rows, Kt, "plain"))
                run_chains(chains)
                v_proj(wv, 128, hsrc, hsrc_t, 8, 2)
                if u + 1 < 4:
                    loads_d(u + 1)
                slots = [dict(q=(Qt, slice(0, 64)), k=(Kt, slice(0, 64)), lo=True),
                         dict(q=(Qt, slice(64, 128)), k=(Kt, slice(64, 128)), lo=False)]

                def items(T):
                    lo_r = min(max(8 * T - 4, 0), 24)
                    hi_r = min(max(8 * T + 7 - 4, 0), 24) + 7
                    out_ = []
                    for kt in range(lo_r // 2, hi_r // 2 + 1):
                        vb = [bq for bq in range(8) if d_row_valid(8 * T + bq, 2 * kt) or d_row_valid(8 * T + bq, 2 * kt + 1)]
                        cr = (64 * min(vb), 64 * (max(vb) + 1))
                        out_.append([(0, kt, None, cr), (1, kt, None, cr)])
                    return out_

                def addm(T, sub, c0, c1):
                    sl, kt = sub[0], sub[1]
                    tab = dEi if T in (1, 2) else dEf
                    jj0 = 7 - 2 * kt + 8 * T + c0 // 64
                    assert 0 <= jj0 and jj0 + (c1 - c0) // 64 <= 16, (T, kt, jj0)
                    return (tab[:, sl, 64 * jj0:64 * jj0 + (c1 - c0)], [tab])

                def post(P, T, it, c0, c1):
                    if T not in (0, 3):
                        return
                    blo = c0 // 64
                    nb = (c1 - c0) // 64
                    for bi, sub in enumerate(it):
                        kt = sub[1]
                        base = 512 * bi
                        for a_ in range(2):
                            kr_ = 2 * kt + a_
                            pr = slice(64 * a_, 64 * (a_ + 1))
                            valid = [d_row_valid(8 * T + bq, kr_) for bq in range(blo, blo + nb)]
                            bq = 0
                            while bq < nb:
                                e0 = bq
                                while bq < nb and valid[bq] == valid[e0]:
                                    bq += 1
                                if not valid[e0]:
                                    cs = slice(base + 64 * e0, base + 64 * bq)
                                    fw.emit("pool", lambda e, cs=cs, pr=pr: e.memset(P[pr, cs], 0.0), writes=[P])

                attention(slots, items, 0.125, 4 + u, lambda sl: None, lambda sl: sl, post_exp=post, act_recip=False, add_mm=addm, act_recip_last=True)
            Wout = w_out_cd[j]

        def out_proj(Wsrc, KC, src_ap, src_tiles, wbufs, wsems):
            Wr = Wsrc.rearrange("(kc p) m -> p kc m", p=128)
            for fo in range(8):
                wt = wbufs[fo % len(wbufs)]
                load_w(wt, wt[:, 0:KC, :], Wr[:, :, 128 * fo:128 * (fo + 1)], wsems[fo % len(wbufs)])
                for t in range(4):
                    b = brr.get()
                    proj_bank(b, 128, wt, slice(0, 128), src_ap, src_tiles, KC, t)
                    fw.emit("dve", lambda e, b=b, fo=fo, t=t: e.tensor_tensor(out=xT[fo][t][:, :], in0=xT[fo][t][:, :], in1=b[:, :], op=ALU.add),
                            reads=[b, xT[fo][t]], writes=[xT[fo][t]])

        out_proj(Wout, 8, lambda kc, cs: OT_ap[:, kc, cs], lambda kc, t: OT[kc][t],
                 [wq[0], wq[1], wk, wv], [s_wq[0], s_wq[1], s_wk, s_wv])
        if dbg_stop == (L, "mix"):
            s_dbg = fw.dmasem("dbg")
            for c in range(8):
                for t in range(4):
                    srcT = hT if os.environ.get("DBG_DUMP", "OT") == "hT" else OT
                    tk = fw.dma("pool", dbg_d[c, :, 512 * t:512 * (t + 1)], srcT[c][t][:, :], s_dbg, reads=[srcT[c][t]])
            fw.wait_tokens("pool", [tk])
            break

        brr = BankRR([0, 1, 2, 3, 4, 5])
        fw.barrier()
        rmsnorm_T(lambda c, cs: xT_h[:, c, cs], S, gb + G_XQ, lambda c, cs: hT_ap[:, c, cs], lambda c, t: hT[c][t],
                  8, brr, ntmp, 1.0 / D, ones_all, src_tiles=lambda c, t: xT[c][t])
        memf_ap = None
        memf = Tile(Vap.rearrange("p t s d -> p (t s d)")[:, 0:4096].bitcast(F32).rearrange("p (c s) -> p c s", s=256), "memf")
        memn = Tile(Vap.rearrange("p t s d -> p (t s d)")[:, 4096:6144].rearrange("p (c s) -> p c s", s=256), "memn")
        fw.barrier()
        fw.dma("sp", memf[:, :, :], memT_d.rearrange("c p s -> p c s"), s_mem, writes=[memf])
        rmsnorm_T(lambda c, cs: memf[:, c, cs], 256, gb + G_MEM, lambda c, cs: memn[:, c, cs], lambda c, t: memn,
                  8, brr, (sq, rs_t), 1.0 / D, ones_all, src_tiles=lambda c, t: memf)
        kx = Tile(Kap[:, 0:1024].rearrange("p (h s) -> p h s", s=256), "kx")
        vx = Tile(Kap[:, 1024:2048].rearrange("p (t f) -> p t f", f=512), "vx")
        Wkv = w_xkv[L].rearrange("(kc p) m -> p kc m", p=128)
        Wq = w_xq[L].rearrange("(kc p) m -> p kc m", p=128)
        fw.barrier()
        xw = [wq[0], wq[1], wk, wv]
        xs = [s_wq[0], s_wq[1], s_wk, s_wv]
        for h in range(4):
            wt = xw[h]
            load_w(wt, wt[:, :, :], Wkv[:, :, 128 * h:128 * (h + 1)], xs[h])
        for h in range(4):
            wt = xw[h]
            b = brr.get()
            pairs = [(wt[:, kc, :], memn[:, kc, :]) for kc in range(8)]
            mm_group(b[:, 0:256], b, pairs, [wt, memn])
            fw.emit("dve", lambda e, b=b, h=h: e.tensor_copy(out=kx[:, h, :], in_=b[:, 0:256]), reads=[b], writes=[kx])
        for hp in range(4):
            wt = xw[hp]
            load_w(wt, wt[:, :, :], Wkv[:, :, 512 + 128 * hp:512 + 128 * (hp + 1)], xs[hp])
        for hp in range(4):
            wt = xw[hp]
            for tt in range(2):
                b = brr.get()
                pairs = [(memn[:, kc, 128 * tt:128 * (tt + 1)], wt[:, kc, :]) for kc in range(8)]
                mm_group(b[:, 0:128], b, pairs, [wt, memn])
                fw.emit("dve", lambda e, b=b, hp=hp, tt=tt: e.tensor_copy(out=vx[:, tt, 128 * hp:128 * (hp + 1)], in_=b[:, 0:128]), reads=[b], writes=[vx])
        xscale = 128.0 ** -0.5
        for h in range(4):
            wt = xw[h]
            load_w(wt, wt[:, :, :], Wq[:, :, 128 * h:128 * (h + 1)], xs[h])
        for h in range(4):
            wt = xw[h]
            for t in range(4):
                b = brr.get()
                proj_bank(b, 128, wt, slice(0, 128), hsrc, hsrc_t, 8, t)
                plain_evac(b, OT[4 + h], t, slice(0, 128))
        xitems = [(h, t) for h in range(4) for t in range(4)]

        def x_qk(i):
            h, t = xitems[i]
            for kt in range(2):
                bb = bank[2 * (i % 2) + kt]
                fw.emit("pe", lambda e, bb=bb, kt=kt, t=t, h=h: e.matmul(bb[:, :], lhsT=kx[:, h, 128 * kt:128 * (kt + 1)], rhs=OT[4 + h][t][:, :], start=True, stop=True),
                        reads=[kx, OT[4 + h][t]], writes=[bb])

        def x_rest(i):
            h, t = xitems[i]
            p = i % 2
            P = Pt[p]
            fw.emit("act", lambda e, p=p, P=P: e.activation(out=P[:, :], in_=psb[p][:, :], func=AF.Exp, scale=xscale),
                    reads=[bank[2 * p], bank[2 * p + 1]], writes=[P])
            ob = bank[4 + 2 * p]; db = bank[5 + 2 * p]; rc = rec_l[p]
            for kt in range(2):
                fw.emit("pe", lambda e, kt=kt, P=P, h=h, ob=ob: e.matmul(ob[:, :], lhsT=vx[:, kt, 128 * h:128 * (h + 1)], rhs=P[:, 512 * kt:512 * (kt + 1)], start=(kt == 0), stop=(kt == 1)),
                        reads=[vx, P], writes=[ob])
            for kt in range(2):
                fw.emit("pe", lambda e, kt=kt, P=P, db=db: e.matmul(db[:, :], lhsT=ones_all, rhs=P[:, 512 * kt:512 * (kt + 1)], start=(kt == 0), stop=(kt == 1)),
                        reads=[permT, P], writes=[db])

        def x_fin(i):
            h, t = xitems[i]
            p = i % 2
            ob = bank[4 + 2 * p]; db = bank[5 + 2 * p]; rc = rec_l[p]
            fw.emit("act", lambda e, db=db, rc=rc: e.activation(out=rc[:, :], in_=db[:, :], func=AF.Ln), reads=[db], writes=[rc])
            fw.emit("act", lambda e, rc=rc: e.activation(out=rc[:, :], in_=rc[:, :], func=AF.Exp, scale=-1.0), reads=[rc], writes=[rc])
            fw.emit("dve", lambda e, h=h, t=t, ob=ob, rc=rc: e.tensor_tensor(out=OT[h][t][:, :], in0=ob[:, :], in1=rc[:, :], op=ALU.mult),
                    reads=[ob, rc], writes=[OT[h][t]])

        nx = len(xitems)
        for i in range(nx + 2):
            if i < nx:
                x_qk(i)
            if 1 <= i <= nx:
                x_rest(i - 1)
            if i >= 2:
                x_fin(i - 2)
        out_proj(w_xo[L], 4, lambda kc, cs: OT_ap[:, kc, cs], lambda kc, t: OT[kc][t],
                 [wq[0], wq[1], wk, wv], [s_wq[0], s_wq[1], s_wk, s_wv])
        if dbg_stop == (L, "xattn"):
            break

        fw.barrier()
        cv = Carver()
        hT_ap = cv.bf(8 * S).rearrange("p (c s) -> p c s", s=S)
        hT = [[Tile(hT_ap[:, c, 512 * t:512 * (t + 1)], f"fh{c}_{t}") for t in range(4)] for c in range(8)]
        act_ap = cv.bf(22 * 1024).rearrange("p (j s) -> p j s", s=1024)
        actT = [[Tile(act_ap[:, jj, 512 * t:512 * (t + 1)], f"a{jj}_{t}") for t in range(2)] for jj in range(22)]
        sq_f = [Tile(cv.bf(512), f"fsq{i}") for i in range(4)]
        rs_f = [Tile(cv.f32(512), "frs"), Tile(cv.f32(512), "frs1")]
        xr_f = [Tile(cv.f32(512), "fxr0"), Tile(cv.f32(512), "fxr1")]
        sg = [Tile(cv.f32(512), f"sg{i}") for i in range(2)]
        NWG, NWD = 4, 3
        wg = [Tile(cv.bf(8 * 256).rearrange("p (k m) -> p k m", m=256), f"wg{i}") for i in range(NWG)]
        wd = [Tile(cv.bf(22 * 128).rearrange("p (k m) -> p k m", m=128), f"wd{i}") for i in range(NWD)]
        brr = BankRR([0, 1, 2, 3, 4, 5, 6, 7])
        rmsnorm_T(lambda c, cs: xT_h[:, c, cs], S, gb + G_FFN, lambda c, cs: hT_ap[:, c, cs], lambda c, t: hT[c][t],
                  8, brr, (sq_f, rs_f, xr_f), 1.0 / D, ones_all, src_tiles=lambda c, t: xT[c][t])
        Wg = w_gu[L].rearrange("(kc p) m -> p kc m", p=128)
        Wd = w_dn[L].rearrange("(kc p) m -> p kc m", p=128)
        for half in range(2):
            for jj in range(22):
                wt = wg[jj % NWG]
                load_w(wt, wt[:, :, 0:128], Wg[:, :, 128 * jj:128 * (jj + 1)], s_wg[jj % NWG])
                load_w(wt, wt[:, :, 128:256], Wg[:, :, DFF + 128 * jj:DFF + 128 * (jj + 1)], s_wg[jj % NWG])
                for t2i in range(2):
                    t = 2 * half + t2i
                    bg = brr.get()
                    proj_bank(bg, 128, wt, slice(0, 128), lambda kc, cs: hT_ap[:, kc, cs], lambda kc, t: hT[kc][t], 8, t)
                    bu = brr.get()
                    proj_bank(bu, 128, wt, slice(128, 256), lambda kc, cs: hT_ap[:, kc, cs], lambda kc, t: hT[kc][t], 8, t)
                    sgt = sg[t2i]
                    fw.emit("act", lambda e, bg=bg, sgt=sgt: e.activation(out=sgt[:, :], in_=bg[:, :], func=AF.Silu), reads=[bg], writes=[sgt])
                    fw.emit("dve", lambda e, bu=bu, sgt=sgt, jj=jj, t2i=t2i: e.tensor_tensor(out=actT[jj][t2i][:, :], in0=bu[:, :], in1=sgt[:, :], op=ALU.mult),
                            reads=[bu, sgt], writes=[actT[jj][t2i]])
            for fo in range(8):
                wt = wd[fo % NWD]
                load_w(wt, wt[:, :, :], Wd[:, :, 128 * fo:128 * (fo + 1)], s_wd[fo % NWD])
                for t2i in range(2):
                    t = 2 * half + t2i
                    b = brr.get()
                    pairs = [(wt[:, jj, :], act_ap[:, jj, 512 * t2i:512 * (t2i + 1)]) for jj in range(22)]
                    mm_group(b[:, :], b, pairs, [wt] + [actT[jj][t2i] for jj in range(22)])
                    fw.emit("dve", lambda e, b=b, fo=fo, t=t: e.tensor_tensor(out=xT[fo][t][:, :], in0=xT[fo][t][:, :], in1=b[:, :], op=ALU.add),
                            reads=[b, xT[fo][t]], writes=[xT[fo][t]])
        if dbg_stop == (L, "ffn"):
            break

    fw.barrier()
    cv = Carver()
    sq_z = [Tile(cv.bf(512), "zsq0"), Tile(cv.bf(512), "zsq1")]
    rs_z = Tile(cv.f32(512), "zrs")
    ob_ap = cv.f32(8 * S).rearrange("p (c s) -> p c s", s=S)
    obT = [[Tile(ob_ap[:, c, 512 * t:512 * (t + 1)], f"ob{c}_{t}") for t in range(4)] for c in range(8)]
    brr = BankRR([0, 1, 2, 3, 4, 5, 6, 7])
    if dbg_stop is None:
        rmsnorm_T(lambda c, cs: xT_h[:, c, cs], S, G_FINAL, lambda c, cs: ob_ap[:, c, cs], lambda c, t: obT[c][t],
                  8, brr, (sq_z, rs_z), 1.0 / D, permT[:, 3, :], src_tiles=lambda c, t: xT[c][t])
        src_t = obT
    else:
        src_t = xT
    last = []
    for t in range(4):
        for c in range(8):
            last.append(fw.dma("sp", out_d[c, :, 512 * t:512 * (t + 1)], src_t[c][t][:, :], s_out, reads=[src_t[c][t]]))
    fw.wait_tokens("sp", [last[-1]])
    fw.build()
    return nc


def make_in_maps(inputs):
    f = lambda a: np.ascontiguousarray(np.asarray(a, dtype=np.float32))
    x = f(inputs["x"]); mem = f(inputs["mem"])
    tables, perm, bm, dcol = host_consts()
    NG = 4 * 32 + 8 + 2 * 4 + 2 * 3
    gcols = np.zeros((128, NG), np.float32)
    colz = lambda g: np.asarray(g, np.float32).reshape(-1, 128).T
    for L in range(4):
        gcols[:, 32 * L + 0:32 * L + 8] = colz(inputs["g_mix"][L])
        gcols[:, 32 * L + 8:32 * L + 16] = colz(inputs["g_xq"][L])
        gcols[:, 32 * L + 16:32 * L + 24] = colz(inputs["g_mem"][L])
        gcols[:, 32 * L + 24:32 * L + 32] = colz(inputs["g_ffn"][L])
    gcols[:, 128:136] = colz(inputs["g_final"])
    p = np.arange(128)
    for j in range(2):
        gq = np.asarray(inputs["g_qa"][j], np.float32); gk = np.asarray(inputs["g_ka"][j], np.float32)
        gcols[:, 136 + 4 * j + 0] = gq[p % 64]
        gcols[:, 136 + 4 * j + 1] = gq[(p % 64 + 32) % 64]
        gcols[:, 136 + 4 * j + 2] = gk[p % 64]
        gcols[:, 136 + 4 * j + 3] = gk[(p % 64 + 32) % 64]
        gcols[:, 144 + 3 * j:144 + 3 * j + 2] = colz(inputs["g_cq"][j])
        gcols[:, 144 + 3 * j + 2] = np.asarray(inputs["g_ckv"][j], np.float32)
    sink = np.zeros((128, 16), np.float32)
    for j in range(2):
        sink[:, 8 * j:8 * j + 8] = np.asarray(inputs["sink_b"][j], np.float32)[None, :]
    rpb = np.asarray(inputs["rpb_d"], np.float32)
    kc = np.arange(64)[:, None]; c = np.arange(64)[None, :]
    cidx = np.clip(kc - c + 15, 0, 30)
    g = rpb[:, :, ::-1, :][:, :, :, cidx]
    g = np.transpose(g, (0, 1, 3, 2, 4))
    dbias = np.zeros((2, 8, 2, 64, 16, 64), np.float32)
    dbias[:, :, 0, :, 0:15, :] = g
    dbias[:, :, 1, :, 1:16, :] = g
    dbias = np.ascontiguousarray(dbias.reshape(2, 8, 128, 1024))
    shared = dict(
        w_in_ab=f(inputs["w_in_ab"]), w_out_ab=f(inputs["w_out_ab"]), w_in_cd=f(inputs["w_in_cd"]),
        w_out_cd=f(inputs["w_out_cd"]), w_uq=f(inputs["w_uq"]), w_ukv=f(inputs["w_ukv"]),
        w_xq=f(inputs["w_xq"]), w_xkv=f(inputs["w_xkv"]), w_xo=f(inputs["w_xo"]),
        w_gate_up=f(inputs["w_gate_up"]), w_down=f(inputs["w_down"]),
        gcols=gcols, sinkb=sink, tables=tables, perm=perm, ident=np.eye(128, dtype=np.float32), bmask=bm, dmask=dcol, dneg=((dcol - 1.0) * 30000.0).astype(np.float32), dbias=dbias)
    maps = []
    for b in range(NCORES):
        m = dict(shared)
        m["xT"] = np.ascontiguousarray(x[b].T.reshape(8, 128, S))
        m["memT"] = np.ascontiguousarray(mem[b].T.reshape(8, 128, 256))
        maps.append(m)
    return maps


def kernel(**inputs):
    nc = build_program()
    maps = make_in_maps(inputs)
    res = run_bass_kernel_spmd(nc, maps, core_ids=list(range(NCORES)))
    out = np.stack([np.asarray(r["outT"], np.float32).reshape(D, S).T for r in res.results], 0)
    return np.ascontiguousarray(out.astype(np.float32))
```

```python
import os
import numpy as np
import concourse.bass as bass
import concourse.mybir as mybir
from concourse.bass_utils import run_bass_kernel_spmd

F32 = mybir.dt.float32
BF16 = mybir.dt.bfloat16
ALU = mybir.AluOpType
AF = mybir.ActivationFunctionType

SEM_ROTATE = 30000
NCORES = 8
S = 2048
D = 1024
DFF = 2816
EPS = 1e-6


class Tile:
    __slots__ = ("ap", "name", "last_w", "readers", "excl", "lw_read")

    def __init__(self, ap, name="", excl=False):
        self.ap = ap
        self.name = name
        self.last_w = None
        self.readers = {}
        self.excl = excl
        self.lw_read = False

    def __getitem__(self, idx):
        return self.ap[idx]


class SemCounter:
    def __init__(self, fw, name, step=1):
        self.fw = fw
        self.name = name
        self.step = step
        self.gen = 0
        self.sem = fw.nc.alloc_semaphore(f"{name}_{self.gen}")
        self.count = 0
        fw.all_ctrs.append(self)

    def next_token(self):
        if (self.count + 1) * self.step > SEM_ROTATE:
            self.gen += 1
            self.sem = self.fw.nc.alloc_semaphore(f"{self.name}_{self.gen}")
            self.count = 0
        return (self.sem, (self.count + 1) * self.step)

    def commit(self):
        self.count += 1

    def cur_token(self):
        if self.count == 0:
            return None
        return (self.sem, self.count * self.step)


class Engine:
    def __init__(self, fw, key):
        self.key = key
        self.ops = []
        self.ctr = SemCounter(fw, f"s_{key}")
        self.seen = {}


class FW:
    def __init__(self, nc):
        self.nc = nc
        self.all_ctrs = []
        self.eng = {k: Engine(self, k) for k in ("pe", "act", "dve", "pool", "sp")}

    def dmasem(self, name):
        return SemCounter(self, name, step=16)

    def _collect(self, e, reads, writes, skip_self=False):
        waits = {}

        def add(tok):
            if tok is None:
                return
            s, v = tok
            if skip_self and s is e.ctr.sem:
                return
            if e.seen.get(s, 0) >= v:
                return
            if waits.get(s, 0) < v:
                waits[s] = v

        for t in reads:
            if t.excl and t.lw_read and t.last_w is not None and t.last_w[0] is e.ctr.sem:
                continue
            add(t.last_w)
            if t.excl:
                for s, v in t.readers.items():
                    add((s, v))
        for t in writes:
            add(t.last_w)
            for s, v in t.readers.items():
                add((s, v))
        for s, v in waits.items():
            e.seen[s] = v
        return list(waits.items())

    def _update(self, tok, reads, writes):
        for t in reads:
            if t.excl:
                t.last_w = tok
                t.lw_read = True
            else:
                t.readers[tok[0]] = max(t.readers.get(tok[0], 0), tok[1])
        for t in writes:
            t.last_w = tok
            t.lw_read = False
            t.readers = {}

    def emit(self, ek, fn, reads=(), writes=(), inc=True):
        e = self.eng[ek]
        waits = self._collect(e, reads, writes, skip_self=(ek == "pe"))
        tok = e.ctr.next_token()
        if inc:
            e.ctr.commit()
        e.ops.append((waits, fn, tok if inc else None))
        self._update(tok, reads, writes)
        return tok

    def dma(self, qk, out, in_, sem, reads=(), writes=()):
        e = self.eng[qk]
        waits = self._collect(e, reads, writes)
        tok = sem.next_token()
        sem.commit()

        def fn(engine, out=out, in_=in_):
            return engine.dma_start(out=out, in_=in_)

        e.ops.append((waits, fn, tok + (True,)))
        self._update(tok, reads, writes)
        return tok

    def wait_tokens(self, ek, toks):
        e = self.eng[ek]
        waits = []
        for tok in toks:
            if tok is None:
                continue
            s, v = tok
            if e.seen.get(s, 0) < v:
                e.seen[s] = v
                waits.append((s, v))
        if waits:
            e.ops.append((waits, None, None))

    def barrier(self):
        toks = [c.cur_token() for c in self.all_ctrs]
        for ek in self.eng:
            self.wait_tokens(ek, toks)

    def build(self):
        nc = self.nc
        handles = {"pe": "tensor", "act": "scalar", "dve": "vector", "pool": "gpsimd", "sp": "sync"}
        with nc.Block() as block:
            for ek, attr in handles.items():
                e = self.eng[ek]
                if not e.ops:
                    continue

                def body(engine, e=e):
                    for waits, fn, tok in e.ops:
                        for s, v in waits:
                            engine.wait_ge(s, v)
                        if fn is None:
                            continue
                        ins = fn(engine)
                        if tok is not None:
                            ins.then_inc(tok[0], 16 if len(tok) == 3 else 1)

                getattr(block, attr)(body)


def _rope_inv(dim):
    return 10000.0 ** (-np.arange(0, dim, 2, dtype=np.float64) / dim)


def host_consts():
    pos = np.arange(S, dtype=np.float64)
    row = np.floor(pos / 64)
    col = pos % 64
    ang1d = pos[:, None] * _rope_inv(64)[None, :]
    ang2d = np.concatenate([row[:, None] * _rope_inv(32)[None, :],
                            col[:, None] * _rope_inv(32)[None, :]], -1)
    angc = pos[:, None] * _rope_inv(32)[None, :]

    def tab64(ang):
        cos = np.zeros((128, S)); sin = np.zeros((128, S))
        for p in range(128):
            d = p % 64
            cos[p] = np.cos(ang[:, d % 32])
            sin[p] = np.sin(ang[:, d % 32]) * (-1.0 if d < 32 else 1.0)
        return cos, sin

    c2, s2 = tab64(ang2d)
    c1, s1 = tab64(ang1d)
    cc = np.zeros((128, S)); sc = np.zeros((128, S))
    cc[0:64] = 1.0
    for p in range(64, 96):
        d = p - 64
        cc[p] = np.cos(angc[:, d % 16])
        sc[p] = np.sin(angc[:, d % 16]) * (-1.0 if d < 16 else 1.0)
    tables = np.stack([c2, s2, c1, s1, cc, sc]).astype(np.float32)

    perm = np.zeros((4, 128, 128), np.float32)
    for m in range(128):
        k = (m % 64 + 32) % 64 + 64 * (m // 64)
        perm[0, k, m] = 1.0
    for m in range(64, 96):
        k = 64 + ((m - 64 + 16) % 32)
        perm[1, k, m] = 1.0
    for k in range(128):
        for m in range(128):
            if k // 64 == m // 64:
                perm[2, k, m] = 1.0
    perm[3] = 1.0

    bm = np.zeros((6, 128, 512), np.float32)
    p = np.arange(128)[:, None]; f = np.arange(512)[None, :]
    for i, rel in enumerate(range(-1, 5)):
        bm[i] = np.where(np.abs(128 * rel + p - f) <= 128, 0.0, -30000.0).astype(np.float32)

    c = np.arange(64)
    c0 = np.clip(c - 8, 0, 48)
    colvalid = ((c[None, :] >= c0[:, None]) & (c[None, :] < c0[:, None] + 16))
    cv = colvalid.T.astype(np.float32)
    dmask = np.zeros((2, 2, 64, 16, 64), np.float32)
    for a_ in range(2):
        for jj in range(16):
            jr = jj - a_
            if 0 <= jr <= 14:
                dmask[0, a_, :, jj, :] = cv
            if 4 <= jr <= 11:
                dmask[1, a_, :, jj, :] = cv
    dcol = dmask.reshape(2, 128, 1024)
    return tables, perm, bm, dcol.astype(np.float32)


def d_row_valid(r, kr):
    r0 = min(max(r - 4, 0), 24)
    return r0 <= kr < r0 + 8


def build_program(depth=4, dbg_stop=None):
    nc = bass.Bass("TRN2", target_bir_lowering=False)
    fw = FW(nc)

    def din(name, shape):
        return nc.dram_tensor(name, list(shape), F32, kind="ExternalInput").ap()

    xT_d = din("xT", [8, 128, S])
    memT_d = din("memT", [8, 128, 256])
    w_in_ab = din("w_in_ab", [2, D, 1536]); w_out_ab = din("w_out_ab", [2, D, D])
    w_in_cd = din("w_in_cd", [2, D, 1952]); w_out_cd = din("w_out_cd", [2, D, D])
    w_uq = din("w_uq", [2, 256, 768]); w_ukv = din("w_ukv", [2, 128, 1024])
    w_xq = din("w_xq", [4, D, 512]); w_xkv = din("w_xkv", [4, D, D]); w_xo = din("w_xo", [4, 512, D])
    w_gu = din("w_gate_up", [4, D, 2 * DFF]); w_dn = din("w_down", [4, DFF, D])
    NG = 4 * 32 + 8 + 2 * 4 + 2 * 3
    gcols_d = din("gcols", [128, NG])
    sink_d = din("sinkb", [128, 16])
    tables_d = din("tables", [6, 128, S])
    perm_d = din("perm", [4, 128, 128])
    ident_d = din("ident", [128, 128])
    bmask_d = din("bmask", [6, 128, 512])
    dmask_d = din("dmask", [2, 128, 1024])
    dneg_d = din("dneg", [2, 128, 1024])
    dbias_d = din("dbias", [2, 8, 128, 1024])
    out_d = nc.dram_tensor("outT", [8, 128, S], F32, kind="ExternalOutput").ap()
    dbg_d = nc.dram_tensor("dbgT", [8, 128, S], F32, kind="ExternalOutput").ap() if dbg_stop is not None else None

    xT_h = nc.alloc_sbuf_tensor("xT_sb", [128, 8, S], F32)
    xT = [[Tile(xT_h[:, c, 512 * t:512 * (t + 1)], f"x{c}_{t}") for t in range(4)] for c in range(8)]
    perm_h = nc.alloc_sbuf_tensor("perm_sb", [128, 4, 128], BF16)
    permT = Tile(perm_h, "perm")
    gcols_h = nc.alloc_sbuf_tensor("gcols_sb", [128, NG], F32)
    gcolsT = Tile(gcols_h, "gcols")
    sink_h = nc.alloc_sbuf_tensor("sink_sb", [128, 16], F32)
    sinkT = Tile(sink_h, "sink")
    ARENA = nc.sbuf_bytes_remaining - 64
    ARENA -= ARENA % 64
    R = nc.alloc_sbuf_tensor("arena", [128, ARENA // 2], BF16)

    class Carver:
        def __init__(self):
            self.off = 0

        def bf(self, n):
            o = self.off
            self.off += (2 * n + 63) // 64 * 64
            assert self.off <= ARENA, f"arena overflow {self.off} > {ARENA}"
            return R[:, o // 2:o // 2 + n]

        def f32(self, n):
            return self.bf(2 * n).bitcast(F32)

    psb = [nc.alloc_psum_tensor(f"psb{i}", [128, 1024], F32) for i in range(4)]
    bank = []
    for i in range(4):
        bank.append(Tile(psb[i][:, 0:512], f"bank{2 * i}", excl=True))
        bank.append(Tile(psb[i][:, 512:1024], f"bank{2 * i + 1}", excl=True))

    s_x = [fw.dmasem(f"dx{t}") for t in range(4)]
    s_c = fw.dmasem("dconst")
    s_out = fw.dmasem("dout")

    for t in range(4):
        for c in range(8):
            tk = fw.dma("sp", xT[c][t][:, :], xT_d[c, :, 512 * t:512 * (t + 1)], s_x[t], writes=[xT[c][t]])
        for c in range(8):
            xT[c][t].last_w = tk
    fw.dma("pool", permT[:, :, :], perm_d.rearrange("a p m -> p a m"), s_c, writes=[permT])
    fw.dma("sp", gcolsT[:, :], gcols_d, fw.dmasem("dconst2"), writes=[gcolsT])
    fw.dma("sp", sinkT[:, :], sink_d, fw.dmasem("dconst3"), writes=[sinkT])
    s_wq = [fw.dmasem(f"wq_{i}") for i in range(2)]
    s_wk = fw.dmasem("wk"); s_wv = fw.dmasem("wv"); s_tab = [fw.dmasem("tab0"), fw.dmasem("tab1")]; s_misc = fw.dmasem("misc"); s_misc2 = fw.dmasem("misc2"); s_misc3 = fw.dmasem("misc3"); s_misc4 = fw.dmasem("misc4"); s_misc5 = fw.dmasem("misc5"); s_misc6 = fw.dmasem("misc6")
    s_mem = fw.dmasem("mem")
    s_wg = [fw.dmasem(f"wg_{i}") for i in range(4)]
    s_wd = [fw.dmasem(f"wd_{i}") for i in range(3)]
    eps_h = nc.alloc_sbuf_tensor("eps_sb", [128, 2], F32)
    epsT = Tile(eps_h, "eps")
    fw.emit("dve", lambda e: e.memset(epsT[:, 0:1], EPS), writes=[epsT])
    fw.emit("dve", lambda e: e.memset(epsT[:, 1:2], 64.0 * EPS), writes=[epsT])

    def rsqrt_ps(b, b_ap, o, o_ap, scale, which, rows):
        fw.emit("act", lambda e: e.activation(out=o_ap, in_=b_ap, func=AF.Ln, bias=epsT[rows, which:which + 1], scale=scale),
                reads=[b, epsT], writes=[o])
        fw.emit("act", lambda e: e.activation(out=o_ap, in_=o_ap, func=AF.Exp, scale=-0.5), reads=[o], writes=[o])

    fw.emit("act", lambda e: e.activation(out=sinkT[:, :], in_=sinkT[:, :], func=AF.Exp), reads=[sinkT], writes=[sinkT])

    swap64 = lambda r: permT[r, 0, r]
    swapC = lambda r: permT[r, 1, r]

    def gcol(i, rows=slice(0, 128)):
        return gcolsT[rows, i:i + 1]

    G_MIX, G_XQ, G_MEM, G_FFN = 0, 8, 16, 24
    G_FINAL = 128
    G_QA = lambda j: 136 + 4 * j
    G_CQ = lambda j: 144 + 3 * j

    class BankRR:
        def __init__(self, ids):
            self.ids = ids
            self.i = 0

        def get(self, exclude=()):
            while True:
                b = bank[self.ids[self.i % len(self.ids)]]
                self.i += 1
                if not any(b is x for x in exclude):
                    return b

    def mm_group(out_ap, out_tile, pairs, reads):
        n = len(pairs)
        for i, (l, r) in enumerate(pairs):
            fw.emit("pe", lambda e, l=l, r=r, i=i: e.matmul(out_ap, lhsT=l, rhs=r, start=(i == 0), stop=(i == n - 1)),
                    reads=reads, writes=[out_tile], inc=(i == n - 1))

    def load_w(dst_tile, dst_ap, src_ap, sem):
        return fw.dma("pool", dst_ap, src_ap, sem, writes=[dst_tile])

    def rmsnorm_T(src, ncols, gbase, dst, dst_tiles, nchunks, brr, tmp, inv_n, ones_ap, rows=slice(0, 128), src_tiles=None):
        fast = len(tmp) == 3
        sq_t = tmp[0]
        rs_l = tmp[1] if fast else [tmp[1]]
        xr_l = tmp[2] if fast else None
        ntt = ncols // 512 if ncols >= 512 else 1
        w = min(512, ncols)
        n_act_sq = 5 if fast else (nchunks + 1) // 2

        def stats(t):
            cs = slice(w * t, w * (t + 1))
            b = brr.get()
            rs_t = rs_l[t % len(rs_l)]
            for c in range(nchunks):
                st = src_tiles(c, t)
                on_act = (c % 2 == 0) if not fast else (c not in (2, 6))
                sqb = sq_t[c % len(sq_t)]
                if on_act:
                    fw.emit("act", lambda e, c=c, cs=cs, sqb=sqb: e.activation(out=sqb[rows, 0:w], in_=src(c, cs), func=AF.Square),
                            reads=[st], writes=[sqb])
                else:
                    fw.emit("pool", lambda e, c=c, cs=cs, sqb=sqb: e.tensor_tensor(out=sqb[rows, 0:w], in0=src(c, cs), in1=src(c, cs), op=ALU.mult),
                            reads=[st], writes=[sqb])
                fw.emit("pe", lambda e, c=c, b=b, sqb=sqb: e.matmul(b[rows, 0:w], lhsT=ones_ap, rhs=sqb[rows, 0:w], start=(c == 0), stop=(c == nchunks - 1)),
                        reads=[sqb, permT], writes=[b], inc=True)
            if not fast:
                rsqrt_ps(b, b[rows, 0:w], rs_t, rs_t[rows, 0:w], inv_n, 0, rows)
            return b

        def rsq(t, b):
            rs_t = rs_l[t % len(rs_l)]
            rsqrt_ps(b, b[rows, 0:w], rs_t, rs_t[rows, 0:w], inv_n, 0, rows)

        def apply(t):
            cs = slice(w * t, w * (t + 1))
            rs_t = rs_l[t % len(rs_l)]
            for c in range(nchunks):
                st = src_tiles(c, t)
                if fast and c in (3, 6):
                    xr = xr_l[0 if c == 3 else 1]
                    fw.emit("pool", lambda e, c=c, cs=cs, xr=xr, rs_t=rs_t: e.tensor_tensor(out=xr[rows, 0:w], in0=src(c, cs), in1=rs_t[rows, 0:w], op=ALU.mult),
                            reads=[st, rs_t], writes=[xr])
                    fw.emit("act", lambda e, c=c, cs=cs, xr=xr: e.activation(out=dst(c, cs), in_=xr[rows, 0:w], func=AF.Copy, scale=gcol(gbase + c, rows)),
                            reads=[xr, gcolsT], writes=[dst_tiles(c, t)])
                else:
                    fw.emit("dve", lambda e, c=c, cs=cs, rs_t=rs_t: e.scalar_tensor_tensor(out=dst(c, cs), in0=src(c, cs), scalar=gcol(gbase + c, rows), in1=rs_t[rows, 0:w], op0=ALU.mult, op1=ALU.mult),
                            reads=[st, rs_t, gcolsT], writes=[dst_tiles(c, t)])

        if fast:
            banks_ = {}
            for t in range(ntt + 2):
                if t < ntt:
                    banks_[t] = stats(t)
                if 1 <= t <= ntt:
                    rsq(t - 1, banks_[t - 1])
                if t >= 2:
                    apply(t - 2)
        else:
            for t in range(ntt):
                stats(t)
                apply(t)

    for L in range(depth):
        even = (L % 2 == 0)
        j = L // 2
        gb = 32 * L
        if L > 0:
            fw.barrier()
        cv = Carver()
        hT_ap = cv.bf(8 * S).rearrange("p (c s) -> p c s", s=S)
        hT = [[Tile(hT_ap[:, c, 512 * t:512 * (t + 1)], f"h{c}_{t}") for t in range(4)] for c in range(8)]
        OT_ap = cv.bf(8 * S).rearrange("p (c s) -> p c s", s=S)
        OT = [[Tile(OT_ap[:, c, 512 * t:512 * (t + 1)], f"o{c}_{t}") for t in range(4)] for c in range(8)]
        cosT = Tile(cv.bf(S), "cos"); sinT = Tile(cv.bf(S), "sin")
        Qap = cv.bf(S); Kap = cv.bf(S)
        Qt = [Tile(Qap[:, 512 * t:512 * (t + 1)], f"q{t}") for t in range(4)]
        Kt = [Tile(Kap[:, 512 * t:512 * (t + 1)], f"k{t}") for t in range(4)]
        Vap = cv.bf(16 * 2 * 192).rearrange("p (t s d) -> p t s d", s=2, d=192)
        Vt = [Tile(Vap[:, 4 * g:4 * (g + 1), :, :], f"v{g}") for g in range(4)]
        Pt = [Tile(cv.bf(1024), f"P{i}") for i in range(2)]
        NB_ROPE = 3 if even else 1
        SKEW = 2 if even else 1
        live_banks = []
        qsb_l = [Tile(cv.bf(512), f"qsb{i}") for i in range(3 if even else 2)]
        sq = [Tile(cv.bf(512), f"sq{i}") for i in range(3 if even else 2)]
        t1_l = [Tile(cv.f32(512), f"t1_{i}") for i in range(NB_ROPE)]
        t2_l = [Tile(cv.f32(512), f"t2_{i}") for i in range(NB_ROPE)]
        spt_l = [Tile(cv.f32(512), f"sp{i}") for i in range(NB_ROPE)] if even else None
        t1 = t1_l[0]; t2 = t2_l[0]
        rope_ctr = [0]
        rec_l = [Tile(cv.f32(512), "rec0"), Tile(cv.f32(512), "rec1")]
        rec = rec_l[0]
        rs_t = Tile(cv.f32(512), "rs")
        if not even:
            spt_l = [rs_t]
        ntmp = ([Tile(Qap[:, 512 * i:512 * (i + 1)], f"nsq{i}") for i in range(4)],
                [Tile(Kap[:, 1024 * i:1024 * (i + 1)].bitcast(F32), f"nrs{i}") for i in range(2)],
                [Tile(Vap.rearrange("p t s d -> p (t s d)")[:, 1024 * i:1024 * (i + 1)].bitcast(F32), f"nxr{i}") for i in range(2)])
        wq = [Tile(cv.bf(8 * 128).rearrange("p (k m) -> p k m", m=128), f"wq{i}") for i in range(2)]
        wk = Tile(cv.bf(8 * 128).rearrange("p (k m) -> p k m", m=128), "wk")
        wv = Tile(cv.bf(8 * 128).rearrange("p (k m) -> p k m", m=128), "wv")
        if even:
            identT = Tile(cv.bf(128), "ident")
            fw.dma("pool", identT[:, :], ident_d, s_misc6, writes=[identT])
            bmask = Tile(cv.bf(6 * 512).rearrange("p (a f) -> p a f", f=512), "bmask")
            fw.dma("pool", bmask[:, :, :], bmask_d.rearrange("a p f -> p a f"), s_misc, writes=[bmask])
        else:
            cqn_ap = cv.bf(2 * S).rearrange("p (c s) -> p c s", s=S)
            cqn = [[Tile(cqn_ap[:, c, 512 * t:512 * (t + 1)], f"cq{c}_{t}") for t in range(4)] for c in range(2)]
            Pt.append(Tile(cqn_ap[:, 0, 0:1024], "P2alias"))
            ckvn_ap = cv.bf(S)
            ckvn = [Tile(ckvn_ap[:, 512 * t:512 * (t + 1)], f"ckv{t}") for t in range(4)]
            kr_ap = cv.bf(S)
            krT = [Tile(kr_ap[:, 512 * t:512 * (t + 1)], f"kr{t}") for t in range(4)]
            dEf = Tile(cv.bf(2 * 1024).rearrange("p (h f) -> p h f", f=1024), "dEf")
            dEi = Tile(cv.bf(2 * 1024).rearrange("p (h f) -> p h f", f=1024), "dEi")
        brr = BankRR([0, 1, 2, 3, 4, 5, 6, 7])
        ones_all = permT[:, 3, :]
        blockones = permT[:, 2, :]

        rmsnorm_T(lambda c, cs: xT_h[:, c, cs], S, gb + G_MIX, lambda c, cs: hT_ap[:, c, cs], lambda c, t: hT[c][t],
                  8, brr, ntmp, 1.0 / D, ones_all, src_tiles=lambda c, t: xT[c][t])
        fw.barrier()

        def proj_bank(b, M, wt, wcols, src_ap, src_tiles, KC, t, krows=slice(0, 128)):
            pairs = [(wt[krows, kc, wcols], src_ap(kc, slice(512 * t, 512 * (t + 1)))) for kc in range(KC)]
            mm_group(b[0:M, :], b, pairs, [wt] + [src_tiles(kc, t) for kc in range(KC)])

        def load_tables(i):
            fw.dma("pool", cosT[:, :], tables_d[2 * i], s_tab[0], writes=[cosT])
            fw.dma("pool", sinT[:, :], tables_d[2 * i + 1], s_tab[1], writes=[sinT])

        def run_chains(chains):
            n = len(chains)
            ctxs = [None] * n
            fins = [None] * n
            for i in range(n + SKEW + 1):
                jf = i - SKEW - 1
                if jf >= 0 and fins[jf] is not None:
                    fins[jf]()
                if i < n:
                    ctxs[i] = chains[i][0]()
                if SKEW <= i < n + SKEW:
                    fins[i - SKEW] = chains[i - SKEW][1](ctxs[i - SKEW])

        def mk_chain(wt, M, wcols, src_ap, src_tiles, KC, t, rows, dst_t, mode, swap_ap=None, gains=None, norm=False, after=None):
            def A():
                b = brr.get(exclude=live_banks)
                live_banks.append(b)
                proj_bank(b, M, wt, wcols, src_ap, src_tiles, KC, t)
                if mode == "plain":
                    return (b, None, 0)
                qi = rope_ctr[0] % len(qsb_l)
                ri = rope_ctr[0] % NB_ROPE
                rope_ctr[0] += 1
                qsb = qsb_l[qi]
                fw.emit("act", lambda e: e.activation(out=qsb[rows, :], in_=b[rows, :], func=AF.Copy), reads=[b], writes=[qsb])
                return (b, qsb, ri)

            def B(ctx):
                b, qsb, ri = ctx
                if mode == "plain":
                    fw.emit("dve", lambda e: e.tensor_copy(out=dst_t[t][rows, :], in_=b[rows, :]), reads=[b], writes=[dst_t[t]])
                    fin = None
                else:
                    fin = rope_B(qsb, ri, dst_t, t, rows, swap_ap, gains, norm, b)
                live_banks.remove(b)
                if after is not None:
                    after()
                return fin
            return (A, B)

        def rope_B(qsb, ri, dst_t, t, rows, swap_ap, gains=None, norm=False, b=None):
            tok = slice(512 * t, 512 * (t + 1))
            t1 = t1_l[ri]; t2 = t2_l[ri]; spt = spt_l[ri]; sqr = sq[ri % len(sq)]
            b2 = brr.get(exclude=live_banks)
            fw.emit("pe", lambda e: e.matmul(b2[rows, :], lhsT=swap_ap, rhs=qsb[rows, :], start=True, stop=True),
                    reads=[qsb, permT], writes=[b2])
            if norm:
                fw.emit("pool", lambda e: e.tensor_tensor(out=sqr[rows, :], in0=qsb[rows, :], in1=qsb[rows, :], op=ALU.mult),
                        reads=[qsb], writes=[sqr])
                b3 = brr.get(exclude=live_banks + [b2])
                fw.emit("pe", lambda e: e.matmul(b3[rows, :], lhsT=blockones, rhs=sqr[rows, :], start=True, stop=True),
                        reads=[sqr, permT], writes=[b3])
                live_banks.append(b3)
            if gains is not None:
                g0, g1 = gains
                fw.emit("dve", lambda e: e.scalar_tensor_tensor(out=t1[rows, :], in0=b[rows, :], scalar=gcol(g0, rows), in1=cosT[rows, tok], op0=ALU.mult, op1=ALU.mult),
                        reads=[b, cosT, gcolsT], writes=[t1])
                fw.emit("dve", lambda e: e.scalar_tensor_tensor(out=t2[rows, :], in0=b2[rows, :], scalar=gcol(g1, rows), in1=sinT[rows, tok], op0=ALU.mult, op1=ALU.mult),
                        reads=[b2, sinT, gcolsT], writes=[t2])
            else:
                fw.emit("dve", lambda e: e.tensor_tensor(out=t1[rows, :], in0=qsb[rows, :], in1=cosT[rows, tok], op=ALU.mult),
                        reads=[qsb, cosT], writes=[t1])
                fw.emit("dve", lambda e: e.tensor_tensor(out=t2[rows, :], in0=b2[rows, :], in1=sinT[rows, tok], op=ALU.mult),
                        reads=[b2, sinT], writes=[t2])
            if norm:
                fw.emit("dve", lambda e: e.tensor_tensor(out=t1[rows, :], in0=t1[rows, :], in1=t2[rows, :], op=ALU.add),
                        reads=[t1, t2], writes=[t1])

                def fin():
                    rsqrt_ps(b3, b3[rows, :], spt, spt[rows, :], 1.0, 1, rows)
                    live_banks.remove(b3)
                    fw.emit("dve", lambda e: e.tensor_tensor(out=dst_t[t][rows, :], in0=t1[rows, :], in1=spt[rows, :], op=ALU.mult),
                            reads=[t1, spt], writes=[dst_t[t]])
                return fin
            else:
                fw.emit("pool", lambda e: e.tensor_tensor(out=dst_t[t][rows, :], in0=t1[rows, :], in1=t2[rows, :], op=ALU.add),
                        reads=[t1, t2], writes=[dst_t[t]])

        def plain_evac(b, dst_t, t, rows):
            fw.emit("dve", lambda e: e.tensor_copy(out=dst_t[t][rows, :], in_=b[rows, :]), reads=[b], writes=[dst_t[t]])

        def init_v_ones():
            for g in range(4):
                fw.emit("pool", lambda e, g=g: e.memset(Vt[g][:, :, :, :], 1.0), writes=[Vt[g]])

        def v_proj(wt, ncols, src_ap, src_tiles, KC, nslots):
            for g in range(4):
                b = brr.get()
                for q in range(4):
                    tt = 4 * g + q
                    pairs = [(src_ap(kc, slice(128 * tt, 128 * (tt + 1))), wt[:, kc, 0:ncols]) for kc in range(KC)]
                    mm_group(b[:, 128 * q:128 * q + ncols], b, pairs, [wt] + [src_tiles(kc, tt // 4) for kc in range(KC)])
                fw.emit("dve", lambda e, g=g, b=b: e.tensor_copy(
                    out=Vt[g][:, :, 0:nslots, 64:128],
                    in_=b[:, :].rearrange("p (q s d) -> p q s d", q=4, s=2)[:, :, 0:nslots, :]),
                    reads=[b], writes=[Vt[g]])

        def attention(slots, items_for_T, scale, chunk, addcols, vslot, post_exp=None, lookahead=1, act_recip=False, nS=2, o_single=False, add_mm=None, act_recip_last=False):
            work = []
            for T in range(4):
                its = items_for_T(T)
                for ii, it in enumerate(its):
                    work.append((T, it, ii == 0, ii == len(its) - 1))
            started = {}

            def crange(it):
                cr = it[0][3] if len(it[0]) > 3 and it[0][3] is not None else (0, 512)
                return cr

            def emit_qk(w, idx):
                T, it, first, last = w
                c0, c1 = crange(it)
                for bi, sub in enumerate(it):
                    sl, kt = sub[0], sub[1]
                    b = bank[2 * (idx % nS) + bi]
                    qt, qr = slots[sl]["q"]; ktl, kr = slots[sl]["k"]
                    am = add_mm(T, sub, c0, c1) if add_mm is not None else None
                    fw.emit("pe", lambda e, b=b, qt=qt, qr=qr, ktl=ktl, kr=kr, kt=kt, T=T, c0=c0, c1=c1, am=am: e.matmul(
                        b[:, 0:c1 - c0], lhsT=ktl[kt // 4][kr, 128 * (kt % 4):128 * (kt % 4 + 1)], rhs=qt[T][qr, c0:c1], start=True, stop=(am is None)),
                        reads=[ktl[kt // 4], qt[T]], writes=[b], inc=(am is None))
                    if am is not None:
                        aap, atiles = am
                        fw.emit("pe", lambda e, b=b, aap=aap, c0=c0, c1=c1, idt=identT: e.matmul(
                            b[:, 0:c1 - c0], lhsT=idt[:, :], rhs=aap, start=False, stop=True),
                            reads=[identT] + atiles, writes=[b])

            def emit_rest(w, idx):
                T, it, first, last = w
                n = len(it)
                c0, c1 = crange(it)
                nc_ = c1 - c0
                bs = [bank[2 * (idx % nS) + bi] for bi in range(n)]
                P = Pt[idx % nS]
                src = psb[idx % nS]
                if nc_ == 512:
                    fw.emit("act", lambda e: e.activation(out=P[:, 0:512 * n], in_=src[:, 0:512 * n], func=AF.Exp, scale=scale),
                            reads=bs, writes=[P])
                else:
                    fw.emit("act", lambda e: e.activation(out=P[:, :].rearrange("p (b f) -> p b f", b=2)[:, 0:n, 0:nc_],
                                                          in_=src[:, :].rearrange("p (b f) -> p b f", b=2)[:, 0:n, 0:nc_], func=AF.Exp, scale=scale),
                            reads=bs, writes=[P])
                if post_exp is not None:
                    post_exp(P, T, it, c0, c1)
                for bi, sub in enumerate(it):
                    sl, kt = sub[0], sub[1]
                    ob = bank[6 + sl] if o_single else bank[4 + 2 * (T % 2) + sl]
                    st = not started.get((sl, T), False)
                    started[(sl, T)] = True
                    is_last = last and all(s2[0] != sl for s2 in it[bi + 1:])
                    vs = vslot(sl)
                    lo = slots[sl]["lo"]
                    vcols = slice(64, 192) if lo else slice(0, 128)
                    fw.emit("pe", lambda e, ob=ob, kt=kt, vs=vs, vcols=vcols, bi=bi, st=st, is_last=is_last, c0=c0, c1=c1, nc_=nc_: e.matmul(
                        ob[:, c0:c1], lhsT=Vt[kt // 4][:, kt % 4, vs, vcols], rhs=P[:, 512 * bi:512 * bi + nc_], start=st, stop=is_last,
                        skip_group_check=(add_mm is not None)),
                        reads=[Vt[kt // 4], P], writes=[ob], inc=True)
                if last:
                    for sl in sorted(set(s2[0] for s2 in it)):
                        ob = bank[6 + sl] if o_single else bank[4 + 2 * (T % 2) + sl]
                        rec = rec_l[(T + sl) % 2]
                        lo = slots[sl]["lo"]
                        orow = slice(0, 64) if lo else slice(64, 128)
                        drow = slice(64, 128) if lo else slice(0, 64)
                        ac = addcols(sl)
                        if act_recip or (act_recip_last and T == 3):
                            if ac is None:
                                fw.emit("act", lambda e, ob=ob, drow=drow, rec=rec: e.activation(out=rec[drow, :], in_=ob[drow, :], func=AF.Ln), reads=[ob], writes=[rec])
                            else:
                                fw.emit("act", lambda e, ob=ob, drow=drow, rec=rec, ac=ac: e.activation(out=rec[drow, :], in_=ob[drow, :], func=AF.Ln, bias=ac[drow, :]), reads=[ob, sinkT], writes=[rec])
                            fw.emit("act", lambda e, drow=drow, rec=rec: e.activation(out=rec[drow, :], in_=rec[drow, :], func=AF.Exp, scale=-1.0), reads=[rec], writes=[rec])
                        elif ac is None:
                            fw.emit("dve", lambda e, ob=ob, drow=drow, rec=rec: e.reciprocal(out=rec[drow, :], in_=ob[drow, :]), reads=[ob], writes=[rec])
                        else:
                            fw.emit("dve", lambda e, ob=ob, drow=drow, ac=ac, rec=rec: e.tensor_scalar(out=rec[drow, :], in0=ob[drow, :], scalar1=ac[drow, :], scalar2=None, op0=ALU.add),
                                    reads=[ob, sinkT], writes=[rec])
                            fw.emit("dve", lambda e, drow=drow, rec=rec: e.reciprocal(out=rec[drow, :], in_=rec[drow, :]), reads=[rec], writes=[rec])
                        fw.emit("dve", lambda e, ob=ob, orow=orow, drow=drow, T=T, rec=rec: e.tensor_tensor(out=OT[chunk][T][orow, :], in0=ob[orow, :], in1=rec[drow, :], op=ALU.mult),
                                reads=[ob, rec], writes=[OT[chunk][T]])

            n = len(work)
            for i in range(n + lookahead):
                if i < n:
                    emit_qk(work[i], i)
                if i - lookahead >= 0:
                    emit_rest(work[i - lookahead], i - lookahead)

        hsrc = lambda kc, cs: hT_ap[:, kc, cs]
        hsrc_t = lambda kc, t: hT[kc][t]
        init_v_ones()

        if even:
            W = w_in_ab[j].rearrange("(kc p) m -> p kc m", p=128)
            def loads_even(ui):
                mixer, u = divmod(ui, 4)
                g = u // 2
                qbase = 0 if mixer == 0 else 768
                kbase = 512 if mixer == 0 else 1280
                vbase = 640 if mixer == 0 else 1408
                wqt = wq[u % 2]
                load_w(wqt, wqt[:, :, :], W[:, :, qbase + 128 * u:qbase + 128 * (u + 1)], s_wq[u % 2])
                if u % 2 == 0:
                    load_w(wk, wk[:, :, 0:64], W[:, :, kbase + 64 * g:kbase + 64 * (g + 1)], s_wk)
                    load_w(wk, wk[:, :, 64:128], W[:, :, kbase + 64 * g:kbase + 64 * (g + 1)], s_wk)
                    load_w(wv, wv[:, :, 0:64], W[:, :, vbase + 64 * g:vbase + 64 * (g + 1)], s_wv)
                    load_w(wv, wv[:, :, 64:128], W[:, :, vbase + 64 * g:vbase + 64 * (g + 1)], s_wv)

            loads_even(0)
            for mixer in range(2):
                load_tables(mixer)
                for u in range(4):
                    g = u // 2
                    wqt = wq[u % 2]
                    newkv = (u % 2 == 0)
                    rows = slice(0, 128)
                    chains = []
                    for t in range(4):
                        if mixer == 0:
                            chains.append(mk_chain(wqt, 128, slice(0, 128), hsrc, hsrc_t, 8, t, rows, Qt, "rope", swap64(rows), (G_QA(j), G_QA(j) + 1), True))
                            if newkv:
                                chains.append(mk_chain(wk, 128, slice(0, 128), hsrc, hsrc_t, 8, t, rows, Kt, "rope", swap64(rows), (G_QA(j) + 2, G_QA(j) + 3), True))
                        else:
                            chains.append(mk_chain(wqt, 128, slice(0, 128), hsrc, hsrc_t, 8, t, rows, Qt, "rope", swap64(rows)))
                            if newkv:
                                chains.append(mk_chain(wk, 128, slice(0, 128), hsrc, hsrc_t, 8, t, rows, Kt, "rope", swap64(rows)))
                    run_chains(chains)
                    if newkv:
                        v_proj(wv, 128, hsrc, hsrc_t, 8, 1)
                    if 4 * mixer + u + 1 < 8:
                        loads_even(4 * mixer + u + 1)
                    slots = [dict(q=(Qt, slice(0, 64)), k=(Kt, slice(0, 64)), lo=True),
                             dict(q=(Qt, slice(64, 128)), k=(Kt, slice(64, 128)), lo=False)]
                    if mixer == 0:
                        items = lambda T: [[(0, kt, None), (1, kt, None)] for kt in range(16)]
                        attention(slots, items, 8.0, u, lambda sl: None, lambda sl: 0)
                    else:
                        def items(T):
                            out_ = []
                            for kt in range(max(0, 4 * T - 1), min(16, 4 * T + 5)):
                                rel = kt - 4 * T
                                cr = (max(0, 128 * rel - 128), min(512, 128 * rel + 256))
                                out_.append([(0, kt, rel + 1, cr), (1, kt, rel + 1, cr)])
                            return out_

                        def addm(T, sub, c0, c1):
                            return (bmask[:, sub[2], c0:c1], [bmask])

                        sc0 = 8 * j
                        attention(slots, items, 0.125, 4 + u, lambda sl, u=u: sinkT[:, sc0 + 2 * u + sl:sc0 + 2 * u + sl + 1], lambda sl: 0, act_recip=False, add_mm=addm, act_recip_last=True)
            Wout = w_out_ab[j]
        else:
            W = w_in_cd[j].rearrange("(kc p) m -> p kc m", p=128)
            Wuq = w_uq[j].rearrange("(kc p) m -> p kc m", p=128)
            Wukv = w_ukv[j].rearrange("(kc p) m -> p kc m", p=128)
            load_tables(2)
            latc = [t1, t2]
            wl0, wl1 = wq[0], wq[1]
            load_w(wl0, wl0[:, :, :], W[:, :, 0:128], s_wq[0])
            load_w(wl1, wl1[:, :, :], W[:, :, 128:256], s_wq[1])
            for t in range(4):
                for c, wl in enumerate((wl0, wl1)):
                    b = brr.get()
                    proj_bank(b, 128, wl, slice(0, 128), hsrc, hsrc_t, 8, t)
                    fw.emit("dve", lambda e, b=b, c=c: e.tensor_copy(out=latc[c][:, :], in_=b[:, :]), reads=[b], writes=[latc[c]])
                bq = brr.get()
                for c in range(2):
                    fw.emit("pool", lambda e, c=c: e.tensor_tensor(out=sq[c][:, :], in0=latc[c][:, :], in1=latc[c][:, :], op=ALU.mult), reads=[latc[c]], writes=[sq[c]])
                    fw.emit("pe", lambda e, c=c, bq=bq: e.matmul(bq[:, :], lhsT=ones_all, rhs=sq[c][:, :], start=(c == 0), stop=(c == 1)), reads=[sq[c], permT], writes=[bq])
                rsqrt_ps(bq, bq[:, :], rs_t, rs_t[:, :], 1.0 / 256, 0, slice(0, 128))
                for c in range(2):
                    fw.emit("dve", lambda e, c=c, t=t: e.scalar_tensor_tensor(out=cqn[c][t][:, :], in0=latc[c][:, :], scalar=gcol(G_CQ(j) + c), in1=rs_t[:, :], op0=ALU.mult, op1=ALU.mult),
                            reads=[latc[c], rs_t, gcolsT], writes=[cqn[c][t]])
            load_w(wk, wk[:, :, :], W[:, :, 256:384], s_wk)
            for t in range(4):
                b = brr.get()
                proj_bank(b, 128, wk, slice(0, 128), hsrc, hsrc_t, 8, t)
                fw.emit("dve", lambda e, b=b: e.tensor_copy(out=t1[:, :], in_=b[:, :]), reads=[b], writes=[t1])
                bq = brr.get()
                fw.emit("pool", lambda e: e.tensor_tensor(out=sq[0][:, :], in0=t1[:, :], in1=t1[:, :], op=ALU.mult), reads=[t1], writes=[sq[0]])
                fw.emit("pe", lambda e, bq=bq: e.matmul(bq[:, :], lhsT=ones_all, rhs=sq[0][:, :], start=True, stop=True), reads=[sq[0], permT], writes=[bq])
                rsqrt_ps(bq, bq[:, :], rs_t, rs_t[:, :], 1.0 / 128, 0, slice(0, 128))
                fw.emit("dve", lambda e, t=t: e.scalar_tensor_tensor(out=ckvn[t][:, :], in0=t1[:, :], scalar=gcol(G_CQ(j) + 2), in1=rs_t[:, :], op0=ALU.mult, op1=ALU.mult),
                        reads=[t1, rs_t, gcolsT], writes=[ckvn[t]])
            load_w(wv, wv[:, :, :], W[:, :, 320:448], s_wv)
            r96 = slice(64, 96)
            run_chains([mk_chain(wv, 128, slice(0, 128), hsrc, hsrc_t, 8, t, slice(0, 128), krT, "rope", swapC(slice(0, 128))) for t in range(4)])

            cq_src = lambda kc, cs: cqn_ap[:, kc, cs]
            cq_src_t = lambda kc, t: cqn[kc][t]
            ckv_src = lambda kc, cs: ckvn_ap[:, cs]
            ckv_src_t = lambda kc, t: ckvn[t]
            r0_96 = slice(0, 96)
            def loads_c(h):
                wqt = wq[h % 2]
                load_w(wqt, wqt[:, 0:2, 0:96], Wuq[:, :, 96 * h:96 * (h + 1)], s_wq[h % 2])
                load_w(wk, wk[:, 0:1, 0:64], Wukv[:, :, 128 * h:128 * h + 64], s_wk)
                load_w(wv, wv[:, 0:1, 0:64], Wukv[:, :, 128 * h + 64:128 * h + 128], s_wv)

            loads_c(0)
            for h in range(8):
                wqt = wq[h % 2]
                chains = []
                for t in range(4):
                    chains.append(mk_chain(wqt, 96, slice(0, 96), cq_src, cq_src_t, 2, t, r0_96, Qt, "rope", swapC(r0_96)))
                    chains.append(mk_chain(wk, 64, slice(0, 64), ckv_src, ckv_src_t, 1, t, slice(0, 64), Kt, "plain",
                                           after=lambda t=t: fw.emit("act", lambda e, t=t: e.activation(out=Kt[t][r96, :], in_=krT[t][r96, :], func=AF.Copy), reads=[krT[t]], writes=[Kt[t]])))
                run_chains(chains)
                v_proj(wv, 64, ckv_src, ckv_src_t, 1, 1)
                if h + 1 < 8:
                    loads_c(h + 1)
                lo = (h % 2 == 0)
                slots = [dict(q=(Qt, r0_96), k=(Kt, r0_96), lo=lo)]
                items = lambda T: [[(0, 2 * i, None), (0, 2 * i + 1, None)] for i in range(8)]
                attention(slots, items, 96.0 ** -0.5, h // 2, lambda sl: None, lambda sl: 0)

            fw.barrier()
            identT = Tile(sq[0].ap[:, 0:128], "ident")
            fw.dma("pool", identT[:, :], ident_d, s_misc6, writes=[identT])
            def loads_d(u):
                wqt = wq[u % 2]
                load_w(wqt, wqt[:, :, :], W[:, :, 416 + 128 * u:416 + 128 * (u + 1)], s_wq[u % 2])
                load_w(wk, wk[:, :, :], W[:, :, 928 + 128 * u:928 + 128 * (u + 1)], s_wk)
                load_w(wv, wv[:, :, :], W[:, :, 1440 + 128 * u:1440 + 128 * (u + 1)], s_wv)

            loads_d(0)
            for u in range(4):
                wqt = wq[u % 2]
                stg = [t1_l[0], t2_l[0]]
                stv = [x[:, :].bitcast(BF16) for x in stg]
                fw.dma("pool", stv[0], dneg_d[0], s_misc4, writes=[stg[0]])
                fw.dma("pool", stv[1], dneg_d[1], s_misc5, writes=[stg[1]])
                for hh in range(2):
                    fw.dma("pool", dEi[:, hh, :], dbias_d[j, 2 * u + hh], s_misc, writes=[dEi])
                for hh in range(2):
                    fw.emit("dve", lambda e, hh=hh: e.scalar_tensor_tensor(out=dEf[:, hh, :], in0=dEi[:, hh, :], scalar=8.0, in1=stv[0], op0=ALU.mult, op1=ALU.add),
                            reads=[dEi, stg[0]], writes=[dEf])
                for hh in range(2):
                    fw.emit("dve", lambda e, hh=hh: e.scalar_tensor_tensor(out=dEi[:, hh, :], in0=dEi[:, hh, :], scalar=8.0, in1=stv[1], op0=ALU.mult, op1=ALU.add),
                            reads=[dEi, stg[1]], writes=[dEi])
                rows = slice(0, 128)
                chains = []
                for t in range(4):
                    chains.append(mk_chain(wqt, 128, slice(0, 128), hsrc, hsrc_t, 8, t, rows, Qt, "plain"))
                    chains.append(mk_chain(wk, 128, slice(0, 128), hsrc, hsrc_t, 8, t, rows, Kt, "plain"))
                run_chains(chains)
                v_proj(wv, 128, hsrc, hsrc_t, 8, 2)
                if u + 1 < 4:
                    loads_d(u + 1)
                slots = [dict(q=(Qt, slice(0, 64)), k=(Kt, slice(0, 64)), lo=True),
                         dict(q=(Qt, slice(64, 128)), k=(Kt, slice(64, 128)), lo=False)]

                def items(T):
                    lo_r = min(max(8 * T - 4, 0), 24)
                    hi_r = min(max(8 * T + 7 - 4, 0), 24) + 7
                    out_ = []
                    for kt in range(lo_r // 2, hi_r // 2 + 1):
                        vb = [bq for bq in range(8) if d_row_valid(8 * T + bq, 2 * kt) or d_row_valid(8 * T + bq, 2 * kt + 1)]
                        cr = (64 * min(vb), 64 * (max(vb) + 1))
                        out_.append([(0, kt, None, cr), (1, kt, None, cr)])
                    return out_

                def addm(T, sub, c0, c1):
                    sl, kt = sub[0], sub[1]
                    tab = dEi if T in (1, 2) else dEf
                    jj0 = 7 - 2 * kt + 8 * T + c0 // 64
                    assert 0 <= jj0 and jj0 + (c1 - c0) // 64 <= 16, (T, kt, jj0)
                    return (tab[:, sl, 64 * jj0:64 * jj0 + (c1 - c0)], [tab])

                def post(P, T, it, c0, c1):
                    if T not in (0, 3):
                        return
                    blo = c0 // 64
                    nb = (c1 - c0) // 64
                    for bi, sub in enumerate(it):
                        kt = sub[1]
                        base = 512 * bi
                        for a_ in range(2):
                            kr_ = 2 * kt + a_
                            pr = slice(64 * a_, 64 * (a_ + 1))
                            valid = [d_row_valid(8 * T + bq, kr_) for bq in range(blo, blo + nb)]
                            bq = 0
                            while bq < nb:
                                e0 = bq
                                while bq < nb and valid[bq] == valid[e0]:
                                    bq += 1
                                if not valid[e0]:
                                    cs = slice(base + 64 * e0, base + 64 * bq)
                                    fw.emit("pool", lambda e, cs=cs, pr=pr: e.memset(P[pr, cs], 0.0), writes=[P])

                attention(slots, items, 0.125, 4 + u, lambda sl: None, lambda sl: sl, post_exp=post, act_recip=False, add_mm=addm, act_recip_last=True)
            Wout = w_out_cd[j]

        def out_proj(Wsrc, KC, src_ap, src_tiles, wbufs, wsems):
            Wr = Wsrc.rearrange("(kc p) m -> p kc m", p=128)
            for fo in range(8):
                wt = wbufs[fo % len(wbufs)]
                load_w(wt, wt[:, 0:KC, :], Wr[:, :, 128 * fo:128 * (fo + 1)], wsems[fo % len(wbufs)])
                for t in range(4):
                    b = brr.get()
                    proj_bank(b, 128, wt, slice(0, 128), src_ap, src_tiles, KC, t)
                    fw.emit("dve", lambda e, b=b, fo=fo, t=t: e.tensor_tensor(out=xT[fo][t][:, :], in0=xT[fo][t][:, :], in1=b[:, :], op=ALU.add),
                            reads=[b, xT[fo][t]], writes=[xT[fo][t]])

        out_proj(Wout, 8, lambda kc, cs: OT_ap[:, kc, cs], lambda kc, t: OT[kc][t],
                 [wq[0], wq[1], wk, wv], [s_wq[0], s_wq[1], s_wk, s_wv])
        if dbg_stop == (L, "mix"):
            s_dbg = fw.dmasem("dbg")
            for c in range(8):
                for t in range(4):
                    srcT = hT if os.environ.get("DBG_DUMP", "OT") == "hT" else OT
                    tk = fw.dma("pool", dbg_d[c, :, 512 * t:512 * (t + 1)], srcT[c][t][:, :], s_dbg, reads=[srcT[c][t]])
            fw.wait_tokens("pool", [tk])
            break

        brr = BankRR([0, 1, 2, 3, 4, 5])
        fw.barrier()
        rmsnorm_T(lambda c, cs: xT_h[:, c, cs], S, gb + G_XQ, lambda c, cs: hT_ap[:, c, cs], lambda c, t: hT[c][t],
                  8, brr, ntmp, 1.0 / D, ones_all, src_tiles=lambda c, t: xT[c][t])
        memf_ap = None
        memf = Tile(Vap.rearrange("p t s d -> p (t s d)")[:, 0:4096].bitcast(F32).rearrange("p (c s) -> p c s", s=256), "memf")
        memn = Tile(Vap.rearrange("p t s d -> p (t s d)")[:, 4096:6144].rearrange("p (c s) -> p c s", s=256), "memn")
        fw.barrier()
        fw.dma("sp", memf[:, :, :], memT_d.rearrange("c p s -> p c s"), s_mem, writes=[memf])
        rmsnorm_T(lambda c, cs: memf[:, c, cs], 256, gb + G_MEM, lambda c, cs: memn[:, c, cs], lambda c, t: memn,
                  8, brr, (sq, rs_t), 1.0 / D, ones_all, src_tiles=lambda c, t: memf)
        kx = Tile(Kap[:, 0:1024].rearrange("p (h s) -> p h s", s=256), "kx")
        vx = Tile(Kap[:, 1024:2048].rearrange("p (t f) -> p t f", f=512), "vx")
        Wkv = w_xkv[L].rearrange("(kc p) m -> p kc m", p=128)
        Wq = w_xq[L].rearrange("(kc p) m -> p kc m", p=128)
        fw.barrier()
        xw = [wq[0], wq[1], wk, wv]
        xs = [s_wq[0], s_wq[1], s_wk, s_wv]
        for h in range(4):
            wt = xw[h]
            load_w(wt, wt[:, :, :], Wkv[:, :, 128 * h:128 * (h + 1)], xs[h])
        for h in range(4):
            wt = xw[h]
            b = brr.get()
            pairs = [(wt[:, kc, :], memn[:, kc, :]) for kc in range(8)]
            mm_group(b[:, 0:256], b, pairs, [wt, memn])
            fw.emit("dve", lambda e, b=b, h=h: e.tensor_copy(out=kx[:, h, :], in_=b[:, 0:256]), reads=[b], writes=[kx])
        for hp in range(4):
            wt = xw[hp]
            load_w(wt, wt[:, :, :], Wkv[:, :, 512 + 128 * hp:512 + 128 * (hp + 1)], xs[hp])
        for hp in range(4):
            wt = xw[hp]
            for tt in range(2):
                b = brr.get()
                pairs = [(memn[:, kc, 128 * tt:128 * (tt + 1)], wt[:, kc, :]) for kc in range(8)]
                mm_group(b[:, 0:128], b, pairs, [wt, memn])
                fw.emit("dve", lambda e, b=b, hp=hp, tt=tt: e.tensor_copy(out=vx[:, tt, 128 * hp:128 * (hp + 1)], in_=b[:, 0:128]), reads=[b], writes=[vx])
        xscale = 128.0 ** -0.5
        for h in range(4):
            wt = xw[h]
            load_w(wt, wt[:, :, :], Wq[:, :, 128 * h:128 * (h + 1)], xs[h])
        for h in range(4):
            wt = xw[h]
            for t in range(4):
                b = brr.get()
                proj_bank(b, 128, wt, slice(0, 128), hsrc, hsrc_t, 8, t)
                plain_evac(b, OT[4 + h], t, slice(0, 128))
        xitems = [(h, t) for h in range(4) for t in range(4)]

        def x_qk(i):
            h, t = xitems[i]
            for kt in range(2):
                bb = bank[2 * (i % 2) + kt]
                fw.emit("pe", lambda e, bb=bb, kt=kt, t=t, h=h: e.matmul(bb[:, :], lhsT=kx[:, h, 128 * kt:128 * (kt + 1)], rhs=OT[4 + h][t][:, :], start=True, stop=True),
                        reads=[kx, OT[4 + h][t]], writes=[bb])

        def x_rest(i):
            h, t = xitems[i]
            p = i % 2
            P = Pt[p]
            fw.emit("act", lambda e, p=p, P=P: e.activation(out=P[:, :], in_=psb[p][:, :], func=AF.Exp, scale=xscale),
                    reads=[bank[2 * p], bank[2 * p + 1]], writes=[P])
            ob = bank[4 + 2 * p]; db = bank[5 + 2 * p]; rc = rec_l[p]
            for kt in range(2):
                fw.emit("pe", lambda e, kt=kt, P=P, h=h, ob=ob: e.matmul(ob[:, :], lhsT=vx[:, kt, 128 * h:128 * (h + 1)], rhs=P[:, 512 * kt:512 * (kt + 1)], start=(kt == 0), stop=(kt == 1)),
                        reads=[vx, P], writes=[ob])
            for kt in range(2):
                fw.emit("pe", lambda e, kt=kt, P=P, db=db: e.matmul(db[:, :], lhsT=ones_all, rhs=P[:, 512 * kt:512 * (kt + 1)], start=(kt == 0), stop=(kt == 1)),
                        reads=[permT, P], writes=[db])

        def x_fin(i):
            h, t = xitems[i]
            p = i % 2
            ob = bank[4 + 2 * p]; db = bank[5 + 2 * p]; rc = rec_l[p]
            fw.emit("act", lambda e, db=db, rc=rc: e.activation(out=rc[:, :], in_=db[:, :], func=AF.Ln), reads=[db], writes=[rc])
            fw.emit("act", lambda e, rc=rc: e.activation(out=rc[:, :], in_=rc[:, :], func=AF.Exp, scale=-1.0), reads=[rc], writes=[rc])
            fw.emit("dve", lambda e, h=h, t=t, ob=ob, rc=rc: e.tensor_tensor(out=OT[h][t][:, :], in0=ob[:, :], in1=rc[:, :], op=ALU.mult),
                    reads=[ob, rc], writes=[OT[h][t]])

        nx = len(xitems)
        for i in range(nx + 2):
            if i < nx:
                x_qk(i)
            if 1 <= i <= nx:
                x_rest(i - 1)
            if i >= 2:
                x_fin(i - 2)
        out_proj(w_xo[L], 4, lambda kc, cs: OT_ap[:, kc, cs], lambda kc, t: OT[kc][t],
                 [wq[0], wq[1], wk, wv], [s_wq[0], s_wq[1], s_wk, s_wv])
        if dbg_stop == (L, "xattn"):
            break

        fw.barrier()
        cv = Carver()
        hT_ap = cv.bf(8 * S).rearrange("p (c s) -> p c s", s=S)
        hT = [[Tile(hT_ap[:, c, 512 * t:512 * (t + 1)], f"fh{c}_{t}") for t in range(4)] for c in range(8)]
        act_ap = cv.bf(22 * 1024).rearrange("p (j s) -> p j s", s=1024)
        actT = [[Tile(act_ap[:, jj, 512 * t:512 * (t + 1)], f"a{jj}_{t}") for t in range(2)] for jj in range(22)]
        sq_f = [Tile(cv.bf(512), f"fsq{i}") for i in range(4)]
        rs_f = [Tile(cv.f32(512), "frs"), Tile(cv.f32(512), "frs1")]
        xr_f = [Tile(cv.f32(512), "fxr0"), Tile(cv.f32(512), "fxr1")]
        sg = [Tile(cv.f32(512), f"sg{i}") for i in range(2)]
        NWG, NWD = 4, 3
        wg = [Tile(cv.bf(8 * 256).rearrange("p (k m) -> p k m", m=256), f"wg{i}") for i in range(NWG)]
        wd = [Tile(cv.bf(22 * 128).rearrange("p (k m) -> p k m", m=128), f"wd{i}") for i in range(NWD)]
        brr = BankRR([0, 1, 2, 3, 4, 5, 6, 7])
        rmsnorm_T(lambda c, cs: xT_h[:, c, cs], S, gb + G_FFN, lambda c, cs: hT_ap[:, c, cs], lambda c, t: hT[c][t],
                  8, brr, (sq_f, rs_f, xr_f), 1.0 / D, ones_all, src_tiles=lambda c, t: xT[c][t])
        Wg = w_gu[L].rearrange("(kc p) m -> p kc m", p=128)
        Wd = w_dn[L].rearrange("(kc p) m -> p kc m", p=128)
        for half in range(2):
            for jj in range(22):
                wt = wg[jj % NWG]
                load_w(wt, wt[:, :, 0:128], Wg[:, :, 128 * jj:128 * (jj + 1)], s_wg[jj % NWG])
                load_w(wt, wt[:, :, 128:256], Wg[:, :, DFF + 128 * jj:DFF + 128 * (jj + 1)], s_wg[jj % NWG])
                for t2i in range(2):
                    t = 2 * half + t2i
                    bg = brr.get()
                    proj_bank(bg, 128, wt, slice(0, 128), lambda kc, cs: hT_ap[:, kc, cs], lambda kc, t: hT[kc][t], 8, t)
                    bu = brr.get()
                    proj_bank(bu, 128, wt, slice(128, 256), lambda kc, cs: hT_ap[:, kc, cs], lambda kc, t: hT[kc][t], 8, t)
                    sgt = sg[t2i]
                    fw.emit("act", lambda e, bg=bg, sgt=sgt: e.activation(out=sgt[:, :], in_=bg[:, :], func=AF.Silu), reads=[bg], writes=[sgt])
                    fw.emit("dve", lambda e, bu=bu, sgt=sgt, jj=jj, t2i=t2i: e.tensor_tensor(out=actT[jj][t2i][:, :], in0=bu[:, :], in1=sgt[:, :], op=ALU.mult),
                            reads=[bu, sgt], writes=[actT[jj][t2i]])
            for fo in range(8):
                wt = wd[fo % NWD]
                load_w(wt, wt[:, :, :], Wd[:, :, 128 * fo:128 * (fo + 1)], s_wd[fo % NWD])
                for t2i in range(2):
                    t = 2 * half + t2i
                    b = brr.get()
                    pairs = [(wt[:, jj, :], act_ap[:, jj, 512 * t2i:512 * (t2i + 1)]) for jj in range(22)]
                    mm_group(b[:, :], b, pairs, [wt] + [actT[jj][t2i] for jj in range(22)])
                    fw.emit("dve", lambda e, b=b, fo=fo, t=t: e.tensor_tensor(out=xT[fo][t][:, :], in0=xT[fo][t][:, :], in1=b[:, :], op=ALU.add),
                            reads=[b, xT[fo][t]], writes=[xT[fo][t]])
        if dbg_stop == (L, "ffn"):
            break

    fw.barrier()
    cv = Carver()
    sq_z = [Tile(cv.bf(512), "zsq0"), Tile(cv.bf(512), "zsq1")]
    rs_z = Tile(cv.f32(512), "zrs")
    ob_ap = cv.f32(8 * S).rearrange("p (c s) -> p c s", s=S)
    obT = [[Tile(ob_ap[:, c, 512 * t:512 * (t + 1)], f"ob{c}_{t}") for t in range(4)] for c in range(8)]
    brr = BankRR([0, 1, 2, 3, 4, 5, 6, 7])
    if dbg_stop is None:
        rmsnorm_T(lambda c, cs: xT_h[:, c, cs], S, G_FINAL, lambda c, cs: ob_ap[:, c, cs], lambda c, t: obT[c][t],
                  8, brr, (sq_z, rs_z), 1.0 / D, permT[:, 3, :], src_tiles=lambda c, t: xT[c][t])
        src_t = obT
    else:
        src_t = xT
    last = []
    for t in range(4):
        for c in range(8):
            last.append(fw.dma("sp", out_d[c, :, 512 * t:512 * (t + 1)], src_t[c][t][:, :], s_out, reads=[src_t[c][t]]))
    fw.wait_tokens("sp", [last[-1]])
    fw.build()
    return nc


def make_in_maps(inputs):
    f = lambda a: np.ascontiguousarray(np.asarray(a, dtype=np.float32))
    x = f(inputs["x"]); mem = f(inputs["mem"])
    tables, perm, bm, dcol = host_consts()
    NG = 4 * 32 + 8 + 2 * 4 + 2 * 3
    gcols = np.zeros((128, NG), np.float32)
    colz = lambda g: np.asarray(g, np.float32).reshape(-1, 128).T
    for L in range(4):
        gcols[:, 32 * L + 0:32 * L + 8] = colz(inputs["g_mix"][L])
        gcols[:, 32 * L + 8:32 * L + 16] = colz(inputs["g_xq"][L])
        gcols[:, 32 * L + 16:32 * L + 24] = colz(inputs["g_mem"][L])
        gcols[:, 32 * L + 24:32 * L + 32] = colz(inputs["g_ffn"][L])
    gcols[:, 128:136] = colz(inputs["g_final"])
    p = np.arange(128)
    for j in range(2):
        gq = np.asarray(inputs["g_qa"][j], np.float32); gk = np.asarray(inputs["g_ka"][j], np.float32)
        gcols[:, 136 + 4 * j + 0] = gq[p % 64]
        gcols[:, 136 + 4 * j + 1] = gq[(p % 64 + 32) % 64]
        gcols[:, 136 + 4 * j + 2] = gk[p % 64]
        gcols[:, 136 + 4 * j + 3] = gk[(p % 64 + 32) % 64]
        gcols[:, 144 + 3 * j:144 + 3 * j + 2] = colz(inputs["g_cq"][j])
        gcols[:, 144 + 3 * j + 2] = np.asarray(inputs["g_ckv"][j], np.float32)
    sink = np.zeros((128, 16), np.float32)
    for j in range(2):
        sink[:, 8 * j:8 * j + 8] = np.asarray(inputs["sink_b"][j], np.float32)[None, :]
    rpb = np.asarray(inputs["rpb_d"], np.float32)
    kc = np.arange(64)[:, None]; c = np.arange(64)[None, :]
    cidx = np.clip(kc - c + 15, 0, 30)
    g = rpb[:, :, ::-1, :][:, :, :, cidx]
    g = np.transpose(g, (0, 1, 3, 2, 4))
    dbias = np.zeros((2, 8, 2, 64, 16, 64), np.float32)
    dbias[:, :, 0, :, 0:15, :] = g
    dbias[:, :, 1, :, 1:16, :] = g
    dbias = np.ascontiguousarray(dbias.reshape(2, 8, 128, 1024))
    shared = dict(
        w_in_ab=f(inputs["w_in_ab"]), w_out_ab=f(inputs["w_out_ab"]), w_in_cd=f(inputs["w_in_cd"]),
        w_out_cd=f(inputs["w_out_cd"]), w_uq=f(inputs["w_uq"]), w_ukv=f(inputs["w_ukv"]),
        w_xq=f(inputs["w_xq"]), w_xkv=f(inputs["w_xkv"]), w_xo=f(inputs["w_xo"]),
        w_gate_up=f(inputs["w_gate_up"]), w_down=f(inputs["w_down"]),
        gcols=gcols, sinkb=sink, tables=tables, perm=perm, ident=np.eye(128, dtype=np.float32), bmask=bm, dmask=dcol, dneg=((dcol - 1.0) * 30000.0).astype(np.float32), dbias=dbias)
    maps = []
    for b in range(NCORES):
        m = dict(shared)
        m["xT"] = np.ascontiguousarray(x[b].T.reshape(8, 128, S))
        m["memT"] = np.ascontiguousarray(mem[b].T.reshape(8, 128, 256))
        maps.append(m)
    return maps


def kernel(**inputs):
    nc = build_program()
    maps = make_in_maps(inputs)
    res = run_bass_kernel_spmd(nc, maps, core_ids=list(range(NCORES)))
    out = np.stack([np.asarray(r["outT"], np.float32).reshape(D, S).T for r in res.results], 0)
    return np.ascontiguousarray(out.astype(np.float32))
```
